# Optimizing a Trainium2 kernel written in Bass

```python
import math
import jax, jax.numpy as jnp
from jax import lax
import numpy as np

D_MODEL = 1024
BATCH = 2
SEQ = 8192
DEPTH = 2
DEC_BATCH = 32
DEC_SEQ = 4
PAST_LEN = 16384
PAGE_SIZE = 128

N_MIXERS = 4
MIX_W = D_MODEL // N_MIXERS
N_HEADS_MIX = 4
HEAD_DIM = MIX_W // N_HEADS_MIX
DILATIONS = ((128, 1), (512, 4), (2048, 16))
MAX_WINDOW = max(w for w, _ in DILATIONS)
ATTN_BLOCK = 128
N_BUCKETS = 32
MAX_DISTANCE = MAX_WINDOW
MLSTM_CHUNK = 128
HGRN_CHUNK = 16
MLP_CHUNK = 128
D_FF = 4 * D_MODEL
EPS = 1e-6
NEG = -1e30
LB_FLOOR = 1e-30

SPLIT_SIZES = (MIX_W, MIX_W, MIX_W,
               MIX_W, MIX_W, MIX_W, MIX_W, N_HEADS_MIX, N_HEADS_MIX,
               MIX_W, MIX_W,
               MIX_W, MIX_W, MIX_W, MIX_W)
D_IN = sum(SPLIT_SIZES)
SPLIT_POINTS = tuple(int(c) for c in np.cumsum(SPLIT_SIZES)[:-1])

kernel_name = 'hybrid_dilated_mlstm_gmlp_hgrn2_step'


def rmsnorm(x, g):
    xf = x.astype(jnp.float32)
    y = xf * lax.rsqrt(jnp.mean(xf * xf, axis=-1, keepdims=True) + EPS)
    return (y * g.astype(jnp.float32)).astype(x.dtype)


def headnorm(h, g):
    return rmsnorm(h, g.reshape(N_HEADS_MIX, HEAD_DIM))


def rel_bucket(dist):
    max_exact = N_BUCKETS // 2
    d = jnp.maximum(dist, 1).astype(jnp.float32)
    large = max_exact + (jnp.log(d / max_exact) / math.log(MAX_DISTANCE / max_exact)
                         * (N_BUCKETS - max_exact)).astype(jnp.int32)
    large = jnp.clip(large, max_exact, N_BUCKETS - 1)
    return jnp.where(dist < max_exact, dist, large)


def _to_chunks(a, L):
    B, T, H = a.shape[:3]
    a = a.astype(jnp.float32).reshape((B, T // L, L, H) + a.shape[3:])
    return jnp.moveaxis(a, (1, 3), (0, 2))


def _from_chunks(a):
    nc, B, H, L = a.shape[:4]
    a = jnp.moveaxis(a, (0, 2), (1, 3))
    return a.reshape((B, nc * L, H) + a.shape[4:])


def dilated_window_attention(q, k, v, k_past, v_past, rel_bias):
    B, T, H, dh = q.shape
    P = k_past.shape[1]
    pad = MAX_WINDOW - P
    f32 = jnp.float32
    zeros = jnp.zeros((B, pad, H, dh), f32)
    kp = jnp.concatenate([zeros, k_past.astype(f32), k.astype(f32)], axis=1)
    vp = jnp.concatenate([zeros, v_past.astype(f32), v.astype(f32)], axis=1)
    qf = q.astype(f32) * (HEAD_DIM ** -0.5)
    qb = ATTN_BLOCK if T % ATTN_BLOCK == 0 else T
    nb = T // qb
    patterns = []
    for w, d in DILATIONS:
        offs = jnp.arange(w // d + 1, dtype=jnp.int32) * d
        bias = rel_bias[rel_bucket(offs)].T.astype(f32)
        patterns.append((offs, bias))

    def block(s):
        qblk = lax.dynamic_slice_in_dim(qf, s, qb, axis=1)
        qpos = MAX_WINDOW + s + jnp.arange(qb, dtype=jnp.int32)
        lses, outs = [], []
        for offs, bias in patterns:
            idx = qpos[:, None] - offs[None, :]
            kg = jnp.take(kp, idx, axis=1, mode='clip')
            vg = jnp.take(vp, idx, axis=1, mode='clip')
            logits = jnp.einsum('bqhd,bqjhd->bhqj', qblk, kg) + bias[None, :, None, :]
            logits = jnp.where((idx >= pad)[None, None], logits, NEG)
            m = jnp.max(logits, axis=-1, keepdims=True)
            p = jnp.exp(logits - m)
            den = jnp.sum(p, axis=-1)
            o = jnp.einsum('bhqj,bqjhd->bqhd', p, vg) / jnp.transpose(den, (0, 2, 1))[..., None]
            lses.append(jnp.transpose(m[..., 0] + jnp.log(den), (0, 2, 1)))
            outs.append(o)
        wts = jax.nn.softmax(jnp.stack(lses, 0), axis=0)
        return jnp.einsum('pbqh,pbqhd->bqhd', wts, jnp.stack(outs, 0))

    out = lax.map(block, jnp.arange(nb, dtype=jnp.int32) * qb)
    return jnp.moveaxis(out, 0, 1).reshape(B, T, H, dh)


def mlstm_chunkwise(q, k, v, i_pre, log_f, C0, n0, m0):
    B, T, H, dh = q.shape
    L = MLSTM_CHUNK if T % MLSTM_CHUNK == 0 else T
    qc, kc, vc = _to_chunks(q, L), _to_chunks(k, L) * (dh ** -0.5), _to_chunks(v, L)
    ic, fc = _to_chunks(i_pre, L), _to_chunks(log_f, L)
    causal = jnp.tril(jnp.ones((L, L), bool))

    def step(carry, xs):
        C, n, m = carry
        qt, kt, vt, it, ft = xs
        b = jnp.cumsum(ft, axis=-1)
        D = jnp.where(causal, b[..., :, None] - b[..., None, :] + it[..., None, :], NEG)
        g = b + m[..., None]
        m_t = jnp.maximum(g, jnp.max(D, axis=-1))
        Dexp = jnp.exp(D - m_t[..., None])
        gexp = jnp.exp(g - m_t)
        S = jnp.einsum('bhtd,bhsd->bhts', qt, kt) * Dexp
        num = jnp.einsum('bhts,bhsd->bhtd', S, vt) + gexp[..., None] * jnp.einsum('bhvk,bhtk->bhtv', C, qt)
        nq = jnp.sum(S, axis=-1) + gexp * jnp.einsum('bhk,bhtk->bht', n, qt)
        h = num / jnp.maximum(jnp.abs(nq), jnp.exp(-m_t))[..., None]
        m_new = m_t[..., -1]
        wk = jnp.exp(b[..., -1:] - b + it - m_new[..., None])
        dc = jnp.exp(b[..., -1] + m - m_new)
        C_new = dc[..., None, None] * C + jnp.einsum('bhs,bhsv,bhsk->bhvk', wk, vt, kt)
        n_new = dc[..., None] * n + jnp.einsum('bhs,bhsk->bhk', wk, kt)
        return (C_new, n_new, m_new), h

    f32 = jnp.float32
    (C, n, m), h = lax.scan(step, (C0.astype(f32), n0.astype(f32), m0.astype(f32)), (qc, kc, vc, ic, fc))
    return _from_chunks(h), C, n, m


def hgrn2_chunkwise(q, log_f, k, v, S0):
    B, T, H, dk = q.shape
    L = HGRN_CHUNK if T % HGRN_CHUNK == 0 else T
    qc, fc, kc, vc = _to_chunks(q, L), _to_chunks(log_f, L), _to_chunks(k, L), _to_chunks(v, L)
    causal = jnp.tril(jnp.ones((L, L), bool))

    def step(S, xs):
        qt, ft, kt, vt = xs
        b = jnp.cumsum(ft, axis=2)
        diff = jnp.where(causal[..., None], b[:, :, :, None, :] - b[:, :, None, :, :], NEG)
        A = jnp.einsum('bhtd,bhsd,bhtsd->bhts', qt, kt, jnp.exp(diff))
        o = jnp.einsum('bhts,bhsv->bhtv', A, vt) + jnp.einsum('bhtd,bhdv->bhtv', qt * jnp.exp(b), S)
        bL = b[:, :, -1:, :]
        S_new = jnp.exp(bL[:, :, 0, :])[..., None] * S + jnp.einsum('bhsd,bhsv->bhdv', kt * jnp.exp(bL - b), vt)
        return S_new, o

    S, o = lax.scan(step, S0.astype(jnp.float32), (qc, fc, kc, vc))
    return _from_chunks(o), S


def chunk_spatial_gate(u, v, w_s, b_s):
    B, T, H, c = v.shape
    nc = -(-T // MLP_CHUNK)
    Tp = nc * MLP_CHUNK
    vp = jnp.pad(v, ((0, 0), (0, Tp - T), (0, 0), (0, 0))).reshape(B, nc, MLP_CHUNK, H, c)
    w = jnp.where(jnp.tril(jnp.ones((MLP_CHUNK, MLP_CHUNK), bool)), w_s, 0)
    s = jnp.einsum('hts,bnshc->bnthc', w, vp) + b_s.T[None, None, :, :, None]
    return u * s.reshape(B, Tp, H, c)[:, :T]


def layer(x, k_past, v_past, C0, n0, m0, S0, keep,
          w_in, w_out, g_attn, g_mlp, w_up, w_down, b_i, b_f, g_mlstm, g_cv, w_s, b_s, lb, g_hgrn, rel_bias):
    B, T, _ = x.shape
    H, dh = N_HEADS_MIX, HEAD_DIM
    f32 = jnp.float32
    z = rmsnorm(x, g_attn) @ w_in
    aq, ak, av, bq, bk, bv, bo, bi, bf, cu, cv, dq, df, di, dg = jnp.split(z, SPLIT_POINTS, axis=-1)
    heads = lambda a: a.reshape(B, T, H, dh)
    ka, va = heads(ak), heads(av)
    out_a = dilated_window_attention(heads(aq), ka, va, k_past, v_past, rel_bias).reshape(B, T, MIX_W)
    k_keep = jnp.concatenate([k_past.astype(ka.dtype), ka], axis=1)[:, -keep:]
    v_keep = jnp.concatenate([v_past.astype(va.dtype), va], axis=1)[:, -keep:]
    i_pre = bi.astype(f32) + b_i.astype(f32)
    log_fb = jax.nn.log_sigmoid(bf.astype(f32) + b_f.astype(f32))
    hb, C1, n1, m1 = mlstm_chunkwise(heads(bq), heads(bk), heads(bv), i_pre, log_fb, C0, n0, m0)
    out_b = jax.nn.sigmoid(bo.astype(f32)) * headnorm(hb, g_mlstm).reshape(B, T, MIX_W)
    v_rows = heads(rmsnorm(jax.nn.gelu(cv), g_cv))
    out_c = chunk_spatial_gate(heads(jax.nn.gelu(cu)), v_rows, w_s, b_s).reshape(B, T, MIX_W)
    lbf = lb.astype(f32)
    dff = df.astype(f32)
    log_fd = jnp.logaddexp(jnp.log(jnp.maximum(lbf, LB_FLOOR)), jnp.log1p(-lbf) + jax.nn.log_sigmoid(dff))
    kd = (1.0 - lbf) * jax.nn.sigmoid(-dff)
    hd, S1 = hgrn2_chunkwise(heads(dq), heads(log_fd), heads(kd), heads(di), S0)
    out_d = headnorm(hd, g_hgrn).reshape(B, T, MIX_W) * jax.nn.silu(dg.astype(f32))
    mix = jnp.concatenate([out_a.astype(x.dtype), out_b.astype(x.dtype),
                           out_c.astype(x.dtype), out_d.astype(x.dtype)], axis=-1)
    x = x + mix @ w_out
    hm = rmsnorm(x, g_mlp) @ w_up
    x = x + jnp.square(jax.nn.relu(hm)) @ w_down
    return x, k_keep, v_keep, C1, n1, m1, S1, v_rows


def setup_inputs(seed: int = 0) -> dict:
    key = jax.random.key(seed)
    ks = jax.random.split(key, 24)
    H, dh = N_HEADS_MIX, HEAD_DIM
    wb = min(MAX_WINDOW, PAST_LEN)
    nrm = lambda k, shape, s: jax.random.normal(k, shape, jnp.float32) * s
    b_f = jnp.broadcast_to(jnp.linspace(3.0, 6.0, H, dtype=jnp.float32), (DEPTH, H)) + nrm(ks[14], (DEPTH, H), 0.1)
    return {
        'x_prompt': nrm(ks[0], (BATCH, SEQ, D_MODEL), 1.0),
        'x_sample': nrm(ks[1], (DEC_BATCH, DEC_SEQ, D_MODEL), 1.0),
        'cache_k_win': nrm(ks[2], (DEPTH, DEC_BATCH, wb, H, dh), 1.0),
        'cache_v_win': nrm(ks[3], (DEPTH, DEC_BATCH, wb, H, dh), 1.0),
        'state_mlstm_C': nrm(ks[4], (DEPTH, DEC_BATCH, H, dh, dh), 0.1),
        'state_mlstm_n': nrm(ks[5], (DEPTH, DEC_BATCH, H, dh), 0.1),
        'state_mlstm_m': nrm(ks[6], (DEPTH, DEC_BATCH, H), 1.0),
        'state_hgrn_S': nrm(ks[7], (DEPTH, DEC_BATCH, H, dh, dh), 0.5),
        'rel_bias': nrm(ks[8], (N_BUCKETS, H), 0.5),
        'w_in': nrm(ks[9], (DEPTH, D_MODEL, D_IN), D_MODEL ** -0.5),
        'w_out': nrm(ks[10], (DEPTH, D_MODEL, D_MODEL), D_MODEL ** -0.5),
        'g_attn': 1.0 + nrm(ks[11], (DEPTH, D_MODEL), 0.01),
        'g_mlp': 1.0 + nrm(ks[12], (DEPTH, D_MODEL), 0.01),
        'w_up': nrm(ks[13], (DEPTH, D_MODEL, D_FF), D_MODEL ** -0.5),
        'w_down': nrm(ks[15], (DEPTH, D_FF, D_MODEL), D_FF ** -0.5),
        'b_i': nrm(ks[16], (DEPTH, H), 0.1),
        'b_f': b_f,
        'g_mlstm': 1.0 + nrm(ks[17], (DEPTH, MIX_W), 0.01),
        'g_cv': 1.0 + nrm(ks[18], (DEPTH, MIX_W), 0.01),
        'w_s': nrm(ks[19], (DEPTH, H, MLP_CHUNK, MLP_CHUNK), MLP_CHUNK ** -0.5),
        'b_s': 1.0 + nrm(ks[20], (DEPTH, H, MLP_CHUNK), 0.01),
        'hgrn_lb': nrm(ks[21], (DEPTH, MIX_W), 0.5),
        'g_hgrn': 1.0 + nrm(ks[22], (DEPTH, MIX_W), 0.01),
        'g_final': 1.0 + nrm(ks[23], (D_MODEL,), 0.01),
    }


def reference(x_prompt, x_sample, cache_k_win, cache_v_win, state_mlstm_C, state_mlstm_n, state_mlstm_m,
              state_hgrn_S, rel_bias, w_in, w_out, g_attn, g_mlp, w_up, w_down, b_i, b_f, g_mlstm, g_cv,
              w_s, b_s, hgrn_lb, g_hgrn, g_final):
    f32 = jnp.float32
    H, dh = N_HEADS_MIX, HEAD_DIM
    B, T = x_prompt.shape[:2]
    keep_prompt = min(MAX_WINDOW, T)
    keep_sample = cache_k_win.shape[2]
    sm = jax.nn.softmax(hgrn_lb.astype(f32), axis=0)
    lb_all = jnp.cumsum(sm, axis=0) - sm[0:1]
    empty = jnp.zeros((B, 0, H, dh), x_prompt.dtype)
    zC = jnp.zeros((B, H, dh, dh), f32)
    zn = jnp.zeros((B, H, dh), f32)
    zm = jnp.zeros((B, H), f32)
    xp, xs = x_prompt, x_sample
    kwp, vwp, kws, vws = [], [], [], []
    Cp, np_, mp, Cs, ns, ms = [], [], [], [], [], []
    Sp, Ss, cvs = [], [], []
    for l in range(DEPTH):
        wl = (w_in[l], w_out[l], g_attn[l], g_mlp[l], w_up[l], w_down[l], b_i[l], b_f[l],
              g_mlstm[l], g_cv[l], w_s[l], b_s[l], lb_all[l], g_hgrn[l], rel_bias)
        xp, k1, v1, C1, n1, m1, S1, _ = layer(xp, empty, empty, zC, zn, zm, zC, keep_prompt, *wl)
        kwp.append(k1); vwp.append(v1); Cp.append(C1); np_.append(n1); mp.append(m1); Sp.append(S1)
        xs, k2, v2, C2, n2, m2, S2, cv2 = layer(xs, cache_k_win[l], cache_v_win[l], state_mlstm_C[l],
                                                state_mlstm_n[l], state_mlstm_m[l], state_hgrn_S[l],
                                                keep_sample, *wl)
        kws.append(k2); vws.append(v2); Cs.append(C2); ns.append(n2); ms.append(m2); Ss.append(S2); cvs.append(cv2)
    y_prompt = rmsnorm(xp, g_final)
    y_sample = rmsnorm(xs, g_final)
    return (y_prompt, y_sample,
            jnp.stack(kwp), jnp.stack(vwp), jnp.stack(kws), jnp.stack(vws),
            jnp.stack(Cp), jnp.stack(np_), jnp.stack(mp),
            jnp.stack(Cs), jnp.stack(ns), jnp.stack(ms),
            jnp.stack(Sp), jnp.stack(Ss), jnp.stack(cvs))
```

```python
import os
from contextlib import ExitStack
import numpy as np
import concourse.bass as bass
import concourse.mybir as mybir
from concourse.bass_utils import run_bass_kernel_spmd

F32 = mybir.dt.float32
BF = mybir.dt.bfloat16
AF = mybir.ActivationFunctionType
ALU = mybir.AluOpType
AX = mybir.AxisListType

NCH = int(os.environ.get("KNCH", "64"))
SAME_SYNC = os.environ.get("KSAME", "1") == "1"
KSTOP = int(os.environ.get("KSTOP", "99"))
KCUT = int(os.environ.get("KCUT", "99"))
KB = int(os.environ.get("KB", "99"))
KFM = int(os.environ.get("KFORK", "1"))
KFORK = KFM in (1, 2)
KFORK2 = KFM in (1, 3)
KP = int(os.environ.get("KP", "99"))
T = 128
DM = 1024
DIN = 3336
DFF = 4096
BS = 64
NW = 17
TABN = 2304
EPS = 1e-6
COMPUTE = ("pe", "act", "dve", "pool")


class Prog:
    def __init__(self, kdma=8):
        self.ops = {e: [] for e in ("pe", "act", "dve", "pool", "sp")}
        self.cnt = {e: 0 for e in self.ops}
        self.dcnt = {e: 0 for e in self.ops}
        self.lastw = {}
        self.readers = {}
        self.waited = {e: {} for e in self.ops}
        self.K = kdma
        self.final = {}

    def fork(self):
        self.threads = [[]]

    def next_thread(self):
        self.threads.append([])

    def join(self):
        lists = self.threads
        self.threads = None
        tot = [max(1, len(x)) for x in lists]
        pos = [0] * len(lists)
        while True:
            best, bf = -1, 2.0
            for i, x in enumerate(lists):
                if pos[i] < len(x):
                    f = pos[i] / tot[i]
                    if f < bf:
                        best, bf = i, f
            if best < 0:
                break
            a = lists[best][pos[best]]
            pos[best] += 1
            self.op(*a)

    def op(self, eng, fn, r=(), w=(), dma=False):
        if getattr(self, "threads", None) is not None:
            self.threads[-1].append((eng, fn, tuple(r), tuple(w), dma))
            return None
        deps = []
        for x in r:
            t = self.lastw.get(x)
            if t:
                deps.append(t)
            if x.startswith("ps"):
                for s, (v, e) in self.readers.get(x, {}).items():
                    if e != eng:
                        deps.append((s, v, e, s.startswith("d_")))
        for x in w:
            t = self.lastw.get(x)
            if t:
                deps.append(t)
            for s, (v, e) in self.readers.get(x, {}).items():
                deps.append((s, v, e, s.startswith("d_")))
        if dma:
            i = self.dcnt[eng]
            self.dcnt[eng] += 1
            sem = f"d_{eng}{i % self.K}"
            val = 16 * (i // self.K + 1)
            if i >= self.K:
                deps.append((sem, val - 16, eng, True))
            tok = (sem, val, eng, True)
            inc = (sem, 16)
        else:
            self.cnt[eng] += 1
            sem = f"c_{eng}"
            val = self.cnt[eng]
            tok = (sem, val, eng, False)
            inc = (sem, 1)
        self.final[sem] = val
        waits = {}
        for (s, v, e, isd) in deps:
            if e == eng and not isd and not dma:
                if eng == "pe" or not SAME_SYNC:
                    continue
            if self.waited[eng].get(s, 0) >= v:
                continue
            waits[s] = max(waits.get(s, 0), v)
        for s, v in waits.items():
            self.waited[eng][s] = v
        self.ops[eng].append((list(waits.items()), fn, inc))
        for x in r:
            d = self.readers.setdefault(x, {})
            if d.get(sem, (0, None))[0] < val:
                d[sem] = (val, eng)
        for x in w:
            self.lastw[x] = tok
            self.readers[x] = {}
        return tok

    def barrier(self):
        for eng in self.ops:
            waits = []
            for s, v in self.final.items():
                if s == f"c_{eng}" and eng == "pe":
                    continue
                if self.waited[eng].get(s, 0) >= v:
                    continue
                self.waited[eng][s] = v
                waits.append((s, v))
            if waits:
                self.ops[eng].append((waits, None, None))


def build(nch):
    nc = bass.Bass("TRN2", target_bir_lowering=False)
    P = Prog()
    es = ExitStack()

    def din(name, shape):
        return nc.dram_tensor(name, list(shape), F32, kind="ExternalInput").ap()

    def dout(name, shape):
        return nc.dram_tensor(name, list(shape), F32, kind="ExternalOutput").ap()

    def dint(name, shape):
        return nc.dram_tensor(name, list(shape), F32, kind="Internal").ap()

    NTP = nch * T
    xp = din("xp", [NTP, DM])
    xs = din("xs", [16, DM])
    kc = din("kc", [2, 4, 2048, 256])
    vc = din("vc", [2, 4, 2048, 256])
    mC = din("mC", [2, 4, 4, 64, 64])
    mn = din("mn", [2, 4, 4, 64])
    mm_ = din("mm", [2, 4, 4])
    hS = din("hS", [2, 4, 4, 64, 64])
    relb = din("relb", [32, 4])
    w_in = din("w_in", [2, DM, DIN])
    w_out = din("w_out", [2, DM, DM])
    g_attn = din("g_attn", [2, DM])
    g_mlp = din("g_mlp", [2, DM])
    w_up = din("w_up", [2, DM, DFF])
    w_down = din("w_down", [2, DFF, DM])
    b_i = din("b_i", [2, 4])
    b_f = din("b_f", [2, 4])
    g_mlstm = din("g_mlstm", [2, 256])
    g_cv = din("g_cv", [2, 256])
    w_s = din("w_s", [2, 4, 128, 128])
    b_s = din("b_s", [2, 4, 128])
    hlb = din("hlb", [2, 256])
    g_hgrn = din("g_hgrn", [2, 256])
    g_final = din("g_final", [1, DM])
    cI = din("cI", [128, 128])
    cJ = din("cJ", [128, 128])
    cTri = din("cTri", [128, 128])
    cLs = din("cLs", [128, 128])
    cMO = din("cMO", [32, TABN])
    cMOr = din("cMOr", [32, TABN])

    y_p = dout("y_p", [NTP, DM])
    y_s = dout("y_s", [16, DM])
    kwp = dout("kwp", [2, 2048, 256])
    vwp = dout("vwp", [2, 2048, 256])
    kws = dout("kws", [2, 4, 2048, 256])
    vws = dout("vws", [2, 4, 2048, 256])
    oCp = dout("oCp", [2, 4, 64, 64])
    onp = dout("onp", [2, 4, 64])
    omp = dout("omp", [2, 4])
    oCs = dout("oCs", [2, 4, 4, 64, 64])
    ons = dout("ons", [2, 4, 4, 64])
    oms = dout("oms", [2, 4, 4])
    oSp = dout("oSp", [2, 4, 64, 64])
    oSs = dout("oSs", [2, 4, 4, 64, 64])
    ocv = dout("ocv", [2, 4, 4, 256])

    xmid = dint("xmid", [NTP + 16, DM])
    xnext = dint("xnext", [NTP + 16, DM])
    wtab = dint("wtab", [4, TABN])
    wtabr = dint("wtabr", [4, TABN])

    uid = [0]

    def sb(name, shape, dt=F32, stack=None):
        uid[0] += 1
        return (stack or es).enter_context(nc.sbuf_tensor(f"{name}_{uid[0]}", list(shape), dt))

    banks = [es.enter_context(nc.psum_tensor(f"ps{i}", [128, 512], F32)) for i in range(8)]
    rings = {"pj": [0, 1, 2], "sc": [3, 4], "pv": [5], "mx": [6, 7]}
    rpos = {k: 0 for k in rings}

    ringsets = {0: {"pj": [0], "sc": [1, 2], "pv": [3], "mx": [3]},
                1: {"pj": [4, 5], "sc": [6], "mx": [7], "pv": [7]}}
    curset = [None]

    def psum(role):
        rg = rings if curset[0] is None else ringsets[curset[0]]
        i = rg[role][rpos[role] % len(rg[role])]
        rpos[role] += 1
        return banks[i], f"ps{i}"

    def mm(out, lhsT, rhs, start=True, stop=True, r=(), w=()):
        P.op("pe", lambda e, o=out, a=lhsT, b=rhs, s=start, t=stop: e.matmul(o, lhsT=a, rhs=b, start=s, stop=t), r, w)

    def act(out, in_, func, r=(), w=(), bias=None, scale=None, accum=None):
        kw = {}
        if bias is not None:
            kw["bias"] = bias
        if scale is not None:
            kw["scale"] = scale
        if accum is not None:
            kw["accum_out"] = accum
        P.op("act", lambda e, o=out, i=in_, f=func, k=kw: e.activation(out=o, in_=i, func=f, **k), r, w)

    def tt(eng, out, a, b, op, r=(), w=()):
        P.op(eng, lambda e, o=out, x=a, y=b, p=op: e.tensor_tensor(out=o, in0=x, in1=y, op=p), r, w)

    def ts(eng, out, a, s1, op0, s2=None, op1=None, r=(), w=()):
        if op1 is None:
            P.op(eng, lambda e, o=out, x=a, q=s1, p=op0: e.tensor_scalar(out=o, in0=x, scalar1=q, scalar2=None, op0=p), r, w)
        else:
            P.op(eng, lambda e, o=out, x=a, q=s1, p=op0, q2=s2, p2=op1: e.tensor_scalar(out=o, in0=x, scalar1=q, scalar2=q2, op0=p, op1=p2), r, w)

    def stt(eng, out, a, s, b, op0, op1, r=(), w=()):
        P.op(eng, lambda e, o=out, x=a, q=s, y=b, p=op0, p2=op1: e.scalar_tensor_tensor(out=o, in0=x, scalar=q, in1=y, op0=p, op1=p2), r, w)

    def cp(eng, out, in_, r=(), w=()):
        if eng == "act":
            P.op("act", lambda e, o=out, i=in_: e.copy(out=o, in_=i), r, w)
        else:
            P.op(eng, lambda e, o=out, i=in_: e.tensor_copy(out=o, in_=i), r, w)

    def memset(eng, ap, val, w=()):
        P.op(eng, lambda e, a=ap, v=val: e.memset(a, v), (), w)

    def dma(out, in_, r=(), w=(), eng="sp", slow=False):
        if slow:
            P.op(eng, lambda e, o=out, i=in_: e.dma_start(out=o, in_=i, allow_slow_non_contiguous=True), r, w, dma=True)
        else:
            P.op(eng, lambda e, o=out, i=in_: e.dma_start(out=o, in_=i), r, w, dma=True)

    def dap(base, off, pat):
        return bass.AP(base.tensor, off, [list(p) for p in pat])

    I_f = sb("I_f", [128, 128]); J_f = sb("J_f", [128, 128]); tri_f = sb("tri_f", [128, 128])
    ls_f = sb("ls_f", [128, 128]); ones_f = sb("ones_f", [128, 128])
    I_bf = sb("I_bf", [128, 128], BF); J_bf = sb("J_bf", [128, 128], BF)
    epsT = sb("epsT", [128, 1]); oneT = sb("oneT", [128, 1]); zeroT = sb("zeroT", [128, 1])
    dma(I_f[:], cI[:, :], w=["I_f"]); dma(J_f[:], cJ[:, :], w=["J_f"])
    dma(tri_f[:], cTri[:, :], w=["tri_f"]); dma(ls_f[:], cLs[:, :], w=["ls_f"])
    memset("pool", ones_f[:], 1.0, ["ones_f"]); memset("pool", epsT[:], EPS, ["epsT"])
    memset("pool", oneT[:], 1.0, ["oneT"]); memset("pool", zeroT[:], 0.0, ["zeroT"])
    cp("pool", I_bf[:], I_f[:], ["I_f"], ["I_bf"]); cp("pool", J_bf[:], J_f[:], ["J_f"], ["J_bf"])

    with ExitStack() as st0:
        rb = sb("rb", [32, 4], stack=st0); erb = sb("erb", [32, 4], stack=st0)
        mo = sb("mo", [32, TABN], stack=st0); wt = sb("wt", [4, TABN], stack=st0)
        dma(rb[:], relb[:, :], w=["rb"])
        act(erb[:], rb[:], AF.Exp, ["rb"], ["erb"])
        for src, dst in ((cMO, wtab), (cMOr, wtabr)):
            dma(mo[:], src[:, :], w=["mo"])
            for c0 in range(0, TABN, 512):
                n = min(512, TABN - c0)
                pb, pk = psum("mx")
                mm(pb[0:4, 0:n], erb[:, :], mo[:, c0:c0 + n], r=["erb", "mo"], w=[pk])
                cp("dve", wt[:, c0:c0 + n], pb[0:4, 0:n], [pk], ["wt"])
            dma(dst[:, :], wt[:], r=["wt"], w=["wtab" if dst is wtab else "wtabr"])
        P.barrier()

    pending = []
    for l in range(2):
        for s in range(4):
            pending.append((kws[l, s, 0:2044, :], kc[l, s, 4:2048, :], f"kws{l}{s}"))
            pending.append((vws[l, s, 0:2044, :], vc[l, s, 4:2048, :], f"vws{l}{s}"))

    def flush_pending(n):
        for _ in range(n):
            if pending:
                o, i, k = pending.pop(0)
                dma(o, i, w=[k], eng="pool")

    xt = sb("xt", [128, DM]); xn = sb("xn", [128, DM], BF)
    xnT = sb("xnT", [128, 8, 128], BF)
    xts = [xt, sb("xt2", [128, DM])]
    xns = [xn, sb("xn2", [128, DM], BF)]
    xnTs = [xnT, sb("xnT2", [128, 8, 128], BF)]
    ssq = sb("ssq", [128, 8]); rstd = sb("rstd", [128, 8]); lnv = sb("lnv", [128, 8])
    stg = [sb(f"stg{i}", [128, 1088]) for i in range(2)]
    stgpos = [0]

    def rmsnorm_rstd(src, L, ncol, scale, ssq_ap, rstd_ap, junk, keys_r, key_junk, sfx="", lcol=0):
        memset("dve", ssq_ap, 0.0, ["ssq" + sfx])
        act(junk, src, AF.Square, keys_r + ["ssq" + sfx], [key_junk, "ssq" + sfx], accum=ssq_ap)
        act(lnv[0:L, lcol:lcol + 1], ssq_ap, AF.Ln, ["ssq" + sfx, "epsT"], ["lnv" + sfx], bias=epsT[0:L, 0:1], scale=scale)
        act(rstd_ap, lnv[0:L, lcol:lcol + 1], AF.Exp, ["lnv" + sfx], ["rstd" + sfx], scale=-0.5)

    def transposes(src, L, nk, dst, mat, kr, kw, kmat):
        for g in range(0, nk, 4):
            pb, pk = psum("pj")
            pv = pb[:, :].rearrange("p (a b) -> p a b", a=4)
            for k in range(g, min(g + 4, nk)):
                mm(pv[:, k - g, 0:L], src[0:L, k * 128:(k + 1) * 128], mat[0:L, 0:L], r=[kr, kmat], w=[pk])
            n = min(4, nk - g)
            cp("act" if (g // 4) % 2 == 0 else "dve", dst[:, g:g + n, 0:L], pv[:, 0:n, 0:L], [pk], [kw])

    def load_weights(dst3, src2, nk, ncols, gT, gsel, keyw, segs=None):
        if segs is None:
            segs = [(0, ncols, 0)]
        segs = [(c0 + o, min(1024, n - o), d0 + o) for (c0, n, d0) in segs for o in range(0, n, 1024)]
        for k in range(nk):
            for (c0, n, d0) in segs:
                i = stgpos[0] % 2
                stgpos[0] += 1
                dma(stg[i][:, 0:n], src2[k * 128:(k + 1) * 128, c0:c0 + n], w=[f"stg{i}"], eng=("sp" if i == 0 else "act"))
                gi = gsel(k) if gT is not None else None
                if gi is None:
                    cp("pool", dst3[:, k, d0:d0 + n], stg[i][:, 0:n], [f"stg{i}"], [keyw])
                else:
                    ts("pool", dst3[:, k, d0:d0 + n], stg[i][:, 0:n], gT[:, gi:gi + 1], ALU.mult,
                       r=[f"stg{i}", "gT"], w=[keyw])

    def phase_M(l, xsrc_p, xsrc_s):
        ph = ExitStack()
        W = sb(f"win{l}", [128, 8, 3392], BF, ph)
        WO = sb(f"wout{l}", [128, 8, DM], BF, ph)
        gT = sb(f"gT{l}", [128, 12], stack=ph)
        dma(gT[:, 0:8], dap(g_attn, l * DM, [[1, 128], [128, 8]]), w=["gT"], slow=True)
        dma(gT[:, 8:10], dap(g_mlstm, l * 256, [[1, 128], [128, 2]]), w=["gT"], slow=True)
        dma(gT[:, 10:12], dap(g_hgrn, l * 256, [[1, 128], [128, 2]]), w=["gT"], slow=True)
        load_weights(W, w_in[l], 8, DIN, gT, lambda k: k, "W", segs=[(0, 1800, 0), (1800, 1536, 1856)])
        load_weights(WO, w_out[l], 8, DM, None, None, "WO")

        bfi = sb(f"bfi{l}", [128, 8], stack=ph)
        dma(bfi[:, 0:4], dap(b_i, l * 4, [[0, 128], [1, 4]]), w=["bfi"])
        dma(bfi[:, 4:8], dap(b_f, l * 4, [[0, 128], [1, 4]]), w=["bfi"])
        gcv = sb(f"gcv{l}", [128, 256], stack=ph)
        dma(gcv[:], dap(g_cv, l * 256, [[0, 128], [1, 256]]), w=["gcv"])
        gml = sb(f"gml{l}", [128, 256], stack=ph); ghg = sb(f"ghg{l}", [128, 256], stack=ph)
        dma(gml[:], dap(g_mlstm, l * 256, [[0, 128], [1, 256]]), w=["gml"])
        dma(ghg[:], dap(g_hgrn, l * 256, [[0, 128], [1, 256]]), w=["ghg"])
        oml = sb(f"oml{l}", [128, 256], stack=ph)
        lbm = sb(f"lbm{l}", [128, 256], stack=ph)
        C_uv = sb("C_uv", [128, 512], stack=ph); C_t = sb("C_t", [128, 512], stack=ph)
        lbt = C_uv[:, 0:256]
        lbT = sb(f"lbT{l}", [64, 4], stack=ph); omlT = sb(f"omlT{l}", [64, 4], stack=ph); nomlT = sb(f"nomlT{l}", [64, 4], stack=ph)
        if l == 0:
            memset("dve", lbt, 0.0, ["C_uv"]); memset("dve", lbT[:], 0.0, ["lbT"])
        else:
            t0 = C_t[:, 0:256]; t1 = C_t[:, 256:512]
            dma(t0, dap(hlb, 0, [[0, 128], [1, 256]]), w=["C_t"])
            dma(t1, dap(hlb, 256, [[0, 128], [1, 256]]), w=["C_t"])
            tt("dve", t1, t1, t0, ALU.subtract, ["C_t"], ["C_t"])
            act(lbt, t1, AF.Sigmoid, ["C_t"], ["C_uv"])
            u0 = sb("lbtmp2", [64, 4], stack=ph); u1 = sb("lbtmp3", [64, 4], stack=ph)
            dma(u0[:], dap(hlb, 0, [[1, 64], [64, 4]]), w=["lbu0"], slow=True)
            dma(u1[:], dap(hlb, 256, [[1, 64], [64, 4]]), w=["lbu1"], slow=True)
            tt("dve", u1[:], u1[:], u0[:], ALU.subtract, ["lbu0", "lbu1"], ["lbu1"])
            act(lbT[:], u1[:], AF.Sigmoid, ["lbu1"], ["lbT"])
        ts("dve", oml[:], lbt, -1.0, ALU.mult, 1.0, ALU.add, r=["C_uv"], w=["oml"])
        ts("dve", lbm[:], lbt, 1e-30, ALU.max, r=["C_uv"], w=["lbm"])
        ts("dve", omlT[:], lbT[:], -1.0, ALU.mult, 1.0, ALU.add, r=["lbT"], w=["omlT"])
        ts("dve", nomlT[:], omlT[:], -1.0, ALU.mult, r=["omlT"], w=["nomlT"])
        WsT = sb(f"WsT{l}", [128, 4, 128], BF, ph); bsT = sb(f"bsT{l}", [128, 4], stack=ph)
        dma(bsT[:], dap(b_s, l * 512, [[1, 128], [128, 4]]), w=["bsT"], slow=True)
        wsm = sb("wsm", [128, 128], BF, ph)
        for h in range(4):
            i = stgpos[0] % 2
            stgpos[0] += 1
            dma(stg[i][:, 0:128], w_s[l, h, :, :], w=[f"stg{i}"])
            tt("dve", wsm[:], stg[i][:, 0:128], ltm[:], ALU.mult, [f"stg{i}", "ltm"], ["wsm"])
            pb, pk = psum("pj")
            mm(pb[:, 0:128], wsm[:, :], I_bf[:, :], r=["wsm", "I_bf"], w=[pk])
            cp("dve", WsT[:, h, :], pb[:, 0:128], [pk], ["WsT"])

        KT = sb("KT", [64, 4, NW, 128], BF, ph)
        VW = sb("VW", [128, NW, 4, 96], BF, ph)
        MP = sb("MP", [128, 4, NW, 128], BF, ph)
        MS = sb("MS", [128, 4, NW, 4], stack=ph)
        for h in range(4):
            for (j0, nj) in ((0, 6), (6, 6), (12, 5)):
                i = stgpos[0] % 2
                stgpos[0] += 1
                mstg = stg[i][:, 0:nj * 128].rearrange("p (j t) -> p j t", j=nj)
                dma(mstg, dap(wtab, h * TABN + 128 * j0, [[1, 128], [128, nj], [1, 128]]), r=["wtab"], w=[f"stg{i}"])
                cp("pool", MP[:, h, j0:j0 + nj, :], mstg, [f"stg{i}"], ["MP"])
            for t in range(4):
                dma(MS[:, h, :, t], dap(wtabr, h * TABN + 127 - t, [[1, 128], [128, NW]]), r=["wtabr"], w=["MS"], slow=True)
        memset("pool", VW[:], 1.0, [f"VW{j}" for j in range(NW)])

        A_qT = sb("A_qT", [64, 4, 128], BF, ph)
        A_kv = stg[0][:, 0:512]
        pexps = [sb("pexp", [128, 4, 128], stack=ph) for _ in range(2)]
        Pts = [sb("Pt", [128, NW, 128], BF, ph) for _ in range(2)]
        rden = sb("rden", [128, 4], stack=ph)
        mix = sb("mix", [128, DM], BF, ph)
        xnTr = sb("xnTr", [128, 8, 128], BF, ph)
        mixT = xnTr
        Kcb = Pts[0][:, 0:8, :].rearrange("p (a b) c -> p a (b c)", a=4)
        B_qT = sb("B_qT", [64, 4, 128], BF, ph); B_kT = sb("B_kT", [64, 4, 128], BF, ph)
        B_k = C_uv[:, 0:256]; Bv = sb("Bv", [128, 4, 96], BF, ph)
        B_o = C_uv[:, 256:512]; Bif = sb("Bif", [128, 8], stack=ph)
        bv4 = sb("bv4", [128, 64], stack=ph)
        ktil = sb("ktil", [128, 256], BF, ph); St = sb("St", [128, 128], BF, ph)
        Cst = sb("Cst", [64, 4, 65], stack=ph); Cst_bf = sb("Cst_bf", [64, 4, 96], BF, ph)
        edL = sb("edL", [64, 4], stack=ph)
        Bh = sb("Bh", [128, 4, 64], stack=ph); Bsq65 = sb("Bsq", [128, 4, 80], stack=ph); Bsq = Bsq65[:, :, 0:64]
        mrun = sb("mrun", [4, 1], stack=ph); m4 = sb("m4", [4, 32], stack=ph)
        memset("pool", Bv[:], 1.0, ["Bv"])
        vrows = sb("vrows", [128, 256], stack=ph); vr_bf = sb("vr_bf", [128, 256], BF, ph)
        D_qT = sb("D_qT", [64, 4, 128], stack=ph); D_sT = sb("D_sT", [64, 4, 128], stack=ph)
        D_bT = sb("D_bT", [64, 4, 128], stack=ph); D_nbT = sb("D_nbT", [64, 4, 128], stack=ph)
        D_e = sb("D_e", [64, 4, 128], stack=ph)
        D_ek = sb("D_ek", [64, 4, 128], stack=ph)
        D_eq = D_ek
        D_qh = sb("D_qh", [64, 4, 128], BF, ph); D_qt = sb("D_qt", [64, 4, 128], F32, ph)
        D_kI = sb("D_kI", [64, 4, 128], F32, ph)
        D_sg = sb("D_sg", [128, 256], stack=ph); D_lf = sb("D_lf", [128, 256], stack=ph)
        D_kd = sb("D_kd", [128, 256], stack=ph); D_kb = sb("D_kb", [128, 256], BF, ph)
        D_v = sb("D_v", [128, 256], BF, ph); D_AT = sb("D_AT", [128, 128], BF, ph)
        Sst = sb("Sst", [64, 4, 64], stack=ph); S_bf = sb("S_bf", [64, 4, 64], BF, ph)
        D_o = Bh; D_gs = sb("D_gs", [128, 256], stack=ph)
        memset("pool", D_AT[:], 0.0, ["D_AT"])
        outst = Bsq65[0:64, :, 0:65]; Cout = Bh[0:64]
        Cin = Cout; nin = sb("nin", [64, 4], stack=ph); em0 = sb("em0", [64, 4], stack=ph)

        print("phase M sbuf remaining", nc.sbuf_bytes_remaining)

        def v4(i, L):
            c = {0: 0, 2: 16}.get(i, 32 + 4 * i if i < 2 else 28 + 4 * i)
            return bv4[0:L, c:c + 4]

        def proj_fm(c0, srcT, ksrc, L, dst, kdst, scale=None, eng="act"):
            pb, pk = psum("pj")
            pv = pb[0:64, :].rearrange("p (a b) -> p a b", a=4)
            for h in range(4):
                for k in range(8):
                    mm(pv[:, h, 0:L], W[:, k, c0 + 64 * h:c0 + 64 * h + 64], srcT[:, k, 0:L], k == 0, k == 7,
                       r=["W", ksrc], w=[pk])
            if scale is None:
                cp(eng, dst[:, :, 0:L], pv[:, :, 0:L], [pk], [kdst])
            else:
                ts("dve", dst[:, :, 0:L], pv[:, :, 0:L], scale, ALU.mult, r=[pk], w=[kdst])

        def proj_tm(c0, n, srcT, ksrc, L):
            pb, pk = psum("pj")
            for k in range(8):
                mm(pb[0:L, 0:n], srcT[:, k, 0:L], W[:, k, c0:c0 + n], k == 0, k == 7, r=["W", ksrc], w=[pk])
            return pb, pk

        def head_rstd(src3, L, sq3, ksrc, col):
            tt("dve", sq3[0:L], src3, src3, ALU.mult, [ksrc], ["Bsq"])
            P.op("dve", lambda e, o=v4(col, L), i=sq3[0:L]: e.tensor_reduce(out=o, in_=i, axis=AX.X, op=ALU.add),
                 ["Bsq"], ["bv4"])
            act(v4(col, L), v4(col, L), AF.Ln, ["bv4", "epsT"], ["bv4"], bias=epsT[0:L, 0:1], scale=1.0 / 64)
            act(v4(col, L), v4(col, L), AF.Exp, ["bv4"], ["bv4"], scale=-0.5)
            return v4(col, L)

        apos = [0]

        def front(L, xrows, xkey, sl):
            xt_, xn_, xnT_ = xts[sl], xns[sl], xnTs[sl]
            kx, kn, kt = f"xt{sl}", f"xn{sl}", f"xnT{sl}"
            dma(xt_[0:L, :], xrows, r=[xkey], w=[kx])
            rmsnorm_rstd(xt_[0:L, :], L, DM, 1.0 / DM, ssq[0:L, sl:sl + 1], rstd[0:L, sl:sl + 1], xn_[0:L, :], [kx], kn, sfx=f"m{sl}", lcol=sl)
            ts("dve", xn_[0:L, :], xt_[0:L, :], rstd[0:L, sl:sl + 1], ALU.mult, r=[kx, f"rstdm{sl}"], w=[kn])
            transposes(xn_, L, 8, xnT_, I_bf, kn, kt, "I_bf")

        def chunk(L, xrows, xkey, orows, okey, wins, cur, maskf, prompt_ci=None, samp=None, sl=0, do_front=True, mid=None):
            if do_front:
                front(L, xrows, xkey, sl)
            xt, xn, xnT = xts[sl], xns[sl], xnTs[sl]
            KX, KN, KT_ = f"xt{sl}", f"xn{sl}", f"xnT{sl}"
            if samp is None:
                transposes(xn, L, 8, xnTr, J_bf, KN, "xnTr", "J_bf")
                ksrcT, kkey = xnTr, "xnTr"
            else:
                ksrcT, kkey = xnT, KT_
            if mid is not None:
                mid()
            if KFORK:
                P.fork()
                curset[0] = 0
            proj_fm(0, xnT, KT_, L, A_qT, "A_qT", scale=0.125)
            pb, pk = psum("pj")
            pv = pb[0:64, :].rearrange("p (a b) -> p a b", a=4)
            for h in range(4):
                for k in range(8):
                    mm(pv[:, h, 0:L], W[:, k, 256 + 64 * h:320 + 64 * h], ksrcT[:, k, 0:L], k == 0, k == 7,
                       r=["W", kkey], w=[pk])
            cp("act", KT[:, :, cur, 0:L], pv[:, :, 0:L], [pk], [f"KT{cur}"])
            pb, pk = proj_tm(512, 256, ksrcT, kkey, L)
            cp("dve", VW[0:L, cur, :, 0:64], pb[0:L, 0:256].rearrange("p (h d) -> p h d", h=4), [pk], [f"VW{cur}"])
            need_kv = (samp is not None) or (prompt_ci is not None and prompt_ci >= nch - 16)
            if need_kv:
                pb, pk = proj_tm(256, 512, xnT, KT_, L)
                cp("act", A_kv[0:L, :], pb[0:L, 0:512], [pk], ["stg0"])
                if samp is None:
                    r0 = (prompt_ci - (nch - 16)) * T + (2048 - 16 * T if nch >= 16 else 0)
                    if nch >= 16:
                        dma(kwp[l, r0:r0 + T, :], A_kv[0:L, 0:256], r=["stg0"], w=["o_kwp"])
                        dma(vwp[l, r0:r0 + T, :], A_kv[0:L, 256:512], r=["stg0"], w=["o_vwp"])
                else:
                    dma(kws[l, samp, 2044:2048, :], A_kv[0:L, 0:256], r=["stg0"], w=[f"kws{l}{samp}n"])
                    dma(vws[l, samp, 2044:2048, :], A_kv[0:L, 256:512], r=["stg0"], w=[f"vws{l}{samp}n"])
            if KCUT < 2:
                return
            pvb, pvk = psum("pv")
            pvv = pvb[:, 0:320].rearrange("p (h d) -> p h d", h=4)[:, :, 0:65]
            nw = len(wins)
            groups = []
            g0 = 0
            while g0 < nw:
                g1 = g0
                while g1 < nw and g1 - g0 < 4 and wins[g1][1] == wins[g0][1]:
                    g1 += 1
                groups.append((g0, g1))
                g0 = g1
            for h in range(4):
                Pt = Pts[h % 2]
                kpt = f"Pt{h % 2}"
                for (ga, gb) in groups:
                    Lk = wins[ga][1]
                    n = gb - ga
                    sbk, sk = psum("sc")
                    sv = sbk[:, :].rearrange("p (a b) -> p a b", a=4)
                    for gi in range(n):
                        slot = wins[ga + gi][0]
                        mm(sv[0:Lk, gi, 0:L], KT[:, h, slot, 0:Lk], A_qT[:, h, 0:L], r=[f"KT{slot}", "A_qT"], w=[sk])
                    pi = apos[0] % 2
                    apos[0] += 1
                    pexp = pexps[pi]
                    act(pexp[0:Lk, 0:n, 0:L], sv[0:Lk, 0:n, 0:L], AF.Exp, [sk], [f"pexp{pi}"])
                    tt("dve" if pi == 0 else "pool", Pt[0:Lk, ga:gb, 0:L], pexp[0:Lk, 0:n, 0:L], maskf(h, ga, n, Lk), ALU.mult,
                       [f"pexp{pi}", "MP", "MS"], [kpt])
                for wi, (slot, Lk) in enumerate(wins):
                    mm(pvv[0:L, h, :], Pt[0:Lk, wi, 0:L], VW[0:Lk, slot, h, 0:65], wi == 0, wi == nw - 1,
                       r=[kpt, f"VW{slot}"], w=[pvk])
            P.op("dve", lambda e, o=rden[0:L, :], i=pvv[0:L, :, 64]: e.reciprocal(out=o, in_=i), [pvk], ["rden"])
            tt("dve", mix[0:L, 0:256].rearrange("p (h d) -> p h d", h=4), pvv[0:L, :, 0:64],
               rden[0:L, :].unsqueeze(2).to_broadcast([L, 4, 64]), ALU.mult, [pvk, "rden"], ["mix"])

            if KFORK:
                P.next_thread()
                curset[0] = 1
            proj_fm(768, xnT, KT_, L, B_qT, "B_qT", eng="act")
            if KP < 1:
                return
            proj_fm(1024, xnT, KT_, L, B_kT, "B_kT", scale=0.125)
            if KP < 2:
                return
            pb, pk = proj_tm(1024, 512, xnT, KT_, L)
            cp("dve", B_k[0:L, :], pb[0:L, 0:256], [pk], ["C_uv"])
            cp("dve", Bv[0:L, :, 0:64], pb[0:L, 256:512].rearrange("p (h d) -> p h d", h=4), [pk], ["Bv"])
            if KP < 3:
                return
            pb, pk = proj_tm(1536, 264, xnT, KT_, L)
            cp("act", B_o[0:L, :], pb[0:L, 0:256], [pk], ["C_uv"])
            cp("dve", Bif[0:L, :], pb[0:L, 256:264], [pk], ["Bif"])
            if KB < 1:
                return
            tt("dve", Bif[0:L, :], Bif[0:L, :], bfi[0:L, :], ALU.add, ["Bif", "bfi"], ["Bif"])
            sp_, cs_, a_, wk_, ecs_, den_, rd_ = (v4(i, L) for i in range(7))
            act(sp_, Bif[0:L, 4:8], AF.Exp, ["Bif"], ["bv4"], scale=-1.0)
            act(sp_, sp_, AF.Ln, ["bv4", "oneT"], ["bv4"], bias=oneT[0:L, 0:1])
            if KB < 2:
                return
            mb, mk = psum("mx")
            mm(mb[0:L, 0:4], tri_f[0:L, 0:L], sp_, r=["tri_f", "bv4"], w=[mk])
            mm(mb[0:64, 16:20], ones_f[0:L, 0:64], sp_, r=["ones_f", "bv4"], w=[mk])
            mm(mb[0:4, 32:33], sp_, ones_f[0:L, 0:1], r=["ones_f", "bv4"], w=[mk])
            cp("dve", cs_, mb[0:L, 0:4], [mk], ["bv4"])
            tt("dve", a_, Bif[0:L, 0:4], cs_, ALU.add, ["Bif", "bv4"], ["bv4"])
            act(wk_, a_, AF.Exp, ["bv4"], ["bv4"])
            act(ecs_, cs_, AF.Exp, ["bv4"], ["bv4"])
            act(edL[:, :], mb[0:64, 16:20], AF.Exp, [mk], ["edL"], scale=-1.0)
            cp("dve", m4[:, 1:2], mb[0:4, 32:33], [mk], ["m4"])
            if KB < 3:
                return
            mb2, mk2 = psum("mx")
            mm(mb2[0:4, 0:L], a_, I_f[0:L, 0:L], r=["bv4", "I_f"], w=[mk2])
            P.op("dve", lambda e, o=m4[:, 0:1], i=mb2[0:4, 0:L]: e.tensor_reduce(out=o, in_=i, axis=AX.X, op=ALU.max),
                 [mk2], ["m4"])
            tt("dve", mrun[:, :], mrun[:, :], m4[:, 0:1], ALU.max, ["mrun", "m4"], ["mrun"])
            tt("dve", mrun[:, :], mrun[:, :], m4[:, 1:2], ALU.subtract, ["mrun", "m4"], ["mrun"])
            if KB < 4:
                return
            for h in range(4):
                ts("dve", ktil[0:L, 64 * h:64 * h + 64], B_k[0:L, 64 * h:64 * h + 64], wk_[:, h:h + 1], ALU.mult,
                   0.125, ALU.mult, r=["C_uv", "bv4"], w=["ktil"])
            if KB < 5:
                return
            brb, brk = psum("mx")
            brv = brb[:, 0:320].rearrange("p (h d) -> p h d", h=4)[:, :, 0:65]
            for h in range(4):
                sbk, sk = psum("sc")
                mm(sbk[0:L, 0:L], B_kT[:, h, 0:L], B_qT[:, h, 0:L], r=["B_kT", "B_qT"], w=[sk])
                stt("dve", St[0:L, 0:L], sbk[0:L, 0:L], wk_[:, h:h + 1], tri_f[0:L, 0:L], ALU.mult, ALU.mult,
                    r=[sk, "bv4", "tri_f"], w=["St"])
                mm(brv[0:L, h, :], St[0:L, 0:L], Bv[0:L, h, 0:65], True, False, r=["St", "Bv"], w=[brk])
                mm(brv[0:L, h, :], B_qT[:, h, 0:L], Cst_bf[:, h, 0:65], False, True, r=["B_qT", "Cst_bf"], w=[brk])
            if KB < 6:
                return
            ts("dve", den_, brv[0:L, :, 64], -1.0, ALU.mult, r=[brk], w=["bv4"])
            tt("dve", den_, den_, brv[0:L, :, 64], ALU.max, [brk, "bv4"], ["bv4"])
            tt("dve", den_, den_, ecs_, ALU.max, ["bv4"], ["bv4"])
            P.op("dve", lambda e, o=rd_, i=den_: e.reciprocal(out=o, in_=i), ["bv4"], ["bv4"])
            tt("dve", Bh[0:L], brv[0:L, :, 0:64], rd_.unsqueeze(2).to_broadcast([L, 4, 64]), ALU.mult,
               [brk, "bv4"], ["Bh"])
            if KB < 7:
                return
            cb, ck = psum("mx")
            cv = cb[0:64, 0:320].rearrange("p (h d) -> p h d", h=4)[:, :, 0:65]
            for h in range(4):
                mm(cv[:, h, :], ktil[0:L, 64 * h:64 * h + 64], Bv[0:L, h, 0:65], r=["ktil", "Bv"], w=[ck])
            tt("dve", Cst[:], Cst[:], cv, ALU.add, ["Cst", ck], ["Cst"])
            tt("dve", Cst[:], Cst[:], edL[:, :].unsqueeze(2).to_broadcast([64, 4, 65]), ALU.mult, ["Cst", "edL"], ["Cst"])
            cp("dve", Cst_bf[:, :, 0:65], Cst[:], ["Cst"], ["Cst_bf"])
            if KB < 8:
                return
            rs = head_rstd(Bh[0:L], L, Bsq, "Bh", 7)
            act(B_o[0:L, :], B_o[0:L, :], AF.Sigmoid, ["C_uv"], ["C_uv"])
            tt("dve", Bh[0:L], Bh[0:L], rs.unsqueeze(2).to_broadcast([L, 4, 64]), ALU.mult, ["Bh", "bv4"], ["Bh"])
            tt("dve", Bh[0:L].rearrange("p h d -> p (h d)"), Bh[0:L].rearrange("p h d -> p (h d)"), gml[0:L, :], ALU.mult, ["Bh", "gml"], ["Bh"])
            tt("dve", mix[0:L, 256:512], Bh[0:L].rearrange("p h d -> p (h d)"), B_o[0:L, :], ALU.mult, ["Bh", "C_uv"], ["mix"])

            if KFORK:
                P.join()
                curset[0] = None
            if KFORK2:
                P.fork()
                curset[0] = 0
            pb, pk = proj_tm(1856, 512, xnT, KT_, L)
            cp("act", C_uv[0:L, :], pb[0:L, 0:512], [pk], ["C_uv"])
            tt("pool", C_t[0:L, :], C_uv[0:L, :], C_uv[0:L, :], ALU.mult, ["C_uv"], ["C_t"])
            ts("pool", C_t[0:L, :], C_t[0:L, :], 0.044715, ALU.mult, 1.0, ALU.add, r=["C_t"], w=["C_t"])
            tt("pool", C_t[0:L, :], C_t[0:L, :], C_uv[0:L, :], ALU.mult, ["C_t", "C_uv"], ["C_t"])
            act(C_t[0:L, :], C_t[0:L, :], AF.Sigmoid, ["C_t"], ["C_t"], scale=1.5957691216057308)
            tt("pool", C_uv[0:L, :], C_uv[0:L, :], C_t[0:L, :], ALU.mult, ["C_t", "C_uv"], ["C_uv"])
            rmsnorm_rstd(C_uv[0:L, 256:512], L, 256, 1.0 / 256, ssq[0:L, 2:3], rstd[0:L, 2:3], C_t[0:L, 0:256], ["C_uv"], "C_t", sfx="c", lcol=2)
            stt("dve", vrows[0:L, :], C_uv[0:L, 256:512], rstd[0:L, 2:3], gcv[0:L, :], ALU.mult, ALU.mult,
                r=["C_uv", "rstdc", "gcv"], w=["vrows"])
            cp("pool", vr_bf[0:L, :], vrows[0:L, :], ["vrows"], ["vr_bf"])
            if samp is not None:
                dma(ocv[l, samp, :, :], vrows[0:L, :], r=["vrows"], w=[f"ocv{l}{samp}"])
            gb, gk = psum("mx")
            for h in range(4):
                mm(gb[0:L, 64 * h:64 * h + 64], WsT[0:L, h, 0:L], vr_bf[0:L, 64 * h:64 * h + 64], r=["WsT", "vr_bf"], w=[gk])
            for h in range(4):
                stt("dve", mix[0:L, 512 + 64 * h:576 + 64 * h], gb[0:L, 64 * h:64 * h + 64], bsT[0:L, h:h + 1],
                    C_uv[0:L, 64 * h:64 * h + 64], ALU.add, ALU.mult, r=[gk, "bsT", "C_uv"], w=["mix"])

            if KFORK2:
                P.next_thread()
                curset[0] = 1
            proj_fm(2368, xnT, KT_, L, D_qT, "D_qT", eng="act")
            proj_fm(2624, xnT, KT_, L, D_sT, "D_sT", eng="dve")
            pb, pk = proj_tm(2624, 512, xnT, KT_, L)
            act(D_sg[0:L, :], pb[0:L, 0:256], AF.Sigmoid, [pk], ["D_sg"])
            cp("dve", D_v[0:L, :], pb[0:L, 256:512], [pk], ["D_v"])
            pb, pk = proj_tm(3136, 256, xnT, KT_, L)
            act(D_gs[0:L, :], pb[0:L, 0:256], AF.Silu, [pk], ["D_gs"])
            tt("pool", D_kd[0:L, :], D_sg[0:L, :], oml[0:L, :], ALU.mult, ["D_sg", "oml"], ["D_kd"])
            tt("pool", D_lf[0:L, :], D_kd[0:L, :], lbm[0:L, :], ALU.add, ["D_kd", "lbm"], ["D_lf"])
            act(D_lf[0:L, :], D_lf[0:L, :], AF.Ln, ["D_lf"], ["D_lf"])
            tt("pool", D_kd[0:L, :], oml[0:L, :], D_kd[0:L, :], ALU.subtract, ["D_kd", "oml"], ["D_kd"])
            act(D_sT[:, :, 0:L], D_sT[:, :, 0:L], AF.Sigmoid, ["D_sT"], ["D_sT"])
            for h in range(4):
                ts("dve", D_sT[:, h, 0:L], D_sT[:, h, 0:L], nomlT[:, h:h + 1], ALU.mult, omlT[:, h:h + 1], ALU.add,
                   r=["D_sT", "nomlT", "omlT"], w=["D_sT"])
            pb, pk = psum("pj")
            pbv = pb[0:64, :].rearrange("p (a b) -> p a b", a=4)
            for h in range(4):
                mm(pbv[:, h, 0:L], D_lf[0:L, 64 * h:64 * h + 64], tri_f[0:L, 0:L], r=["D_lf", "tri_f"], w=[pk])
            cp("act", D_bT[:, :, 0:L], pbv[:, :, 0:L], [pk], ["D_bT"])
            ts("dve", D_nbT[:, :, 0:L], pbv[:, :, 0:L], -1.0, ALU.mult, r=[pk], w=["D_nbT"])
            act(D_e[:, :, 0:L], D_bT[:, :, 0:L], AF.Exp, ["D_bT"], ["D_e"])
            tt("dve", D_qh[:, :, 0:L], D_qT[:, :, 0:L], D_e[:, :, 0:L], ALU.mult, ["D_qT", "D_e"], ["D_qh"])
            bs_ = min(BS, L)
            nb = (L + BS - 1) // BS
            cp("pool", D_eq[:, :, 0:bs_], D_e[:, :, 0:bs_], ["D_e"], ["D_ek"])
            for h in range(4):
                for I in range(1, nb):
                    act(D_eq[:, h, BS * I:BS * I + BS], D_bT[:, h, BS * I:BS * I + BS], AF.Exp, ["D_bT", "D_nbT"], ["D_ek"],
                        bias=D_nbT[:, h, BS * I - 1:BS * I])
            tt("dve", D_qt[:, :, 0:L], D_qT[:, :, 0:L], D_eq[:, :, 0:L], ALU.mult, ["D_qT", "D_ek"], ["D_qt"])
            ob, ok_ = psum("mx")
            ov = ob[:, 0:256].rearrange("p (h d) -> p h d", h=4)
            for h in range(4):
                for I in range(nb):
                    n = min(L, BS * (I + 1))
                    bias = zeroT[0:64, 0:1] if I == 0 else D_bT[:, h, BS * I - 1:BS * I]
                    act(D_ek[:, I, 0:n], D_bT[:, h, 0:n], AF.Exp, ["D_bT", "zeroT"], ["D_ek"], bias=bias, scale=-1.0)
                for I in range(nb):
                    n = min(L, BS * (I + 1))
                    tt("dve", D_kI[:, I, 0:n], D_sT[:, h, 0:n], D_ek[:, I, 0:n], ALU.mult, ["D_sT", "D_ek"], ["D_kI"])
                sbk, sk = psum("sc")
                for I in range(nb):
                    n = min(L, BS * (I + 1))
                    mm(sbk[0:n, BS * I:BS * I + bs_], D_kI[:, I, 0:n], D_qt[:, h, BS * I:BS * I + bs_], r=["D_kI", "D_qt"], w=[sk])
                for I in range(nb):
                    n = min(L, BS * (I + 1))
                    tt("dve", D_AT[0:n, BS * I:BS * I + bs_], sbk[0:n, BS * I:BS * I + bs_], tri_f[0:n, BS * I:BS * I + bs_],
                       ALU.mult, [sk, "tri_f"], ["D_AT"])
                mm(ov[0:L, h, :], D_AT[0:L, 0:L], D_v[0:L, 64 * h:64 * h + 64], True, False, r=["D_AT", "D_v"], w=[ok_])
                mm(ov[0:L, h, :], D_qh[:, h, 0:L], S_bf[:, h, :], False, True, r=["D_qh", "S_bf"], w=[ok_])
            cp("act", D_o[0:L], ov[0:L], [ok_], ["Bh"])
            db, dk = psum("pj")
            mm(db[0:L, 0:256], ls_f[0:L, 0:L], D_lf[0:L, :], r=["ls_f", "D_lf"], w=[dk])
            act(D_sg[0:L, :], db[0:L, 0:256], AF.Exp, [dk], ["D_sg"])
            tt("pool", D_kb[0:L, :], D_kd[0:L, :], D_sg[0:L, :], ALU.mult, ["D_kd", "D_sg"], ["D_kb"])
            sb2, sk2 = psum("mx")
            sv2 = sb2[0:64, 0:256].rearrange("p (h d) -> p h d", h=4)
            for h in range(4):
                mm(sv2[:, h, :], D_kb[0:L, 64 * h:64 * h + 64], D_v[0:L, 64 * h:64 * h + 64], r=["D_kb", "D_v"], w=[sk2])
            for h in range(4):
                stt("dve", Sst[:, h, :], Sst[:, h, :], D_e[:, h, L - 1:L], sv2[:, h, :], ALU.mult, ALU.add,
                    r=["Sst", "D_e", sk2], w=["Sst"])
            cp("dve", S_bf[:], Sst[:], ["Sst"], ["S_bf"])
            rs = head_rstd(D_o[0:L], L, Bsq, "Bh", 8)
            tt("dve", D_o[0:L], D_o[0:L], rs.unsqueeze(2).to_broadcast([L, 4, 64]), ALU.mult, ["Bh", "bv4"], ["Bh"])
            tt("dve", D_o[0:L].rearrange("p h d -> p (h d)"), D_o[0:L].rearrange("p h d -> p (h d)"), ghg[0:L, :], ALU.mult, ["Bh", "ghg"], ["Bh"])
            tt("dve", mix[0:L, 768:1024], D_o[0:L].rearrange("p h d -> p (h d)"), D_gs[0:L, :], ALU.mult, ["Bh", "D_gs"], ["mix"])

            if KFORK2:
                P.join()
                curset[0] = None
            transposes(mix, L, 8, mixT, I_bf, "mix", "xnTr", "I_bf")
            for hf in range(2):
                pb, pk = psum("pj")
                for k in range(8):
                    mm(pb[0:L, :], mixT[:, k, 0:L], WO[:, k, 512 * hf:512 * hf + 512], k == 0, k == 7, r=["xnTr", "WO"], w=[pk])
                tt("dve", xt[0:L, 512 * hf:512 * hf + 512], xt[0:L, 512 * hf:512 * hf + 512], pb[0:L, :], ALU.add, [KX, pk], [KX])
            dma(orows, xt[0:L, :], r=[KX], w=[okey])

        def state_out(dC, dn, dm, dS, tag):
            act(m4[:, 2:3], mrun[:, :], AF.Exp, ["mrun"], ["m4"], scale=-1.0)
            ts("dve", m4[:, 16:20], I_f[0:4, 0:4], m4[:, 2:3], ALU.mult, r=["I_f", "m4"], w=["m4"])
            mb, mk = psum("mx")
            mm(mb[0:64, 0:4], ones_f[0:4, 0:64], m4[:, 16:20], r=["ones_f", "m4"], w=[mk])
            cp("dve", em0[:, :], mb[0:64, 0:4], [mk], ["em0"])
            tt("dve", outst[:], Cst[:], em0[:, :].unsqueeze(2).to_broadcast([64, 4, 65]), ALU.mult, ["Cst", "em0"], ["Bsq"])
            mb, mk = psum("mx")
            mv = mb[0:64, 0:256].rearrange("p (h d) -> p h d", h=4)
            for h in range(4):
                mm(mv[:, h, :], outst[:, h, 0:64], I_f[0:64, 0:64], r=["Bsq", "I_f"], w=[mk])
            cp("dve", Cout[:], mv, [mk], ["Bh"])
            dma(dC.rearrange("h v k -> v h k"), Cout[:], r=["Bh"], w=["oC" + tag])
            dma(dn.rearrange("h k -> k h"), outst[:, :, 64], r=["Bsq"], w=["on" + tag], slow=True)
            dma(dm.rearrange("(h o) -> h o", o=1), mrun[:, :], r=["mrun"], w=["om" + tag])
            dma(dS.rearrange("h d v -> d h v"), Sst[:], r=["Sst"], w=["oS" + tag])

        if KSTOP < 2:
            P.barrier(); ph.close(); return
        memset("dve", Cst[:], 0.0, ["Cst"]); memset("dve", Cst_bf[:], 0.0, ["Cst_bf"])
        memset("dve", Sst[:], 0.0, ["Sst"]); memset("dve", S_bf[:], 0.0, ["S_bf"])
        memset("dve", mrun[:], 0.0, ["mrun"])
        front(T, xsrc_p[0:T, :], "xsrc0", 0)
        for ci in range(nch):
            wins = [((ci - j) % NW, T) for j in range(0, min(16, ci) + 1)]
            nxt = None
            if ci + 1 < nch:
                nxt = (lambda c=ci + 1: front(T, xsrc_p[c * T:(c + 1) * T, :], f"xsrc{c}", c % 2))
            chunk(T, xsrc_p[ci * T:(ci + 1) * T, :], f"xsrc{ci}", xmid[ci * T:(ci + 1) * T, :], f"xmid{ci}", wins, ci % NW,
                  (lambda h, i0, n, Lk: MP[0:Lk, h, i0:i0 + n, :]), prompt_ci=ci, sl=ci % 2, do_front=False, mid=nxt)
            if KSTOP >= 3:
                flush_pending(1)
        if KSTOP >= 3:
            flush_pending(100)
        if KSTOP >= 4:
            state_out(oCp[l], onp[l], omp[l], oSp[l], "p")
        if KSTOP < 5:
            P.barrier(); ph.close(); return

        for s in range(4):
            for q4 in range(4):
                i = stgpos[0] % 2; stgpos[0] += 1
                dma(stg[i][:, 0:1024].rearrange("p (j c) -> p j c", j=4),
                    kc[l, s, q4 * 512:(q4 + 1) * 512, :].rearrange("(j p) c -> p j c", p=128), w=[f"stg{i}"])
                cp("pool", Kcb[:], stg[i][:, 0:1024].rearrange("p (j c) -> p j c", j=4), [f"stg{i}"], ["Pt0"])
                for h in range(4):
                    pb, pk = psum("pj")
                    pv = pb[0:64, :].rearrange("p (a b) -> p a b", a=4)
                    for jj in range(4):
                        mm(pv[:, jj, :], Kcb[:, jj, 64 * h:64 * h + 64], I_bf[:, :], r=["Pt0", "I_bf"], w=[pk])
                    j0 = q4 * 4
                    cp("act" if h % 2 == 0 else "dve", KT[:, h, j0:j0 + 4, :], pv, [pk], [f"KT{j0 + q}" for q in range(4)])
                i = stgpos[0] % 2; stgpos[0] += 1
                dma(stg[i][:, 0:1024].rearrange("p (j c) -> p j c", j=4),
                    vc[l, s, q4 * 512:(q4 + 1) * 512, :].rearrange("(j p) c -> p j c", p=128), w=[f"stg{i}"])
                for jj in range(4):
                    j = q4 * 4 + jj
                    cp("pool", VW[:, j, :, 0:64], stg[i][:, jj * 256:(jj + 1) * 256].rearrange("p (h d) -> p h d", h=4),
                       [f"stg{i}"], [f"VW{j}"])
            dma(Cin[:], mC[l, s].rearrange("h v k -> v h k"), w=["Bh"])
            dma(nin[:], mn[l, s].rearrange("h k -> k h"), w=["nin"], slow=True)
            dma(em0[:], dap(mm_, (l * 4 + s) * 4, [[0, 64], [1, 4]]), w=["em0"])
            dma(mrun[:], dap(mm_, (l * 4 + s) * 4, [[1, 4], [1, 1]]), w=["mrun"])
            dma(Sst[:], hS[l, s].rearrange("h d v -> d h v"), w=["Sst"])
            cp("dve", S_bf[:], Sst[:], ["Sst"], ["S_bf"])
            act(em0[:], em0[:], AF.Exp, ["em0"], ["em0"])
            mb, mk = psum("mx")
            mv = mb[0:64, 0:256].rearrange("p (h d) -> p h d", h=4)
            for h in range(4):
                mm(mv[:, h, :], Cin[:, h, :], I_f[0:64, 0:64], r=["Bh", "I_f"], w=[mk])
            tt("dve", Cst[:, :, 0:64], mv, em0[:, :].unsqueeze(2).to_broadcast([64, 4, 64]), ALU.mult, [mk, "em0"], ["Cst"])
            tt("dve", Cst[:, :, 64], nin[:], em0[:], ALU.mult, ["nin", "em0"], ["Cst"])
            cp("dve", Cst_bf[:, :, 0:65], Cst[:], ["Cst"], ["Cst_bf"])
            wins = [(j, T) for j in range(16)] + [(16, 4)]
            r0 = NTP + 4 * s
            chunk(4, xsrc_s[4 * s:4 * s + 4, :], f"xsrcs{s}", xmid[r0:r0 + 4, :], f"xmids{s}", wins, 16,
                  (lambda h, i0, n, Lk: MS[0:Lk, h, i0:i0 + n, :]), samp=s)
            state_out(oCs[l, s], ons[l, s], oms[l, s], oSs[l, s], f"s{s}")
        P.barrier()
        ph.close()

    def phase_F(l, last):
        ph = ExitStack()
        WU = sb(f"wup{l}", [128, 8, DFF], BF, ph)
        WD = sb(f"wdn{l}", [128, 32, DM], BF, ph)
        gT = sb(f"gTf{l}", [128, 8], stack=ph)
        dma(gT[:, 0:8], dap(g_mlp, l * DM, [[1, 128], [128, 8]]), w=["gT"], slow=True)
        load_weights(WU, w_up[l], 8, DFF, gT, lambda k: k, "WU")
        load_weights(WD, w_down[l], 32, DM, None, None, "WD")
        hTs = [sb("hT", [128, 32, 128], BF, ph) for _ in range(2)]
        rls = [sb("rl", [128, 4, 128], stack=ph) for _ in range(2)]
        gfin = None
        if last:
            gfin = sb("gfin", [128, DM], stack=ph)
            dma(gfin[:], dap(g_final, 0, [[0, 128], [1, DM]]), w=["gfin"])
        rlpos = [0]

        def fchunk(L, rows_in, kin, rows_out, kout, sl):
            xt_, xn_, xnT_, hT = xts[sl], xns[sl], xnTs[sl], hTs[sl]
            kx, kn, kt, kh = f"xt{sl}", f"xn{sl}", f"xnT{sl}", f"hT{sl}"
            sf = str(sl)
            dma(xt_[0:L, :], rows_in, r=[kin], w=[kx])
            rmsnorm_rstd(xt_[0:L, :], L, DM, 1.0 / DM, ssq[0:L, 4 + sl:5 + sl], rstd[0:L, 4 + sl:5 + sl], xn_[0:L, :], [kx], kn, sfx=sf, lcol=4 + sl)
            ts("dve", xn_[0:L, :], xt_[0:L, :], rstd[0:L, 4 + sl:5 + sl], ALU.mult, r=[kx, "rstd" + sf], w=[kn])
            transposes(xn_, L, 8, xnT_, I_bf, kn, kt, "I_bf")
            for g in range(8):
                pb, pk = psum("sc" if g % 2 else "mx")
                pv = pb[:, :].rearrange("p (a b) -> p a b", a=4)
                for q in range(4):
                    f = 4 * g + q
                    for k in range(8):
                        mm(pv[:, q, 0:L], WU[:, k, 128 * f:128 * f + 128], xnT_[:, k, 0:L], k == 0, k == 7, r=["WU", kt], w=[pk])
                ri = rlpos[0] % 2
                rlpos[0] += 1
                rl = rls[ri]
                act(rl[:, :, 0:L], pv[:, :, 0:L], AF.Relu, [pk], [f"rl{ri}"])
                tt("pool" if g % 2 else "dve", hT[:, 4 * g:4 * g + 4, 0:L], rl[:, :, 0:L], rl[:, :, 0:L], ALU.mult, [f"rl{ri}"], [kh])
            for hf in range(2):
                pb, pk = psum("pj")
                for f in range(32):
                    mm(pb[0:L, :], hT[:, f, 0:L], WD[:, f, 512 * hf:512 * hf + 512], f == 0, f == 31, r=[kh, "WD"], w=[pk])
                tt("dve", xt_[0:L, 512 * hf:512 * hf + 512], xt_[0:L, 512 * hf:512 * hf + 512], pb[0:L, :], ALU.add, [kx, pk], [kx])
            if last:
                rmsnorm_rstd(xt_[0:L, :], L, DM, 1.0 / DM, ssq[0:L, 6 + sl:7 + sl], rstd[0:L, 6 + sl:7 + sl], xn_[0:L, :], [kx], kn, sfx="f" + sf, lcol=6 + sl)
                stt("dve", xt_[0:L, :], xt_[0:L, :], rstd[0:L, 6 + sl:7 + sl], gfin[0:L, :], ALU.mult, ALU.mult, r=[kx, "rstdf" + sf, "gfin"], w=[kx])
            dma(rows_out, xt_[0:L, :], r=[kx], w=[kout])

        for ci in range(nch):
            dst = y_p[ci * T:(ci + 1) * T, :] if last else xnext[ci * T:(ci + 1) * T, :]
            fchunk(T, xmid[ci * T:(ci + 1) * T, :], f"xmid{ci}", dst, f"xsrc{ci}", ci % 2)
        dst = y_s[:, :] if last else xnext[NTP:NTP + 16, :]
        fchunk(16, xmid[NTP:NTP + 16, :], "xmids_all", dst, "xsrcs_all", nch % 2)
        P.barrier()
        ph.close()

    ltm = sb("ltm", [128, 128])
    tt("dve", ltm[:], ls_f[:], I_f[:], ALU.add, ["ls_f", "I_f"], ["ltm"])

    if KSTOP >= 1:
        phase_M(0, xp, xs)
    if KSTOP >= 6:
        phase_F(0, False)
    if KSTOP >= 7 and os.environ.get("KSKIPM1", "0") != "1":
        phase_M(1, xnext, xnext[NTP:NTP + 16, :])
    if KSTOP >= 8:
        phase_F(1, True)

    semnames = sorted(P.final.keys())
    sems = {s: es.enter_context(nc.semaphore(s)) for s in semnames}
    blk = es.enter_context(nc.Block())

    def run(engname, handle):
        for waits, fn, inc in P.ops[engname]:
            for s, v in waits:
                handle.wait_ge(sems[s], v)
            if fn is not None:
                fn(handle).then_inc(sems[inc[0]], inc[1])

    @blk.tensor
    def _(e):
        run("pe", e)

    @blk.scalar
    def _(e):
        run("act", e)

    @blk.vector
    def _(e):
        run("dve", e)

    @blk.gpsimd
    def _(e):
        run("pool", e)

    @blk.sync
    def _(e):
        run("sp", e)
        for s, v in P.final.items():
            e.wait_ge(sems[s], v)

    es.close()
    return nc


def _consts():
    i = np.arange(128)
    cI = np.eye(128, dtype=np.float32)
    cJ = cI[::-1].copy()
    cTri = (i[:, None] <= i[None, :]).astype(np.float32)
    cLs = (i[:, None] > i[None, :]).astype(np.float32)
    mult = np.zeros(2049, np.float32)
    for w, d in ((128, 1), (512, 4), (2048, 16)):
        mult[np.arange(w // d + 1) * d] += 1.0
    o = np.arange(2049)
    dd = np.maximum(o, 1).astype(np.float32)
    large = 16 + (np.log(dd / 16) / np.float32(np.log(2048 / 16)) * 16).astype(np.int32)
    large = np.clip(large, 16, 31)
    bucket = np.where(o < 16, o, large)
    MO = np.zeros((32, TABN), np.float32)
    MO[bucket, o + 127] = mult
    MOr = MO[:, ::-1][:, -TABN:].copy()
    MOr = np.zeros((32, TABN), np.float32)
    MOr[:, 0:2303] = MO[:, 0:2303][:, ::-1]
    return cI, cJ, cTri, cLs, MO, MOr


_NC_CACHE = {}


def kernel(x_prompt, x_sample, cache_k_win, cache_v_win, state_mlstm_C, state_mlstm_n, state_mlstm_m,
           state_hgrn_S, rel_bias, w_in, w_out, g_attn, g_mlp, w_up, w_down, b_i, b_f, g_mlstm, g_cv,
           w_s, b_s, hgrn_lb, g_hgrn, g_final):
    f = lambda a: np.ascontiguousarray(np.asarray(a, dtype=np.float32))
    nch = NCH
    if nch not in _NC_CACHE:
        _NC_CACHE[nch] = build(nch)
    nc = _NC_CACHE[nch]
    cI, cJ, cTri, cLs, MO, MOr = _consts()
    shared = dict(relb=f(rel_bias), w_in=f(w_in), w_out=f(w_out), g_attn=f(g_attn), g_mlp=f(g_mlp), w_up=f(w_up),
                  w_down=f(w_down), b_i=f(b_i), b_f=f(b_f), g_mlstm=f(g_mlstm), g_cv=f(g_cv), w_s=f(w_s), b_s=f(b_s),
                  hlb=f(hgrn_lb), g_hgrn=f(g_hgrn), g_final=f(g_final).reshape(1, DM),
                  cI=cI, cJ=cJ, cTri=cTri, cLs=cLs, cMO=MO, cMOr=MOr)
    xp_ = f(x_prompt); xs_ = f(x_sample)
    kc_ = f(cache_k_win); vc_ = f(cache_v_win)
    in_maps = []
    for c in range(8):
        sl = slice(4 * c, 4 * c + 4)
        m = dict(shared)
        m["xp"] = np.ascontiguousarray(xp_[c % 2, :nch * T])
        m["xs"] = np.ascontiguousarray(xs_[sl].reshape(16, DM))
        m["kc"] = np.ascontiguousarray(kc_[:, sl].reshape(2, 4, 2048, 256))
        m["vc"] = np.ascontiguousarray(vc_[:, sl].reshape(2, 4, 2048, 256))
        m["mC"] = f(state_mlstm_C)[:, sl].copy()
        m["mn"] = f(state_mlstm_n)[:, sl].copy()
        m["mm"] = f(state_mlstm_m)[:, sl].copy()
        m["hS"] = f(state_hgrn_S)[:, sl].copy()
        in_maps.append(m)
    res = run_bass_kernel_spmd(nc, in_maps, core_ids=list(range(8))).results
    B = 2
    cat = lambda name, ax: np.concatenate([res[c][name] for c in range(8)], axis=ax)
    y_p = np.zeros((B, 8192, DM), np.float32)
    y_p[:, :nch * T] = np.stack([res[b]["y_p"] for b in range(B)])
    y_s = cat("y_s", 0).reshape(32, 4, DM)
    stack2 = lambda name: np.stack([res[b][name] for b in range(B)], axis=1)
    kwp = stack2("kwp").reshape(2, B, 2048, 4, 64)
    vwp = stack2("vwp").reshape(2, B, 2048, 4, 64)
    kws = cat("kws", 1).reshape(2, 32, 2048, 4, 64)
    vws = cat("vws", 1).reshape(2, 32, 2048, 4, 64)
    Cp = stack2("oCp"); np_ = stack2("onp"); mp = stack2("omp")
    Cs = cat("oCs", 1); ns = cat("ons", 1); ms = cat("oms", 1)
    Sp = stack2("oSp"); Ss = cat("oSs", 1)
    cvs = cat("ocv", 1).reshape(2, 32, 4, 4, 64)
    return (y_p, y_s, kwp, vwp, kws, vws, Cp, np_, mp, Cs, ns, ms, Sp, Ss, cvs)
```

```python
import os
from contextlib import ExitStack
import numpy as np
import concourse.bass as bass
import concourse.mybir as mybir
from concourse.bass_utils import run_bass_kernel_spmd

F32 = mybir.dt.float32
BF = mybir.dt.bfloat16
AF = mybir.ActivationFunctionType
ALU = mybir.AluOpType
AX = mybir.AxisListType

NCH = int(os.environ.get("KNCH", "64"))
SAME_SYNC = os.environ.get("KSAME", "1") == "1"
KSTOP = int(os.environ.get("KSTOP", "99"))
KCUT = int(os.environ.get("KCUT", "99"))
KB = int(os.environ.get("KB", "99"))
KFM = int(os.environ.get("KFORK", "1"))
KFORK = KFM in (1, 2)
KFORK2 = KFM in (1, 3)
KP = int(os.environ.get("KP", "99"))
T = 128
DM = 1024
DIN = 3336
DFF = 4096
BS = 64
NW = 17
TABN = 2304
EPS = 1e-6
COMPUTE = ("pe", "act", "dve", "pool")


class Prog:
    def __init__(self, kdma=8):
        self.ops = {e: [] for e in ("pe", "act", "dve", "pool", "sp")}
        self.cnt = {e: 0 for e in self.ops}
        self.dcnt = {e: 0 for e in self.ops}
        self.lastw = {}
        self.readers = {}
        self.waited = {e: {} for e in self.ops}
        self.K = kdma
        self.final = {}

    def fork(self):
        self.threads = [[]]

    def next_thread(self):
        self.threads.append([])

    def join(self):
        lists = self.threads
        self.threads = None
        tot = [max(1, len(x)) for x in lists]
        pos = [0] * len(lists)
        while True:
            best, bf = -1, 2.0
            for i, x in enumerate(lists):
                if pos[i] < len(x):
                    f = pos[i] / tot[i]
                    if f < bf:
                        best, bf = i, f
            if best < 0:
                break
            a = lists[best][pos[best]]
            pos[best] += 1
            self.op(*a)

    def op(self, eng, fn, r=(), w=(), dma=False):
        if getattr(self, "threads", None) is not None:
            self.threads[-1].append((eng, fn, tuple(r), tuple(w), dma))
            return None
        deps = []
        for x in r:
            t = self.lastw.get(x)
            if t:
                deps.append(t)
            if x.startswith("ps"):
                for s, (v, e) in self.readers.get(x, {}).items():
                    if e != eng:
                        deps.append((s, v, e, s.startswith("d_")))
        for x in w:
            t = self.lastw.get(x)
            if t:
                deps.append(t)
            for s, (v, e) in self.readers.get(x, {}).items():
                deps.append((s, v, e, s.startswith("d_")))
        if dma:
            i = self.dcnt[eng]
            self.dcnt[eng] += 1
            sem = f"d_{eng}{i % self.K}"
            val = 16 * (i // self.K + 1)
            if i >= self.K:
                deps.append((sem, val - 16, eng, True))
            tok = (sem, val, eng, True)
            inc = (sem, 16)
        else:
            self.cnt[eng] += 1
            sem = f"c_{eng}"
            val = self.cnt[eng]
            tok = (sem, val, eng, False)
            inc = (sem, 1)
        self.final[sem] = val
        waits = {}
        for (s, v, e, isd) in deps:
            if e == eng and not isd and not dma:
                if eng == "pe" or not SAME_SYNC:
                    continue
            if self.waited[eng].get(s, 0) >= v:
                continue
            waits[s] = max(waits.get(s, 0), v)
        for s, v in waits.items():
            self.waited[eng][s] = v
        self.ops[eng].append((list(waits.items()), fn, inc))
        for x in r:
            d = self.readers.setdefault(x, {})
            if d.get(sem, (0, None))[0] < val:
                d[sem] = (val, eng)
        for x in w:
            self.lastw[x] = tok
            self.readers[x] = {}
        return tok

    def barrier(self):
        for eng in self.ops:
            waits = []
            for s, v in self.final.items():
                if s == f"c_{eng}" and eng == "pe":
                    continue
                if self.waited[eng].get(s, 0) >= v:
                    continue
                self.waited[eng][s] = v
                waits.append((s, v))
            if waits:
                self.ops[eng].append((waits, None, None))


def build(nch):
    nc = bass.Bass("TRN2", target_bir_lowering=False)
    P = Prog()
    es = ExitStack()

    def din(name, shape):
        return nc.dram_tensor(name, list(shape), F32, kind="ExternalInput").ap()

    def dout(name, shape):
        return nc.dram_tensor(name, list(shape), F32, kind="ExternalOutput").ap()

    def dint(name, shape):
        return nc.dram_tensor(name, list(shape), F32, kind="Internal").ap()

    NTP = nch * T
    xp = din("xp", [NTP, DM])
    xs = din("xs", [16, DM])
    kc = din("kc", [2, 4, 2048, 256])
    vc = din("vc", [2, 4, 2048, 256])
    mC = din("mC", [2, 4, 4, 64, 64])
    mn = din("mn", [2, 4, 4, 64])
    mm_ = din("mm", [2, 4, 4])
    hS = din("hS", [2, 4, 4, 64, 64])
    relb = din("relb", [32, 4])
    w_in = din("w_in", [2, DM, DIN])
    w_out = din("w_out", [2, DM, DM])
    g_attn = din("g_attn", [2, DM])
    g_mlp = din("g_mlp", [2, DM])
    w_up = din("w_up", [2, DM, DFF])
    w_down = din("w_down", [2, DFF, DM])
    b_i = din("b_i", [2, 4])
    b_f = din("b_f", [2, 4])
    g_mlstm = din("g_mlstm", [2, 256])
    g_cv = din("g_cv", [2, 256])
    w_s = din("w_s", [2, 4, 128, 128])
    b_s = din("b_s", [2, 4, 128])
    hlb = din("hlb", [2, 256])
    g_hgrn = din("g_hgrn", [2, 256])
    g_final = din("g_final", [1, DM])
    cI = din("cI", [128, 128])
    cJ = din("cJ", [128, 128])
    cTri = din("cTri", [128, 128])
    cLs = din("cLs", [128, 128])
    cMO = din("cMO", [32, TABN])
    cMOr = din("cMOr", [32, TABN])

    y_p = dout("y_p", [NTP, DM])
    y_s = dout("y_s", [16, DM])
    kwp = dout("kwp", [2, 2048, 256])
    vwp = dout("vwp", [2, 2048, 256])
    kws = dout("kws", [2, 4, 2048, 256])
    vws = dout("vws", [2, 4, 2048, 256])
    oCp = dout("oCp", [2, 4, 64, 64])
    onp = dout("onp", [2, 4, 64])
    omp = dout("omp", [2, 4])
    oCs = dout("oCs", [2, 4, 4, 64, 64])
    ons = dout("ons", [2, 4, 4, 64])
    oms = dout("oms", [2, 4, 4])
    oSp = dout("oSp", [2, 4, 64, 64])
    oSs = dout("oSs", [2, 4, 4, 64, 64])
    ocv = dout("ocv", [2, 4, 4, 256])

    xmid = dint("xmid", [NTP + 16, DM])
    xnext = dint("xnext", [NTP + 16, DM])
    wtab = dint("wtab", [4, TABN])
    wtabr = dint("wtabr", [4, TABN])

    uid = [0]

    def sb(name, shape, dt=F32, stack=None):
        uid[0] += 1
        return (stack or es).enter_context(nc.sbuf_tensor(f"{name}_{uid[0]}", list(shape), dt))

    banks = [es.enter_context(nc.psum_tensor(f"ps{i}", [128, 512], F32)) for i in range(8)]
    rings = {"pj": [0, 1, 2], "sc": [3, 4], "pv": [5], "mx": [6, 7]}
    rpos = {k: 0 for k in rings}

    ringsets = {0: {"pj": [0], "sc": [1, 2], "pv": [3], "mx": [3]},
                1: {"pj": [4, 5], "sc": [6], "mx": [7], "pv": [7]}}
    curset = [None]

    def psum(role):
        rg = rings if curset[0] is None else ringsets[curset[0]]
        i = rg[role][rpos[role] % len(rg[role])]
        rpos[role] += 1
        return banks[i], f"ps{i}"

    def mm(out, lhsT, rhs, start=True, stop=True, r=(), w=()):
        P.op("pe", lambda e, o=out, a=lhsT, b=rhs, s=start, t=stop: e.matmul(o, lhsT=a, rhs=b, start=s, stop=t), r, w)

    def act(out, in_, func, r=(), w=(), bias=None, scale=None, accum=None):
        kw = {}
        if bias is not None:
            kw["bias"] = bias
        if scale is not None:
            kw["scale"] = scale
        if accum is not None:
            kw["accum_out"] = accum
        P.op("act", lambda e, o=out, i=in_, f=func, k=kw: e.activation(out=o, in_=i, func=f, **k), r, w)

    def tt(eng, out, a, b, op, r=(), w=()):
        P.op(eng, lambda e, o=out, x=a, y=b, p=op: e.tensor_tensor(out=o, in0=x, in1=y, op=p), r, w)

    def ts(eng, out, a, s1, op0, s2=None, op1=None, r=(), w=()):
        if op1 is None:
            P.op(eng, lambda e, o=out, x=a, q=s1, p=op0: e.tensor_scalar(out=o, in0=x, scalar1=q, scalar2=None, op0=p), r, w)
        else:
            P.op(eng, lambda e, o=out, x=a, q=s1, p=op0, q2=s2, p2=op1: e.tensor_scalar(out=o, in0=x, scalar1=q, scalar2=q2, op0=p, op1=p2), r, w)

    def stt(eng, out, a, s, b, op0, op1, r=(), w=()):
        P.op(eng, lambda e, o=out, x=a, q=s, y=b, p=op0, p2=op1: e.scalar_tensor_tensor(out=o, in0=x, scalar=q, in1=y, op0=p, op1=p2), r, w)

    def cp(eng, out, in_, r=(), w=()):
        if eng == "act":
            P.op("act", lambda e, o=out, i=in_: e.copy(out=o, in_=i), r, w)
        else:
            P.op(eng, lambda e, o=out, i=in_: e.tensor_copy(out=o, in_=i), r, w)

    def memset(eng, ap, val, w=()):
        P.op(eng, lambda e, a=ap, v=val: e.memset(a, v), (), w)

    def dma(out, in_, r=(), w=(), eng="sp", slow=False):
        if slow:
            P.op(eng, lambda e, o=out, i=in_: e.dma_start(out=o, in_=i, allow_slow_non_contiguous=True), r, w, dma=True)
        else:
            P.op(eng, lambda e, o=out, i=in_: e.dma_start(out=o, in_=i), r, w, dma=True)

    def dap(base, off, pat):
        return bass.AP(base.tensor, off, [list(p) for p in pat])

    I_f = sb("I_f", [128, 128]); J_f = sb("J_f", [128, 128]); tri_f = sb("tri_f", [128, 128])
    ls_f = sb("ls_f", [128, 128]); ones_f = sb("ones_f", [128, 128])
    I_bf = sb("I_bf", [128, 128], BF); J_bf = sb("J_bf", [128, 128], BF)
    epsT = sb("epsT", [128, 1]); oneT = sb("oneT", [128, 1]); zeroT = sb("zeroT", [128, 1])
    dma(I_f[:], cI[:, :], w=["I_f"]); dma(J_f[:], cJ[:, :], w=["J_f"])
    dma(tri_f[:], cTri[:, :], w=["tri_f"]); dma(ls_f[:], cLs[:, :], w=["ls_f"])
    memset("pool", ones_f[:], 1.0, ["ones_f"]); memset("pool", epsT[:], EPS, ["epsT"])
    memset("pool", oneT[:], 1.0, ["oneT"]); memset("pool", zeroT[:], 0.0, ["zeroT"])
    cp("pool", I_bf[:], I_f[:], ["I_f"], ["I_bf"]); cp("pool", J_bf[:], J_f[:], ["J_f"], ["J_bf"])

    with ExitStack() as st0:
        rb = sb("rb", [32, 4], stack=st0); erb = sb("erb", [32, 4], stack=st0)
        mo = sb("mo", [32, TABN], stack=st0); wt = sb("wt", [4, TABN], stack=st0)
        dma(rb[:], relb[:, :], w=["rb"])
        act(erb[:], rb[:], AF.Exp, ["rb"], ["erb"])
        for src, dst in ((cMO, wtab), (cMOr, wtabr)):
            dma(mo[:], src[:, :], w=["mo"])
            for c0 in range(0, TABN, 512):
                n = min(512, TABN - c0)
                pb, pk = psum("mx")
                mm(pb[0:4, 0:n], erb[:, :], mo[:, c0:c0 + n], r=["erb", "mo"], w=[pk])
                cp("dve", wt[:, c0:c0 + n], pb[0:4, 0:n], [pk], ["wt"])
            dma(dst[:, :], wt[:], r=["wt"], w=["wtab" if dst is wtab else "wtabr"])
        P.barrier()

    pending = []
    for l in range(2):
        for s in range(4):
            pending.append((kws[l, s, 0:2044, :], kc[l, s, 4:2048, :], f"kws{l}{s}"))
            pending.append((vws[l, s, 0:2044, :], vc[l, s, 4:2048, :], f"vws{l}{s}"))

    def flush_pending(n):
        for _ in range(n):
            if pending:
                o, i, k = pending.pop(0)
                dma(o, i, w=[k], eng="pool")

    xt = sb("xt", [128, DM]); xn = sb("xn", [128, DM], BF)
    xnT = sb("xnT", [128, 8, 128], BF)
    xts = [xt, sb("xt2", [128, DM])]
    xns = [xn, sb("xn2", [128, DM], BF)]
    xnTs = [xnT, sb("xnT2", [128, 8, 128], BF)]
    ssq = sb("ssq", [128, 8]); rstd = sb("rstd", [128, 8]); lnv = sb("lnv", [128, 8])
    stg = [sb(f"stg{i}", [128, 1088]) for i in range(2)]
    stgpos = [0]

    def rmsnorm_rstd(src, L, ncol, scale, ssq_ap, rstd_ap, junk, keys_r, key_junk, sfx="", lcol=0):
        memset("dve", ssq_ap, 0.0, ["ssq" + sfx])
        act(junk, src, AF.Square, keys_r + ["ssq" + sfx], [key_junk, "ssq" + sfx], accum=ssq_ap)
        act(lnv[0:L, lcol:lcol + 1], ssq_ap, AF.Ln, ["ssq" + sfx, "epsT"], ["lnv" + sfx], bias=epsT[0:L, 0:1], scale=scale)
        act(rstd_ap, lnv[0:L, lcol:lcol + 1], AF.Exp, ["lnv" + sfx], ["rstd" + sfx], scale=-0.5)

    def transposes(src, L, nk, dst, mat, kr, kw, kmat):
        for g in range(0, nk, 4):
            pb, pk = psum("pj")
            pv = pb[:, :].rearrange("p (a b) -> p a b", a=4)
            for k in range(g, min(g + 4, nk)):
                mm(pv[:, k - g, 0:L], src[0:L, k * 128:(k + 1) * 128], mat[0:L, 0:L], r=[kr, kmat], w=[pk])
            n = min(4, nk - g)
            cp("act" if (g // 4) % 2 == 0 else "dve", dst[:, g:g + n, 0:L], pv[:, 0:n, 0:L], [pk], [kw])

    def load_weights(dst3, src2, nk, ncols, gT, gsel, keyw, segs=None, bufs=None):
        if segs is None:
            segs = [(0, ncols, 0)]
        segs = [(c0 + o, min(1024, n - o), d0 + o) for (c0, n, d0) in segs for o in range(0, n, 1024)]
        for k in range(nk):
            for (c0, n, d0) in segs:
                sg = bufs if bufs is not None else stg
                i = stgpos[0] % len(sg)
                stgpos[0] += 1
                ce = "pool" if (bufs is None or i % 2 == 0) else "dve"
                dma(sg[i][:, 0:n], src2[k * 128:(k + 1) * 128, c0:c0 + n], w=[f"stg{i}"], eng=("sp" if i % 2 == 0 else "act"))
                gi = gsel(k) if gT is not None else None
                if gi is None:
                    cp(ce, dst3[:, k, d0:d0 + n], sg[i][:, 0:n], [f"stg{i}"], [keyw])
                else:
                    ts(ce, dst3[:, k, d0:d0 + n], sg[i][:, 0:n], gT[:, gi:gi + 1], ALU.mult,
                       r=[f"stg{i}", "gT"], w=[keyw])

    def phase_M(l, xsrc_p, xsrc_s):
        ph = ExitStack()
        W = sb(f"win{l}", [128, 8, 3392], BF, ph)
        WO = sb(f"wout{l}", [128, 8, DM], BF, ph)
        gT = sb(f"gT{l}", [128, 12], stack=ph)
        dma(gT[:, 0:8], dap(g_attn, l * DM, [[1, 128], [128, 8]]), w=["gT"], slow=True)
        dma(gT[:, 8:10], dap(g_mlstm, l * 256, [[1, 128], [128, 2]]), w=["gT"], slow=True)
        dma(gT[:, 10:12], dap(g_hgrn, l * 256, [[1, 128], [128, 2]]), w=["gT"], slow=True)
        load_weights(W, w_in[l], 8, DIN, gT, lambda k: k, "W", segs=[(0, 1800, 0), (1800, 1536, 1856)])
        load_weights(WO, w_out[l], 8, DM, None, None, "WO")

        bfi = sb(f"bfi{l}", [128, 8], stack=ph)
        dma(bfi[:, 0:4], dap(b_i, l * 4, [[0, 128], [1, 4]]), w=["bfi"])
        dma(bfi[:, 4:8], dap(b_f, l * 4, [[0, 128], [1, 4]]), w=["bfi"])
        gcv = sb(f"gcv{l}", [128, 256], stack=ph)
        dma(gcv[:], dap(g_cv, l * 256, [[0, 128], [1, 256]]), w=["gcv"])
        gml = sb(f"gml{l}", [128, 256], stack=ph); ghg = sb(f"ghg{l}", [128, 256], stack=ph)
        dma(gml[:], dap(g_mlstm, l * 256, [[0, 128], [1, 256]]), w=["gml"])
        dma(ghg[:], dap(g_hgrn, l * 256, [[0, 128], [1, 256]]), w=["ghg"])
        oml = sb(f"oml{l}", [128, 256], stack=ph)
        lbm = sb(f"lbm{l}", [128, 256], stack=ph)
        C_uv = sb("C_uv", [128, 512], stack=ph); C_t = sb("C_t", [128, 512], stack=ph)
        lbt = C_uv[:, 0:256]
        lbT = sb(f"lbT{l}", [64, 4], stack=ph); omlT = sb(f"omlT{l}", [64, 4], stack=ph); nomlT = sb(f"nomlT{l}", [64, 4], stack=ph)
        if l == 0:
            memset("dve", lbt, 0.0, ["C_uv"]); memset("dve", lbT[:], 0.0, ["lbT"])
        else:
            t0 = C_t[:, 0:256]; t1 = C_t[:, 256:512]
            dma(t0, dap(hlb, 0, [[0, 128], [1, 256]]), w=["C_t"])
            dma(t1, dap(hlb, 256, [[0, 128], [1, 256]]), w=["C_t"])
            tt("dve", t1, t1, t0, ALU.subtract, ["C_t"], ["C_t"])
            act(lbt, t1, AF.Sigmoid, ["C_t"], ["C_uv"])
            u0 = sb("lbtmp2", [64, 4], stack=ph); u1 = sb("lbtmp3", [64, 4], stack=ph)
            dma(u0[:], dap(hlb, 0, [[1, 64], [64, 4]]), w=["lbu0"], slow=True)
            dma(u1[:], dap(hlb, 256, [[1, 64], [64, 4]]), w=["lbu1"], slow=True)
            tt("dve", u1[:], u1[:], u0[:], ALU.subtract, ["lbu0", "lbu1"], ["lbu1"])
            act(lbT[:], u1[:], AF.Sigmoid, ["lbu1"], ["lbT"])
        ts("dve", oml[:], lbt, -1.0, ALU.mult, 1.0, ALU.add, r=["C_uv"], w=["oml"])
        ts("dve", lbm[:], lbt, 1e-30, ALU.max, r=["C_uv"], w=["lbm"])
        ts("dve", omlT[:], lbT[:], -1.0, ALU.mult, 1.0, ALU.add, r=["lbT"], w=["omlT"])
        ts("dve", nomlT[:], omlT[:], -1.0, ALU.mult, r=["omlT"], w=["nomlT"])
        WsT = sb(f"WsT{l}", [128, 4, 128], BF, ph); bsT = sb(f"bsT{l}", [128, 4], stack=ph)
        dma(bsT[:], dap(b_s, l * 512, [[1, 128], [128, 4]]), w=["bsT"], slow=True)
        wsm = sb("wsm", [128, 128], BF, ph)
        for h in range(4):
            i = stgpos[0] % 2
            stgpos[0] += 1
            dma(stg[i][:, 0:128], w_s[l, h, :, :], w=[f"stg{i}"])
            tt("dve", wsm[:], stg[i][:, 0:128], ltm[:], ALU.mult, [f"stg{i}", "ltm"], ["wsm"])
            pb, pk = psum("pj")
            mm(pb[:, 0:128], wsm[:, :], I_bf[:, :], r=["wsm", "I_bf"], w=[pk])
            cp("dve", WsT[:, h, :], pb[:, 0:128], [pk], ["WsT"])

        KT = sb("KT", [64, 4, NW, 128], BF, ph)
        VW = sb("VW", [128, NW, 4, 96], BF, ph)
        MP = sb("MP", [128, 4, NW, 128], BF, ph)
        MS = sb("MS", [128, 4, NW, 4], stack=ph)
        for h in range(4):
            for (j0, nj) in ((0, 6), (6, 6), (12, 5)):
                i = stgpos[0] % 2
                stgpos[0] += 1
                mstg = stg[i][:, 0:nj * 128].rearrange("p (j t) -> p j t", j=nj)
                dma(mstg, dap(wtab, h * TABN + 128 * j0, [[1, 128], [128, nj], [1, 128]]), r=["wtab"], w=[f"stg{i}"])
                cp("pool", MP[:, h, j0:j0 + nj, :], mstg, [f"stg{i}"], ["MP"])
            for t in range(4):
                dma(MS[:, h, :, t], dap(wtabr, h * TABN + 127 - t, [[1, 128], [128, NW]]), r=["wtabr"], w=["MS"], slow=True)
        memset("pool", VW[:], 1.0, [f"VW{j}" for j in range(NW)])

        A_qT = sb("A_qT", [64, 4, 128], BF, ph)
        A_kv = stg[0][:, 0:512]
        pexps = [sb("pexp", [128, 4, 128], stack=ph) for _ in range(2)]
        Pts = [sb("Pt", [128, NW, 128], BF, ph) for _ in range(2)]
        rden = sb("rden", [128, 4], stack=ph)
        mix = sb("mix", [128, DM], BF, ph)
        xnTr = sb("xnTr", [128, 8, 128], BF, ph)
        mixT = xnTr
        Kcb = Pts[0][:, 0:8, :].rearrange("p (a b) c -> p a (b c)", a=4)
        B_qT = sb("B_qT", [64, 4, 128], BF, ph); B_kT = sb("B_kT", [64, 4, 128], BF, ph)
        B_k = C_uv[:, 0:256]; Bv = sb("Bv", [128, 4, 96], BF, ph)
        B_o = C_uv[:, 256:512]; Bif = sb("Bif", [128, 8], stack=ph)
        bv4 = sb("bv4", [128, 64], stack=ph)
        ktil = sb("ktil", [128, 256], BF, ph); St = sb("St", [128, 128], BF, ph)
        Cst = sb("Cst", [64, 4, 65], stack=ph); Cst_bf = sb("Cst_bf", [64, 4, 96], BF, ph)
        edL = sb("edL", [64, 4], stack=ph)
        Bh = sb("Bh", [128, 4, 64], stack=ph); Bsq65 = sb("Bsq", [128, 4, 80], stack=ph); Bsq = Bsq65[:, :, 0:64]
        mrun = sb("mrun", [4, 1], stack=ph); m4 = sb("m4", [4, 32], stack=ph)
        memset("pool", Bv[:], 1.0, ["Bv"])
        vrows = sb("vrows", [128, 256], stack=ph); vr_bf = sb("vr_bf", [128, 256], BF, ph)
        D_qT = sb("D_qT", [64, 4, 128], stack=ph); D_sT = sb("D_sT", [64, 4, 128], stack=ph)
        D_bT = sb("D_bT", [64, 4, 128], stack=ph); D_nbT = sb("D_nbT", [64, 4, 128], stack=ph)
        D_e = sb("D_e", [64, 4, 128], stack=ph)
        D_ek = sb("D_ek", [64, 4, 128], stack=ph)
        D_eq = D_ek
        D_qh = sb("D_qh", [64, 4, 128], BF, ph); D_qt = sb("D_qt", [64, 4, 128], F32, ph)
        D_kI = sb("D_kI", [64, 4, 128], F32, ph)
        D_sg = sb("D_sg", [128, 256], stack=ph); D_lf = sb("D_lf", [128, 256], stack=ph)
        D_kd = sb("D_kd", [128, 256], stack=ph); D_kb = sb("D_kb", [128, 256], BF, ph)
        D_v = sb("D_v", [128, 256], BF, ph); D_AT = sb("D_AT", [128, 128], BF, ph)
        Sst = sb("Sst", [64, 4, 64], stack=ph); S_bf = sb("S_bf", [64, 4, 64], BF, ph)
        D_o = Bh; D_gs = sb("D_gs", [128, 256], stack=ph)
        memset("pool", D_AT[:], 0.0, ["D_AT"])
        outst = Bsq65[0:64, :, 0:65]; Cout = Bh[0:64]
        Cin = Cout; nin = sb("nin", [64, 4], stack=ph); em0 = sb("em0", [64, 4], stack=ph)

        print("phase M sbuf remaining", nc.sbuf_bytes_remaining)

        def v4(i, L):
            c = {0: 0, 2: 16}.get(i, 32 + 4 * i if i < 2 else 28 + 4 * i)
            return bv4[0:L, c:c + 4]

        def proj_fm(c0, srcT, ksrc, L, dst, kdst, scale=None, eng="act"):
            pb, pk = psum("pj")
            pv = pb[0:64, :].rearrange("p (a b) -> p a b", a=4)
            for h in range(4):
                for k in range(8):
                    mm(pv[:, h, 0:L], W[:, k, c0 + 64 * h:c0 + 64 * h + 64], srcT[:, k, 0:L], k == 0, k == 7,
                       r=["W", ksrc], w=[pk])
            if scale is None:
                cp(eng, dst[:, :, 0:L], pv[:, :, 0:L], [pk], [kdst])
            else:
                ts("dve", dst[:, :, 0:L], pv[:, :, 0:L], scale, ALU.mult, r=[pk], w=[kdst])

        def proj_tm(c0, n, srcT, ksrc, L):
            pb, pk = psum("pj")
            for k in range(8):
                mm(pb[0:L, 0:n], srcT[:, k, 0:L], W[:, k, c0:c0 + n], k == 0, k == 7, r=["W", ksrc], w=[pk])
            return pb, pk

        def head_rstd(src3, L, sq3, ksrc, col):
            tt("dve", sq3[0:L], src3, src3, ALU.mult, [ksrc], ["Bsq"])
            P.op("dve", lambda e, o=v4(col, L), i=sq3[0:L]: e.tensor_reduce(out=o, in_=i, axis=AX.X, op=ALU.add),
                 ["Bsq"], ["bv4"])
            act(v4(col, L), v4(col, L), AF.Ln, ["bv4", "epsT"], ["bv4"], bias=epsT[0:L, 0:1], scale=1.0 / 64)
            act(v4(col, L), v4(col, L), AF.Exp, ["bv4"], ["bv4"], scale=-0.5)
            return v4(col, L)

        apos = [0]

        def front(L, xrows, xkey, sl):
            xt_, xn_, xnT_ = xts[sl], xns[sl], xnTs[sl]
            kx, kn, kt = f"xt{sl}", f"xn{sl}", f"xnT{sl}"
            dma(xt_[0:L, :], xrows, r=[xkey], w=[kx])
            rmsnorm_rstd(xt_[0:L, :], L, DM, 1.0 / DM, ssq[0:L, sl:sl + 1], rstd[0:L, sl:sl + 1], xn_[0:L, :], [kx], kn, sfx=f"m{sl}", lcol=sl)
            ts("dve", xn_[0:L, :], xt_[0:L, :], rstd[0:L, sl:sl + 1], ALU.mult, r=[kx, f"rstdm{sl}"], w=[kn])
            transposes(xn_, L, 8, xnT_, I_bf, kn, kt, "I_bf")

        def chunk(L, xrows, xkey, orows, okey, wins, cur, maskf, prompt_ci=None, samp=None, sl=0, do_front=True, mid=None):
            if do_front:
                front(L, xrows, xkey, sl)
            xt, xn, xnT = xts[sl], xns[sl], xnTs[sl]
            KX, KN, KT_ = f"xt{sl}", f"xn{sl}", f"xnT{sl}"
            if samp is None:
                transposes(xn, L, 8, xnTr, J_bf, KN, "xnTr", "J_bf")
                ksrcT, kkey = xnTr, "xnTr"
            else:
                ksrcT, kkey = xnT, KT_
            if KFORK:
                P.fork()
                curset[0] = 0
            proj_fm(0, xnT, KT_, L, A_qT, "A_qT", scale=0.125)
            pb, pk = psum("pj")
            pv = pb[0:64, :].rearrange("p (a b) -> p a b", a=4)
            for h in range(4):
                for k in range(8):
                    mm(pv[:, h, 0:L], W[:, k, 256 + 64 * h:320 + 64 * h], ksrcT[:, k, 0:L], k == 0, k == 7,
                       r=["W", kkey], w=[pk])
            cp("act", KT[:, :, cur, 0:L], pv[:, :, 0:L], [pk], [f"KT{cur}"])
            pb, pk = proj_tm(512, 256, ksrcT, kkey, L)
            cp("dve", VW[0:L, cur, :, 0:64], pb[0:L, 0:256].rearrange("p (h d) -> p h d", h=4), [pk], [f"VW{cur}"])
            need_kv = (samp is not None) or (prompt_ci is not None and prompt_ci >= nch - 16)
            if need_kv:
                pb, pk = proj_tm(256, 512, xnT, KT_, L)
                cp("act", A_kv[0:L, :], pb[0:L, 0:512], [pk], ["stg0"])
                if samp is None:
                    r0 = (prompt_ci - (nch - 16)) * T + (2048 - 16 * T if nch >= 16 else 0)
                    if nch >= 16:
                        dma(kwp[l, r0:r0 + T, :], A_kv[0:L, 0:256], r=["stg0"], w=["o_kwp"])
                        dma(vwp[l, r0:r0 + T, :], A_kv[0:L, 256:512], r=["stg0"], w=["o_vwp"])
                else:
                    dma(kws[l, samp, 2044:2048, :], A_kv[0:L, 0:256], r=["stg0"], w=[f"kws{l}{samp}n"])
                    dma(vws[l, samp, 2044:2048, :], A_kv[0:L, 256:512], r=["stg0"], w=[f"vws{l}{samp}n"])
            if KCUT < 2:
                return
            pvb, pvk = psum("pv")
            pvv = pvb[:, 0:320].rearrange("p (h d) -> p h d", h=4)[:, :, 0:65]
            nw = len(wins)
            groups = []
            g0 = 0
            while g0 < nw:
                g1 = g0
                while g1 < nw and g1 - g0 < 4 and wins[g1][1] == wins[g0][1]:
                    g1 += 1
                groups.append((g0, g1))
                g0 = g1
            for h in range(4):
                Pt = Pts[h % 2]
                kpt = f"Pt{h % 2}"
                for (ga, gb) in groups:
                    Lk = wins[ga][1]
                    n = gb - ga
                    sbk, sk = psum("sc")
                    sv = sbk[:, :].rearrange("p (a b) -> p a b", a=4)
                    for gi in range(n):
                        slot = wins[ga + gi][0]
                        mm(sv[0:Lk, gi, 0:L], KT[:, h, slot, 0:Lk], A_qT[:, h, 0:L], r=[f"KT{slot}", "A_qT"], w=[sk])
                    pi = apos[0] % 2
                    apos[0] += 1
                    pexp = pexps[pi]
                    act(pexp[0:Lk, 0:n, 0:L], sv[0:Lk, 0:n, 0:L], AF.Exp, [sk], [f"pexp{pi}"])
                    tt("dve" if pi == 0 else "pool", Pt[0:Lk, ga:gb, 0:L], pexp[0:Lk, 0:n, 0:L], maskf(h, ga, n, Lk), ALU.mult,
                       [f"pexp{pi}", "MP", "MS"], [kpt])
                for wi, (slot, Lk) in enumerate(wins):
                    mm(pvv[0:L, h, :], Pt[0:Lk, wi, 0:L], VW[0:Lk, slot, h, 0:65], wi == 0, wi == nw - 1,
                       r=[kpt, f"VW{slot}"], w=[pvk])
            P.op("dve", lambda e, o=rden[0:L, :], i=pvv[0:L, :, 64]: e.reciprocal(out=o, in_=i), [pvk], ["rden"])
            tt("dve", mix[0:L, 0:256].rearrange("p (h d) -> p h d", h=4), pvv[0:L, :, 0:64],
               rden[0:L, :].unsqueeze(2).to_broadcast([L, 4, 64]), ALU.mult, [pvk, "rden"], ["mix"])

            if KFORK:
                P.next_thread()
                curset[0] = 1
            proj_fm(768, xnT, KT_, L, B_qT, "B_qT", eng="act")
            if KP < 1:
                return
            proj_fm(1024, xnT, KT_, L, B_kT, "B_kT", scale=0.125)
            if KP < 2:
                return
            pb, pk = proj_tm(1024, 512, xnT, KT_, L)
            cp("dve", B_k[0:L, :], pb[0:L, 0:256], [pk], ["C_uv"])
            cp("dve", Bv[0:L, :, 0:64], pb[0:L, 256:512].rearrange("p (h d) -> p h d", h=4), [pk], ["Bv"])
            if KP < 3:
                return
            pb, pk = proj_tm(1536, 264, xnT, KT_, L)
            cp("act", B_o[0:L, :], pb[0:L, 0:256], [pk], ["C_uv"])
            cp("dve", Bif[0:L, :], pb[0:L, 256:264], [pk], ["Bif"])
            if KB < 1:
                return
            tt("dve", Bif[0:L, :], Bif[0:L, :], bfi[0:L, :], ALU.add, ["Bif", "bfi"], ["Bif"])
            sp_, cs_, a_, wk_, ecs_, den_, rd_ = (v4(i, L) for i in range(7))
            act(sp_, Bif[0:L, 4:8], AF.Exp, ["Bif"], ["bv4"], scale=-1.0)
            act(sp_, sp_, AF.Ln, ["bv4", "oneT"], ["bv4"], bias=oneT[0:L, 0:1])
            if KB < 2:
                return
            mb, mk = psum("mx")
            mm(mb[0:L, 0:4], tri_f[0:L, 0:L], sp_, r=["tri_f", "bv4"], w=[mk])
            mm(mb[0:64, 16:20], ones_f[0:L, 0:64], sp_, r=["ones_f", "bv4"], w=[mk])
            mm(mb[0:4, 32:33], sp_, ones_f[0:L, 0:1], r=["ones_f", "bv4"], w=[mk])
            cp("dve", cs_, mb[0:L, 0:4], [mk], ["bv4"])
            tt("dve", a_, Bif[0:L, 0:4], cs_, ALU.add, ["Bif", "bv4"], ["bv4"])
            act(wk_, a_, AF.Exp, ["bv4"], ["bv4"])
            act(ecs_, cs_, AF.Exp, ["bv4"], ["bv4"])
            act(edL[:, :], mb[0:64, 16:20], AF.Exp, [mk], ["edL"], scale=-1.0)
            cp("dve", m4[:, 1:2], mb[0:4, 32:33], [mk], ["m4"])
            if KB < 3:
                return
            mb2, mk2 = psum("mx")
            mm(mb2[0:4, 0:L], a_, I_f[0:L, 0:L], r=["bv4", "I_f"], w=[mk2])
            P.op("dve", lambda e, o=m4[:, 0:1], i=mb2[0:4, 0:L]: e.tensor_reduce(out=o, in_=i, axis=AX.X, op=ALU.max),
                 [mk2], ["m4"])
            tt("dve", mrun[:, :], mrun[:, :], m4[:, 0:1], ALU.max, ["mrun", "m4"], ["mrun"])
            tt("dve", mrun[:, :], mrun[:, :], m4[:, 1:2], ALU.subtract, ["mrun", "m4"], ["mrun"])
            if KB < 4:
                return
            for h in range(4):
                ts("dve", ktil[0:L, 64 * h:64 * h + 64], B_k[0:L, 64 * h:64 * h + 64], wk_[:, h:h + 1], ALU.mult,
                   0.125, ALU.mult, r=["C_uv", "bv4"], w=["ktil"])
            if KB < 5:
                return
            brb, brk = psum("mx")
            brv = brb[:, 0:320].rearrange("p (h d) -> p h d", h=4)[:, :, 0:65]
            for h in range(4):
                sbk, sk = psum("sc")
                mm(sbk[0:L, 0:L], B_kT[:, h, 0:L], B_qT[:, h, 0:L], r=["B_kT", "B_qT"], w=[sk])
                stt("dve", St[0:L, 0:L], sbk[0:L, 0:L], wk_[:, h:h + 1], tri_f[0:L, 0:L], ALU.mult, ALU.mult,
                    r=[sk, "bv4", "tri_f"], w=["St"])
                mm(brv[0:L, h, :], St[0:L, 0:L], Bv[0:L, h, 0:65], True, False, r=["St", "Bv"], w=[brk])
                mm(brv[0:L, h, :], B_qT[:, h, 0:L], Cst_bf[:, h, 0:65], False, True, r=["B_qT", "Cst_bf"], w=[brk])
            if KB < 6:
                return
            ts("dve", den_, brv[0:L, :, 64], -1.0, ALU.mult, r=[brk], w=["bv4"])
            tt("dve", den_, den_, brv[0:L, :, 64], ALU.max, [brk, "bv4"], ["bv4"])
            tt("dve", den_, den_, ecs_, ALU.max, ["bv4"], ["bv4"])
            P.op("dve", lambda e, o=rd_, i=den_: e.reciprocal(out=o, in_=i), ["bv4"], ["bv4"])
            tt("dve", Bh[0:L], brv[0:L, :, 0:64], rd_.unsqueeze(2).to_broadcast([L, 4, 64]), ALU.mult,
               [brk, "bv4"], ["Bh"])
            if KB < 7:
                return
            cb, ck = psum("mx")
            cv = cb[0:64, 0:320].rearrange("p (h d) -> p h d", h=4)[:, :, 0:65]
            for h in range(4):
                mm(cv[:, h, :], ktil[0:L, 64 * h:64 * h + 64], Bv[0:L, h, 0:65], r=["ktil", "Bv"], w=[ck])
            tt("dve", Cst[:], Cst[:], cv, ALU.add, ["Cst", ck], ["Cst"])
            tt("dve", Cst[:], Cst[:], edL[:, :].unsqueeze(2).to_broadcast([64, 4, 65]), ALU.mult, ["Cst", "edL"], ["Cst"])
            cp("dve", Cst_bf[:, :, 0:65], Cst[:], ["Cst"], ["Cst_bf"])
            if KB < 8:
                return
            rs = head_rstd(Bh[0:L], L, Bsq, "Bh", 7)
            act(B_o[0:L, :], B_o[0:L, :], AF.Sigmoid, ["C_uv"], ["C_uv"])
            tt("dve", Bh[0:L], Bh[0:L], rs.unsqueeze(2).to_broadcast([L, 4, 64]), ALU.mult, ["Bh", "bv4"], ["Bh"])
            tt("dve", Bh[0:L].rearrange("p h d -> p (h d)"), Bh[0:L].rearrange("p h d -> p (h d)"), gml[0:L, :], ALU.mult, ["Bh", "gml"], ["Bh"])
            tt("dve", mix[0:L, 256:512], Bh[0:L].rearrange("p h d -> p (h d)"), B_o[0:L, :], ALU.mult, ["Bh", "C_uv"], ["mix"])

            if KFORK:
                P.join()
                curset[0] = None
            if mid is not None:
                mid()
            if KFORK2:
                P.fork()
                curset[0] = 0
            pb, pk = proj_tm(1856, 512, xnT, KT_, L)
            cp("act", C_uv[0:L, :], pb[0:L, 0:512], [pk], ["C_uv"])
            tt("pool", C_t[0:L, :], C_uv[0:L, :], C_uv[0:L, :], ALU.mult, ["C_uv"], ["C_t"])
            ts("pool", C_t[0:L, :], C_t[0:L, :], 0.044715, ALU.mult, 1.0, ALU.add, r=["C_t"], w=["C_t"])
            tt("pool", C_t[0:L, :], C_t[0:L, :], C_uv[0:L, :], ALU.mult, ["C_t", "C_uv"], ["C_t"])
            act(C_t[0:L, :], C_t[0:L, :], AF.Sigmoid, ["C_t"], ["C_t"], scale=1.5957691216057308)
            tt("pool", C_uv[0:L, :], C_uv[0:L, :], C_t[0:L, :], ALU.mult, ["C_t", "C_uv"], ["C_uv"])
            rmsnorm_rstd(C_uv[0:L, 256:512], L, 256, 1.0 / 256, ssq[0:L, 2:3], rstd[0:L, 2:3], C_t[0:L, 0:256], ["C_uv"], "C_t", sfx="c", lcol=2)
            stt("dve", vrows[0:L, :], C_uv[0:L, 256:512], rstd[0:L, 2:3], gcv[0:L, :], ALU.mult, ALU.mult,
                r=["C_uv", "rstdc", "gcv"], w=["vrows"])
            cp("pool", vr_bf[0:L, :], vrows[0:L, :], ["vrows"], ["vr_bf"])
            if samp is not None:
                dma(ocv[l, samp, :, :], vrows[0:L, :], r=["vrows"], w=[f"ocv{l}{samp}"])
            gb, gk = psum("mx")
            for h in range(4):
                mm(gb[0:L, 64 * h:64 * h + 64], WsT[0:L, h, 0:L], vr_bf[0:L, 64 * h:64 * h + 64], r=["WsT", "vr_bf"], w=[gk])
            for h in range(4):
                stt("dve", mix[0:L, 512 + 64 * h:576 + 64 * h], gb[0:L, 64 * h:64 * h + 64], bsT[0:L, h:h + 1],
                    C_uv[0:L, 64 * h:64 * h + 64], ALU.add, ALU.mult, r=[gk, "bsT", "C_uv"], w=["mix"])

            if KFORK2:
                P.next_thread()
                curset[0] = 1
            proj_fm(2368, xnT, KT_, L, D_qT, "D_qT", eng="act")
            proj_fm(2624, xnT, KT_, L, D_sT, "D_sT", eng="dve")
            pb, pk = proj_tm(2624, 512, xnT, KT_, L)
            act(D_sg[0:L, :], pb[0:L, 0:256], AF.Sigmoid, [pk], ["D_sg"])
            cp("dve", D_v[0:L, :], pb[0:L, 256:512], [pk], ["D_v"])
            pb, pk = proj_tm(3136, 256, xnT, KT_, L)
            act(D_gs[0:L, :], pb[0:L, 0:256], AF.Silu, [pk], ["D_gs"])
            tt("pool", D_kd[0:L, :], D_sg[0:L, :], oml[0:L, :], ALU.mult, ["D_sg", "oml"], ["D_kd"])
            tt("pool", D_lf[0:L, :], D_kd[0:L, :], lbm[0:L, :], ALU.add, ["D_kd", "lbm"], ["D_lf"])
            act(D_lf[0:L, :], D_lf[0:L, :], AF.Ln, ["D_lf"], ["D_lf"])
            tt("pool", D_kd[0:L, :], oml[0:L, :], D_kd[0:L, :], ALU.subtract, ["D_kd", "oml"], ["D_kd"])
            act(D_sT[:, :, 0:L], D_sT[:, :, 0:L], AF.Sigmoid, ["D_sT"], ["D_sT"])
            for h in range(4):
                ts("dve", D_sT[:, h, 0:L], D_sT[:, h, 0:L], nomlT[:, h:h + 1], ALU.mult, omlT[:, h:h + 1], ALU.add,
                   r=["D_sT", "nomlT", "omlT"], w=["D_sT"])
            pb, pk = psum("pj")
            pbv = pb[0:64, :].rearrange("p (a b) -> p a b", a=4)
            for h in range(4):
                mm(pbv[:, h, 0:L], D_lf[0:L, 64 * h:64 * h + 64], tri_f[0:L, 0:L], r=["D_lf", "tri_f"], w=[pk])
            cp("act", D_bT[:, :, 0:L], pbv[:, :, 0:L], [pk], ["D_bT"])
            ts("dve", D_nbT[:, :, 0:L], pbv[:, :, 0:L], -1.0, ALU.mult, r=[pk], w=["D_nbT"])
            act(D_e[:, :, 0:L], D_bT[:, :, 0:L], AF.Exp, ["D_bT"], ["D_e"])
            tt("dve", D_qh[:, :, 0:L], D_qT[:, :, 0:L], D_e[:, :, 0:L], ALU.mult, ["D_qT", "D_e"], ["D_qh"])
            bs_ = min(BS, L)
            nb = (L + BS - 1) // BS
            cp("pool", D_eq[:, :, 0:bs_], D_e[:, :, 0:bs_], ["D_e"], ["D_ek"])
            for h in range(4):
                for I in range(1, nb):
                    act(D_eq[:, h, BS * I:BS * I + BS], D_bT[:, h, BS * I:BS * I + BS], AF.Exp, ["D_bT", "D_nbT"], ["D_ek"],
                        bias=D_nbT[:, h, BS * I - 1:BS * I])
            tt("dve", D_qt[:, :, 0:L], D_qT[:, :, 0:L], D_eq[:, :, 0:L], ALU.mult, ["D_qT", "D_ek"], ["D_qt"])
            ob, ok_ = psum("mx")
            ov = ob[:, 0:256].rearrange("p (h d) -> p h d", h=4)
            for h in range(4):
                for I in range(nb):
                    n = min(L, BS * (I + 1))
                    bias = zeroT[0:64, 0:1] if I == 0 else D_bT[:, h, BS * I - 1:BS * I]
                    act(D_ek[:, I, 0:n], D_bT[:, h, 0:n], AF.Exp, ["D_bT", "zeroT"], ["D_ek"], bias=bias, scale=-1.0)
                for I in range(nb):
                    n = min(L, BS * (I + 1))
                    tt("dve", D_kI[:, I, 0:n], D_sT[:, h, 0:n], D_ek[:, I, 0:n], ALU.mult, ["D_sT", "D_ek"], ["D_kI"])
                sbk, sk = psum("sc")
                for I in range(nb):
                    n = min(L, BS * (I + 1))
                    mm(sbk[0:n, BS * I:BS * I + bs_], D_kI[:, I, 0:n], D_qt[:, h, BS * I:BS * I + bs_], r=["D_kI", "D_qt"], w=[sk])
                for I in range(nb):
                    n = min(L, BS * (I + 1))
                    tt("dve", D_AT[0:n, BS * I:BS * I + bs_], sbk[0:n, BS * I:BS * I + bs_], tri_f[0:n, BS * I:BS * I + bs_],
                       ALU.mult, [sk, "tri_f"], ["D_AT"])
                mm(ov[0:L, h, :], D_AT[0:L, 0:L], D_v[0:L, 64 * h:64 * h + 64], True, False, r=["D_AT", "D_v"], w=[ok_])
                mm(ov[0:L, h, :], D_qh[:, h, 0:L], S_bf[:, h, :], False, True, r=["D_qh", "S_bf"], w=[ok_])
            cp("act", D_o[0:L], ov[0:L], [ok_], ["Bh"])
            db, dk = psum("pj")
            mm(db[0:L, 0:256], ls_f[0:L, 0:L], D_lf[0:L, :], r=["ls_f", "D_lf"], w=[dk])
            act(D_sg[0:L, :], db[0:L, 0:256], AF.Exp, [dk], ["D_sg"])
            tt("pool", D_kb[0:L, :], D_kd[0:L, :], D_sg[0:L, :], ALU.mult, ["D_kd", "D_sg"], ["D_kb"])
            sb2, sk2 = psum("mx")
            sv2 = sb2[0:64, 0:256].rearrange("p (h d) -> p h d", h=4)
            for h in range(4):
                mm(sv2[:, h, :], D_kb[0:L, 64 * h:64 * h + 64], D_v[0:L, 64 * h:64 * h + 64], r=["D_kb", "D_v"], w=[sk2])
            for h in range(4):
                stt("dve", Sst[:, h, :], Sst[:, h, :], D_e[:, h, L - 1:L], sv2[:, h, :], ALU.mult, ALU.add,
                    r=["Sst", "D_e", sk2], w=["Sst"])
            cp("dve", S_bf[:], Sst[:], ["Sst"], ["S_bf"])
            rs = head_rstd(D_o[0:L], L, Bsq, "Bh", 8)
            tt("dve", D_o[0:L], D_o[0:L], rs.unsqueeze(2).to_broadcast([L, 4, 64]), ALU.mult, ["Bh", "bv4"], ["Bh"])
            tt("dve", D_o[0:L].rearrange("p h d -> p (h d)"), D_o[0:L].rearrange("p h d -> p (h d)"), ghg[0:L, :], ALU.mult, ["Bh", "ghg"], ["Bh"])
            tt("dve", mix[0:L, 768:1024], D_o[0:L].rearrange("p h d -> p (h d)"), D_gs[0:L, :], ALU.mult, ["Bh", "D_gs"], ["mix"])

            if KFORK2:
                P.join()
                curset[0] = None
            transposes(mix, L, 8, mixT, I_bf, "mix", "xnTr", "I_bf")
            for hf in range(2):
                pb, pk = psum("pj")
                for k in range(8):
                    mm(pb[0:L, :], mixT[:, k, 0:L], WO[:, k, 512 * hf:512 * hf + 512], k == 0, k == 7, r=["xnTr", "WO"], w=[pk])
                tt("dve", xt[0:L, 512 * hf:512 * hf + 512], xt[0:L, 512 * hf:512 * hf + 512], pb[0:L, :], ALU.add, [KX, pk], [KX])
            dma(orows, xt[0:L, :], r=[KX], w=[okey])

        def state_out(dC, dn, dm, dS, tag):
            act(m4[:, 2:3], mrun[:, :], AF.Exp, ["mrun"], ["m4"], scale=-1.0)
            ts("dve", m4[:, 16:20], I_f[0:4, 0:4], m4[:, 2:3], ALU.mult, r=["I_f", "m4"], w=["m4"])
            mb, mk = psum("mx")
            mm(mb[0:64, 0:4], ones_f[0:4, 0:64], m4[:, 16:20], r=["ones_f", "m4"], w=[mk])
            cp("dve", em0[:, :], mb[0:64, 0:4], [mk], ["em0"])
            tt("dve", outst[:], Cst[:], em0[:, :].unsqueeze(2).to_broadcast([64, 4, 65]), ALU.mult, ["Cst", "em0"], ["Bsq"])
            mb, mk = psum("mx")
            mv = mb[0:64, 0:256].rearrange("p (h d) -> p h d", h=4)
            for h in range(4):
                mm(mv[:, h, :], outst[:, h, 0:64], I_f[0:64, 0:64], r=["Bsq", "I_f"], w=[mk])
            cp("dve", Cout[:], mv, [mk], ["Bh"])
            dma(dC.rearrange("h v k -> v h k"), Cout[:], r=["Bh"], w=["oC" + tag])
            dma(dn.rearrange("h k -> k h"), outst[:, :, 64], r=["Bsq"], w=["on" + tag], slow=True)
            dma(dm.rearrange("(h o) -> h o", o=1), mrun[:, :], r=["mrun"], w=["om" + tag])
            dma(dS.rearrange("h d v -> d h v"), Sst[:], r=["Sst"], w=["oS" + tag])

        if KSTOP < 2:
            P.barrier(); ph.close(); return
        memset("dve", Cst[:], 0.0, ["Cst"]); memset("dve", Cst_bf[:], 0.0, ["Cst_bf"])
        memset("dve", Sst[:], 0.0, ["Sst"]); memset("dve", S_bf[:], 0.0, ["S_bf"])
        memset("dve", mrun[:], 0.0, ["mrun"])
        front(T, xsrc_p[0:T, :], "xsrc0", 0)
        for ci in range(nch):
            wins = [((ci - j) % NW, T) for j in range(0, min(16, ci) + 1)]
            nxt = None
            if ci + 1 < nch:
                nxt = (lambda c=ci + 1: front(T, xsrc_p[c * T:(c + 1) * T, :], f"xsrc{c}", c % 2))
            chunk(T, xsrc_p[ci * T:(ci + 1) * T, :], f"xsrc{ci}", xmid[ci * T:(ci + 1) * T, :], f"xmid{ci}", wins, ci % NW,
                  (lambda h, i0, n, Lk: MP[0:Lk, h, i0:i0 + n, :]), prompt_ci=ci, sl=ci % 2, do_front=False, mid=nxt)
            if KSTOP >= 3:
                flush_pending(1)
        if KSTOP >= 3:
            flush_pending(100)
        if KSTOP >= 4:
            state_out(oCp[l], onp[l], omp[l], oSp[l], "p")
        if KSTOP < 5:
            P.barrier(); ph.close(); return

        for s in range(4):
            for q4 in range(4):
                i = stgpos[0] % 2; stgpos[0] += 1
                dma(stg[i][:, 0:1024].rearrange("p (j c) -> p j c", j=4),
                    kc[l, s, q4 * 512:(q4 + 1) * 512, :].rearrange("(j p) c -> p j c", p=128), w=[f"stg{i}"])
                cp("pool", Kcb[:], stg[i][:, 0:1024].rearrange("p (j c) -> p j c", j=4), [f"stg{i}"], ["Pt0"])
                for h in range(4):
                    pb, pk = psum("pj")
                    pv = pb[0:64, :].rearrange("p (a b) -> p a b", a=4)
                    for jj in range(4):
                        mm(pv[:, jj, :], Kcb[:, jj, 64 * h:64 * h + 64], I_bf[:, :], r=["Pt0", "I_bf"], w=[pk])
                    j0 = q4 * 4
                    cp("act" if h % 2 == 0 else "dve", KT[:, h, j0:j0 + 4, :], pv, [pk], [f"KT{j0 + q}" for q in range(4)])
                i = stgpos[0] % 2; stgpos[0] += 1
                dma(stg[i][:, 0:1024].rearrange("p (j c) -> p j c", j=4),
                    vc[l, s, q4 * 512:(q4 + 1) * 512, :].rearrange("(j p) c -> p j c", p=128), w=[f"stg{i}"])
                for jj in range(4):
                    j = q4 * 4 + jj
                    cp("pool", VW[:, j, :, 0:64], stg[i][:, jj * 256:(jj + 1) * 256].rearrange("p (h d) -> p h d", h=4),
                       [f"stg{i}"], [f"VW{j}"])
            dma(Cin[:], mC[l, s].rearrange("h v k -> v h k"), w=["Bh"])
            dma(nin[:], mn[l, s].rearrange("h k -> k h"), w=["nin"], slow=True)
            dma(em0[:], dap(mm_, (l * 4 + s) * 4, [[0, 64], [1, 4]]), w=["em0"])
            dma(mrun[:], dap(mm_, (l * 4 + s) * 4, [[1, 4], [1, 1]]), w=["mrun"])
            dma(Sst[:], hS[l, s].rearrange("h d v -> d h v"), w=["Sst"])
            cp("dve", S_bf[:], Sst[:], ["Sst"], ["S_bf"])
            act(em0[:], em0[:], AF.Exp, ["em0"], ["em0"])
            mb, mk = psum("mx")
            mv = mb[0:64, 0:256].rearrange("p (h d) -> p h d", h=4)
            for h in range(4):
                mm(mv[:, h, :], Cin[:, h, :], I_f[0:64, 0:64], r=["Bh", "I_f"], w=[mk])
            tt("dve", Cst[:, :, 0:64], mv, em0[:, :].unsqueeze(2).to_broadcast([64, 4, 64]), ALU.mult, [mk, "em0"], ["Cst"])
            tt("dve", Cst[:, :, 64], nin[:], em0[:], ALU.mult, ["nin", "em0"], ["Cst"])
            cp("dve", Cst_bf[:, :, 0:65], Cst[:], ["Cst"], ["Cst_bf"])
            wins = [(j, T) for j in range(16)] + [(16, 4)]
            r0 = NTP + 4 * s
            chunk(4, xsrc_s[4 * s:4 * s + 4, :], f"xsrcs{s}", xmid[r0:r0 + 4, :], f"xmids{s}", wins, 16,
                  (lambda h, i0, n, Lk: MS[0:Lk, h, i0:i0 + n, :]), samp=s)
            state_out(oCs[l, s], ons[l, s], oms[l, s], oSs[l, s], f"s{s}")
        P.barrier()
        ph.close()

    def phase_F(l, last):
        ph = ExitStack()
        WU = sb(f"wup{l}", [128, 8, DFF], BF, ph)
        WD = sb(f"wdn{l}", [128, 32, DM], BF, ph)
        gT = sb(f"gTf{l}", [128, 8], stack=ph)
        dma(gT[:, 0:8], dap(g_mlp, l * DM, [[1, 128], [128, 8]]), w=["gT"], slow=True)
        stg4 = stg + [sb("stgx", [128, 1088], stack=ph) for _ in range(2)]
        load_weights(WU, w_up[l], 8, DFF, gT, lambda k: k, "WU", bufs=stg4)
        load_weights(WD, w_down[l], 32, DM, None, None, "WD", bufs=stg4)
        hTs = [sb("hT", [128, 32, 128], BF, ph) for _ in range(2)]
        rls = [sb("rl", [128, 4, 128], stack=ph) for _ in range(2)]
        gfin = None
        if last:
            gfin = sb("gfin", [128, DM], stack=ph)
            dma(gfin[:], dap(g_final, 0, [[0, 128], [1, DM]]), w=["gfin"])
        rlpos = [0]

        def fchunk(L, rows_in, kin, rows_out, kout, sl):
            xt_, xn_, xnT_, hT = xts[sl], xns[sl], xnTs[sl], hTs[sl]
            kx, kn, kt, kh = f"xt{sl}", f"xn{sl}", f"xnT{sl}", f"hT{sl}"
            sf = str(sl)
            dma(xt_[0:L, :], rows_in, r=[kin], w=[kx])
            rmsnorm_rstd(xt_[0:L, :], L, DM, 1.0 / DM, ssq[0:L, 4 + sl:5 + sl], rstd[0:L, 4 + sl:5 + sl], xn_[0:L, :], [kx], kn, sfx=sf, lcol=4 + sl)
            ts("dve", xn_[0:L, :], xt_[0:L, :], rstd[0:L, 4 + sl:5 + sl], ALU.mult, r=[kx, "rstd" + sf], w=[kn])
            transposes(xn_, L, 8, xnT_, I_bf, kn, kt, "I_bf")
            for g in range(8):
                pb, pk = psum("sc" if g % 2 else "mx")
                pv = pb[:, :].rearrange("p (a b) -> p a b", a=4)
                for q in range(4):
                    f = 4 * g + q
                    for k in range(8):
                        mm(pv[:, q, 0:L], WU[:, k, 128 * f:128 * f + 128], xnT_[:, k, 0:L], k == 0, k == 7, r=["WU", kt], w=[pk])
                ri = rlpos[0] % 2
                rlpos[0] += 1
                rl = rls[ri]
                act(rl[:, :, 0:L], pv[:, :, 0:L], AF.Relu, [pk], [f"rl{ri}"])
                tt("pool" if g % 2 else "dve", hT[:, 4 * g:4 * g + 4, 0:L], rl[:, :, 0:L], rl[:, :, 0:L], ALU.mult, [f"rl{ri}"], [kh])
            for hf in range(2):
                pb, pk = psum("pj")
                for f in range(32):
                    mm(pb[0:L, :], hT[:, f, 0:L], WD[:, f, 512 * hf:512 * hf + 512], f == 0, f == 31, r=[kh, "WD"], w=[pk])
                tt("dve", xt_[0:L, 512 * hf:512 * hf + 512], xt_[0:L, 512 * hf:512 * hf + 512], pb[0:L, :], ALU.add, [kx, pk], [kx])
            if last:
                rmsnorm_rstd(xt_[0:L, :], L, DM, 1.0 / DM, ssq[0:L, 6 + sl:7 + sl], rstd[0:L, 6 + sl:7 + sl], xn_[0:L, :], [kx], kn, sfx="f" + sf, lcol=6 + sl)
                stt("dve", xt_[0:L, :], xt_[0:L, :], rstd[0:L, 6 + sl:7 + sl], gfin[0:L, :], ALU.mult, ALU.mult, r=[kx, "rstdf" + sf, "gfin"], w=[kx])
            dma(rows_out, xt_[0:L, :], r=[kx], w=[kout])

        for ci in range(nch):
            dst = y_p[ci * T:(ci + 1) * T, :] if last else xnext[ci * T:(ci + 1) * T, :]
            fchunk(T, xmid[ci * T:(ci + 1) * T, :], f"xmid{ci}", dst, f"xsrc{ci}", ci % 2)
        dst = y_s[:, :] if last else xnext[NTP:NTP + 16, :]
        fchunk(16, xmid[NTP:NTP + 16, :], "xmids_all", dst, "xsrcs_all", nch % 2)
        P.barrier()
        ph.close()

    ltm = sb("ltm", [128, 128])
    tt("dve", ltm[:], ls_f[:], I_f[:], ALU.add, ["ls_f", "I_f"], ["ltm"])

    if KSTOP >= 1:
        phase_M(0, xp, xs)
    if KSTOP >= 6:
        phase_F(0, False)
    if KSTOP >= 7 and os.environ.get("KSKIPM1", "0") != "1":
        phase_M(1, xnext, xnext[NTP:NTP + 16, :])
    if KSTOP >= 8:
        phase_F(1, True)

    semnames = sorted(P.final.keys())
    sems = {s: es.enter_context(nc.semaphore(s)) for s in semnames}
    blk = es.enter_context(nc.Block())

    def run(engname, handle):
        for waits, fn, inc in P.ops[engname]:
            for s, v in waits:
                handle.wait_ge(sems[s], v)
            if fn is not None:
                fn(handle).then_inc(sems[inc[0]], inc[1])

    @blk.tensor
    def _(e):
        run("pe", e)

    @blk.scalar
    def _(e):
        run("act", e)

    @blk.vector
    def _(e):
        run("dve", e)

    @blk.gpsimd
    def _(e):
        run("pool", e)

    @blk.sync
    def _(e):
        run("sp", e)
        for s, v in P.final.items():
            e.wait_ge(sems[s], v)

    es.close()
    return nc


def _consts():
    i = np.arange(128)
    cI = np.eye(128, dtype=np.float32)
    cJ = cI[::-1].copy()
    cTri = (i[:, None] <= i[None, :]).astype(np.float32)
    cLs = (i[:, None] > i[None, :]).astype(np.float32)
    mult = np.zeros(2049, np.float32)
    for w, d in ((128, 1), (512, 4), (2048, 16)):
        mult[np.arange(w // d + 1) * d] += 1.0
    o = np.arange(2049)
    dd = np.maximum(o, 1).astype(np.float32)
    large = 16 + (np.log(dd / 16) / np.float32(np.log(2048 / 16)) * 16).astype(np.int32)
    large = np.clip(large, 16, 31)
    bucket = np.where(o < 16, o, large)
    MO = np.zeros((32, TABN), np.float32)
    MO[bucket, o + 127] = mult
    MOr = MO[:, ::-1][:, -TABN:].copy()
    MOr = np.zeros((32, TABN), np.float32)
    MOr[:, 0:2303] = MO[:, 0:2303][:, ::-1]
    return cI, cJ, cTri, cLs, MO, MOr


_NC_CACHE = {}


def kernel(x_prompt, x_sample, cache_k_win, cache_v_win, state_mlstm_C, state_mlstm_n, state_mlstm_m,
           state_hgrn_S, rel_bias, w_in, w_out, g_attn, g_mlp, w_up, w_down, b_i, b_f, g_mlstm, g_cv,
           w_s, b_s, hgrn_lb, g_hgrn, g_final):
    f = lambda a: np.ascontiguousarray(np.asarray(a, dtype=np.float32))
    nch = NCH
    if nch not in _NC_CACHE:
        _NC_CACHE[nch] = build(nch)
    nc = _NC_CACHE[nch]
    cI, cJ, cTri, cLs, MO, MOr = _consts()
    shared = dict(relb=f(rel_bias), w_in=f(w_in), w_out=f(w_out), g_attn=f(g_attn), g_mlp=f(g_mlp), w_up=f(w_up),
                  w_down=f(w_down), b_i=f(b_i), b_f=f(b_f), g_mlstm=f(g_mlstm), g_cv=f(g_cv), w_s=f(w_s), b_s=f(b_s),
                  hlb=f(hgrn_lb), g_hgrn=f(g_hgrn), g_final=f(g_final).reshape(1, DM),
                  cI=cI, cJ=cJ, cTri=cTri, cLs=cLs, cMO=MO, cMOr=MOr)
    xp_ = f(x_prompt); xs_ = f(x_sample)
    kc_ = f(cache_k_win); vc_ = f(cache_v_win)
    in_maps = []
    for c in range(8):
        sl = slice(4 * c, 4 * c + 4)
        m = dict(shared)
        m["xp"] = np.ascontiguousarray(xp_[c % 2, :nch * T])
        m["xs"] = np.ascontiguousarray(xs_[sl].reshape(16, DM))
        m["kc"] = np.ascontiguousarray(kc_[:, sl].reshape(2, 4, 2048, 256))
        m["vc"] = np.ascontiguousarray(vc_[:, sl].reshape(2, 4, 2048, 256))
        m["mC"] = f(state_mlstm_C)[:, sl].copy()
        m["mn"] = f(state_mlstm_n)[:, sl].copy()
        m["mm"] = f(state_mlstm_m)[:, sl].copy()
        m["hS"] = f(state_hgrn_S)[:, sl].copy()
        in_maps.append(m)
    res = run_bass_kernel_spmd(nc, in_maps, core_ids=list(range(8))).results
    B = 2
    cat = lambda name, ax: np.concatenate([res[c][name] for c in range(8)], axis=ax)
    y_p = np.zeros((B, 8192, DM), np.float32)
    y_p[:, :nch * T] = np.stack([res[b]["y_p"] for b in range(B)])
    y_s = cat("y_s", 0).reshape(32, 4, DM)
    stack2 = lambda name: np.stack([res[b][name] for b in range(B)], axis=1)
    kwp = stack2("kwp").reshape(2, B, 2048, 4, 64)
    vwp = stack2("vwp").reshape(2, B, 2048, 4, 64)
    kws = cat("kws", 1).reshape(2, 32, 2048, 4, 64)
    vws = cat("vws", 1).reshape(2, 32, 2048, 4, 64)
    Cp = stack2("oCp"); np_ = stack2("onp"); mp = stack2("omp")
    Cs = cat("oCs", 1); ns = cat("ons", 1); ms = cat("oms", 1)
    Sp = stack2("oSp"); Ss = cat("oSs", 1)
    cvs = cat("ocv", 1).reshape(2, 32, 4, 4, 64)
    return (y_p, y_s, kwp, vwp, kws, vws, Cp, np_, mp, Cs, ns, ms, Sp, Ss, cvs)
```

```python
import os
from contextlib import ExitStack
import numpy as np
import concourse.bass as bass
import concourse.mybir as mybir
from concourse.bass_utils import run_bass_kernel_spmd

F32 = mybir.dt.float32
BF = mybir.dt.bfloat16
AF = mybir.ActivationFunctionType
ALU = mybir.AluOpType
AX = mybir.AxisListType

NCH = int(os.environ.get("KNCH", "64"))
SAME_SYNC = os.environ.get("KSAME", "1") == "1"
KSTOP = int(os.environ.get("KSTOP", "99"))
KCUT = int(os.environ.get("KCUT", "99"))
KB = int(os.environ.get("KB", "99"))
KFM = int(os.environ.get("KFORK", "1"))
KFORK = KFM in (1, 2)
KFORK2 = KFM in (1, 3)
KP = int(os.environ.get("KP", "99"))
T = 128
DM = 1024
DIN = 3336
DFF = 4096
BS = 64
NW = 17
TABN = 2304
EPS = 1e-6
COMPUTE = ("pe", "act", "dve", "pool")


class Prog:
    def __init__(self, kdma=8):
        self.ops = {e: [] for e in ("pe", "act", "dve", "pool", "sp")}
        self.cnt = {e: 0 for e in self.ops}
        self.dcnt = {e: 0 for e in self.ops}
        self.lastw = {}
        self.readers = {}
        self.waited = {e: {} for e in self.ops}
        self.K = kdma
        self.final = {}

    def fork(self):
        self.threads = [[]]

    def next_thread(self):
        self.threads.append([])

    def join(self):
        lists = self.threads
        self.threads = None
        tot = [max(1, len(x)) for x in lists]
        pos = [0] * len(lists)
        while True:
            best, bf = -1, 2.0
            for i, x in enumerate(lists):
                if pos[i] < len(x):
                    f = pos[i] / tot[i]
                    if f < bf:
                        best, bf = i, f
            if best < 0:
                break
            a = lists[best][pos[best]]
            pos[best] += 1
            self.op(*a)

    def op(self, eng, fn, r=(), w=(), dma=False):
        if getattr(self, "threads", None) is not None:
            self.threads[-1].append((eng, fn, tuple(r), tuple(w), dma))
            return None
        deps = []
        for x in r:
            t = self.lastw.get(x)
            if t:
                deps.append(t)
            if x.startswith("ps"):
                for s, (v, e) in self.readers.get(x, {}).items():
                    if e != eng:
                        deps.append((s, v, e, s.startswith("d_")))
        for x in w:
            t = self.lastw.get(x)
            if t:
                deps.append(t)
            for s, (v, e) in self.readers.get(x, {}).items():
                deps.append((s, v, e, s.startswith("d_")))
        if dma:
            i = self.dcnt[eng]
            self.dcnt[eng] += 1
            sem = f"d_{eng}{i % self.K}"
            val = 16 * (i // self.K + 1)
            if i >= self.K:
                deps.append((sem, val - 16, eng, True))
            tok = (sem, val, eng, True)
            inc = (sem, 16)
        else:
            self.cnt[eng] += 1
            sem = f"c_{eng}"
            val = self.cnt[eng]
            tok = (sem, val, eng, False)
            inc = (sem, 1)
        self.final[sem] = val
        waits = {}
        for (s, v, e, isd) in deps:
            if e == eng and not isd and not dma:
                if eng == "pe" or not SAME_SYNC:
                    continue
            if self.waited[eng].get(s, 0) >= v:
                continue
            waits[s] = max(waits.get(s, 0), v)
        for s, v in waits.items():
            self.waited[eng][s] = v
        self.ops[eng].append((list(waits.items()), fn, inc))
        for x in r:
            d = self.readers.setdefault(x, {})
            if d.get(sem, (0, None))[0] < val:
                d[sem] = (val, eng)
        for x in w:
            self.lastw[x] = tok
            self.readers[x] = {}
        return tok

    def barrier(self):
        for eng in self.ops:
            waits = []
            for s, v in self.final.items():
                if s == f"c_{eng}" and eng == "pe":
                    continue
                if self.waited[eng].get(s, 0) >= v:
                    continue
                self.waited[eng][s] = v
                waits.append((s, v))
            if waits:
                self.ops[eng].append((waits, None, None))


def build(nch):
    nc = bass.Bass("TRN2", target_bir_lowering=False)
    P = Prog()
    es = ExitStack()

    def din(name, shape):
        return nc.dram_tensor(name, list(shape), F32, kind="ExternalInput").ap()

    def dout(name, shape):
        return nc.dram_tensor(name, list(shape), F32, kind="ExternalOutput").ap()

    def dint(name, shape):
        return nc.dram_tensor(name, list(shape), F32, kind="Internal").ap()

    NTP = nch * T
    xp = din("xp", [NTP, DM])
    xs = din("xs", [16, DM])
    kc = din("kc", [2, 4, 2048, 256])
    vc = din("vc", [2, 4, 2048, 256])
    mC = din("mC", [2, 4, 4, 64, 64])
    mn = din("mn", [2, 4, 4, 64])
    mm_ = din("mm", [2, 4, 4])
    hS = din("hS", [2, 4, 4, 64, 64])
    relb = din("relb", [32, 4])
    w_in = din("w_in", [2, DM, DIN])
    w_out = din("w_out", [2, DM, DM])
    g_attn = din("g_attn", [2, DM])
    g_mlp = din("g_mlp", [2, DM])
    w_up = din("w_up", [2, DM, DFF])
    w_down = din("w_down", [2, DFF, DM])
    b_i = din("b_i", [2, 4])
    b_f = din("b_f", [2, 4])
    g_mlstm = din("g_mlstm", [2, 256])
    g_cv = din("g_cv", [2, 256])
    w_s = din("w_s", [2, 4, 128, 128])
    b_s = din("b_s", [2, 4, 128])
    hlb = din("hlb", [2, 256])
    g_hgrn = din("g_hgrn", [2, 256])
    g_final = din("g_final", [1, DM])
    cI = din("cI", [128, 128])
    cJ = din("cJ", [128, 128])
    cTri = din("cTri", [128, 128])
    cLs = din("cLs", [128, 128])
    cMO = din("cMO", [32, TABN])
    cMOr = din("cMOr", [32, TABN])

    y_p = dout("y_p", [NTP, DM])
    y_s = dout("y_s", [16, DM])
    kwp = dout("kwp", [2, 2048, 256])
    vwp = dout("vwp", [2, 2048, 256])
    kws = dout("kws", [2, 4, 2048, 256])
    vws = dout("vws", [2, 4, 2048, 256])
    oCp = dout("oCp", [2, 4, 64, 64])
    onp = dout("onp", [2, 4, 64])
    omp = dout("omp", [2, 4])
    oCs = dout("oCs", [2, 4, 4, 64, 64])
    ons = dout("ons", [2, 4, 4, 64])
    oms = dout("oms", [2, 4, 4])
    oSp = dout("oSp", [2, 4, 64, 64])
    oSs = dout("oSs", [2, 4, 4, 64, 64])
    ocv = dout("ocv", [2, 4, 4, 256])

    xmid = dint("xmid", [NTP + 16, DM])
    xnext = dint("xnext", [NTP + 16, DM])
    wtab = dint("wtab", [4, TABN])
    wtabr = dint("wtabr", [4, TABN])

    uid = [0]

    def sb(name, shape, dt=F32, stack=None):
        uid[0] += 1
        return (stack or es).enter_context(nc.sbuf_tensor(f"{name}_{uid[0]}", list(shape), dt))

    banks = [es.enter_context(nc.psum_tensor(f"ps{i}", [128, 512], F32)) for i in range(8)]
    rings = {"pj": [0, 1, 2], "sc": [3, 4], "pv": [5], "mx": [6, 7]}
    rpos = {k: 0 for k in rings}

    ringsets = {0: {"pj": [0], "sc": [1, 2], "pv": [3], "mx": [3]},
                1: {"pj": [4, 5], "sc": [6], "mx": [7], "pv": [7]}}
    curset = [None]

    def psum(role):
        rg = rings if curset[0] is None else ringsets[curset[0]]
        i = rg[role][rpos[role] % len(rg[role])]
        rpos[role] += 1
        return banks[i], f"ps{i}"

    def mm(out, lhsT, rhs, start=True, stop=True, r=(), w=()):
        P.op("pe", lambda e, o=out, a=lhsT, b=rhs, s=start, t=stop: e.matmul(o, lhsT=a, rhs=b, start=s, stop=t), r, w)

    def act(out, in_, func, r=(), w=(), bias=None, scale=None, accum=None):
        kw = {}
        if bias is not None:
            kw["bias"] = bias
        if scale is not None:
            kw["scale"] = scale
        if accum is not None:
            kw["accum_out"] = accum
        P.op("act", lambda e, o=out, i=in_, f=func, k=kw: e.activation(out=o, in_=i, func=f, **k), r, w)

    def tt(eng, out, a, b, op, r=(), w=()):
        P.op(eng, lambda e, o=out, x=a, y=b, p=op: e.tensor_tensor(out=o, in0=x, in1=y, op=p), r, w)

    def ts(eng, out, a, s1, op0, s2=None, op1=None, r=(), w=()):
        if op1 is None:
            P.op(eng, lambda e, o=out, x=a, q=s1, p=op0: e.tensor_scalar(out=o, in0=x, scalar1=q, scalar2=None, op0=p), r, w)
        else:
            P.op(eng, lambda e, o=out, x=a, q=s1, p=op0, q2=s2, p2=op1: e.tensor_scalar(out=o, in0=x, scalar1=q, scalar2=q2, op0=p, op1=p2), r, w)

    def stt(eng, out, a, s, b, op0, op1, r=(), w=()):
        P.op(eng, lambda e, o=out, x=a, q=s, y=b, p=op0, p2=op1: e.scalar_tensor_tensor(out=o, in0=x, scalar=q, in1=y, op0=p, op1=p2), r, w)

    def cp(eng, out, in_, r=(), w=()):
        if eng == "act":
            P.op("act", lambda e, o=out, i=in_: e.copy(out=o, in_=i), r, w)
        else:
            P.op(eng, lambda e, o=out, i=in_: e.tensor_copy(out=o, in_=i), r, w)

    def memset(eng, ap, val, w=()):
        P.op(eng, lambda e, a=ap, v=val: e.memset(a, v), (), w)

    def dma(out, in_, r=(), w=(), eng="sp", slow=False):
        if slow:
            P.op(eng, lambda e, o=out, i=in_: e.dma_start(out=o, in_=i, allow_slow_non_contiguous=True), r, w, dma=True)
        else:
            P.op(eng, lambda e, o=out, i=in_: e.dma_start(out=o, in_=i), r, w, dma=True)

    def dap(base, off, pat):
        return bass.AP(base.tensor, off, [list(p) for p in pat])

    I_f = sb("I_f", [128, 128]); J_f = sb("J_f", [128, 128]); tri_f = sb("tri_f", [128, 128])
    ls_f = sb("ls_f", [128, 128]); ones_f = sb("ones_f", [128, 128])
    I_bf = sb("I_bf", [128, 128], BF); J_bf = sb("J_bf", [128, 128], BF)
    epsT = sb("epsT", [128, 1]); oneT = sb("oneT", [128, 1]); zeroT = sb("zeroT", [128, 1])
    dma(I_f[:], cI[:, :], w=["I_f"]); dma(J_f[:], cJ[:, :], w=["J_f"])
    dma(tri_f[:], cTri[:, :], w=["tri_f"]); dma(ls_f[:], cLs[:, :], w=["ls_f"])
    memset("pool", ones_f[:], 1.0, ["ones_f"]); memset("pool", epsT[:], EPS, ["epsT"])
    memset("pool", oneT[:], 1.0, ["oneT"]); memset("pool", zeroT[:], 0.0, ["zeroT"])
    cp("pool", I_bf[:], I_f[:], ["I_f"], ["I_bf"]); cp("pool", J_bf[:], J_f[:], ["J_f"], ["J_bf"])

    with ExitStack() as st0:
        rb = sb("rb", [32, 4], stack=st0); erb = sb("erb", [32, 4], stack=st0)
        mo = sb("mo", [32, TABN], stack=st0); wt = sb("wt", [4, TABN], stack=st0)
        dma(rb[:], relb[:, :], w=["rb"])
        act(erb[:], rb[:], AF.Exp, ["rb"], ["erb"])
        for src, dst in ((cMO, wtab), (cMOr, wtabr)):
            dma(mo[:], src[:, :], w=["mo"])
            for c0 in range(0, TABN, 512):
                n = min(512, TABN - c0)
                pb, pk = psum("mx")
                mm(pb[0:4, 0:n], erb[:, :], mo[:, c0:c0 + n], r=["erb", "mo"], w=[pk])
                cp("dve", wt[:, c0:c0 + n], pb[0:4, 0:n], [pk], ["wt"])
            dma(dst[:, :], wt[:], r=["wt"], w=["wtab" if dst is wtab else "wtabr"])
        P.barrier()

    pending = []
    for l in range(2):
        for s in range(4):
            pending.append((kws[l, s, 0:2044, :], kc[l, s, 4:2048, :], f"kws{l}{s}"))
            pending.append((vws[l, s, 0:2044, :], vc[l, s, 4:2048, :], f"vws{l}{s}"))

    def flush_pending(n):
        for _ in range(n):
            if pending:
                o, i, k = pending.pop(0)
                dma(o, i, w=[k], eng="pool")

    xt = sb("xt", [128, DM]); xn = sb("xn", [128, DM], BF)
    xnT = sb("xnT", [128, 8, 128], BF)
    xts = [xt, sb("xt2", [128, DM])]
    xns = [xn, sb("xn2", [128, DM], BF)]
    xnTs = [xnT, sb("xnT2", [128, 8, 128], BF)]
    ssq = sb("ssq", [128, 8]); rstd = sb("rstd", [128, 8]); lnv = sb("lnv", [128, 8])
    stg = [sb(f"stg{i}", [128, 1088]) for i in range(2)]
    stgpos = [0]

    def rmsnorm_rstd(src, L, ncol, scale, ssq_ap, rstd_ap, junk, keys_r, key_junk, sfx="", lcol=0):
        memset("dve", ssq_ap, 0.0, ["ssq" + sfx])
        act(junk, src, AF.Square, keys_r + ["ssq" + sfx], [key_junk, "ssq" + sfx], accum=ssq_ap)
        act(lnv[0:L, lcol:lcol + 1], ssq_ap, AF.Ln, ["ssq" + sfx, "epsT"], ["lnv" + sfx], bias=epsT[0:L, 0:1], scale=scale)
        act(rstd_ap, lnv[0:L, lcol:lcol + 1], AF.Exp, ["lnv" + sfx], ["rstd" + sfx], scale=-0.5)

    def transposes(src, L, nk, dst, mat, kr, kw, kmat):
        for g in range(0, nk, 4):
            pb, pk = psum("pj")
            pv = pb[:, :].rearrange("p (a b) -> p a b", a=4)
            for k in range(g, min(g + 4, nk)):
                mm(pv[:, k - g, 0:L], src[0:L, k * 128:(k + 1) * 128], mat[0:L, 0:L], r=[kr, kmat], w=[pk])
            n = min(4, nk - g)
            cp("act" if (g // 4) % 2 == 0 else "dve", dst[:, g:g + n, 0:L], pv[:, 0:n, 0:L], [pk], [kw])

    def load_weights(dst3, src2, nk, ncols, gT, gsel, keyw, segs=None, bufs=None, bkeys=None):
        if segs is None:
            segs = [(0, ncols, 0)]
        segs = [(c0 + o, min(1024, n - o), d0 + o) for (c0, n, d0) in segs for o in range(0, n, 1024)]
        for k in range(nk):
            for (c0, n, d0) in segs:
                sg = bufs if bufs is not None else stg
                i = stgpos[0] % len(sg)
                stgpos[0] += 1
                ce = "pool" if (bufs is None or i % 2 == 0) else "dve"
                sk_ = bkeys[i] if bkeys is not None else f"stg{i}"
                dma(sg[i][:, 0:n], src2[k * 128:(k + 1) * 128, c0:c0 + n], w=[sk_], eng=("sp" if i % 2 == 0 else "act"))
                gi = gsel(k) if gT is not None else None
                if gi is None:
                    cp(ce, dst3[:, k, d0:d0 + n], sg[i][:, 0:n], [sk_], [keyw])
                else:
                    ts(ce, dst3[:, k, d0:d0 + n], sg[i][:, 0:n], gT[:, gi:gi + 1], ALU.mult,
                       r=[sk_, "gT"], w=[keyw])

    def phase_M(l, xsrc_p, xsrc_s):
        ph = ExitStack()
        W = sb(f"win{l}", [128, 8, 3392], BF, ph)
        WO = sb(f"wout{l}", [128, 8, DM], BF, ph)
        gT = sb(f"gT{l}", [128, 12], stack=ph)
        dma(gT[:, 0:8], dap(g_attn, l * DM, [[1, 128], [128, 8]]), w=["gT"], slow=True)
        dma(gT[:, 8:10], dap(g_mlstm, l * 256, [[1, 128], [128, 2]]), w=["gT"], slow=True)
        dma(gT[:, 10:12], dap(g_hgrn, l * 256, [[1, 128], [128, 2]]), w=["gT"], slow=True)
        Pts = [sb("Pt", [128, NW, 128], BF, ph) for _ in range(2)]
        stgm = stg + [p_[:, :, :].rearrange("p a b -> p (a b)").bitcast(F32) for p_ in Pts]
        stgk = ["stg0", "stg1", "Pt0", "Pt1"]
        load_weights(W, w_in[l], 8, DIN, gT, lambda k: k, "W", segs=[(0, 1800, 0), (1800, 1536, 1856)], bufs=stgm, bkeys=stgk)
        load_weights(WO, w_out[l], 8, DM, None, None, "WO", bufs=stgm, bkeys=stgk)

        bfi = sb(f"bfi{l}", [128, 8], stack=ph)
        dma(bfi[:, 0:4], dap(b_i, l * 4, [[0, 128], [1, 4]]), w=["bfi"])
        dma(bfi[:, 4:8], dap(b_f, l * 4, [[0, 128], [1, 4]]), w=["bfi"])
        gcv = sb(f"gcv{l}", [128, 256], stack=ph)
        dma(gcv[:], dap(g_cv, l * 256, [[0, 128], [1, 256]]), w=["gcv"])
        gml = sb(f"gml{l}", [128, 256], stack=ph); ghg = sb(f"ghg{l}", [128, 256], stack=ph)
        dma(gml[:], dap(g_mlstm, l * 256, [[0, 128], [1, 256]]), w=["gml"])
        dma(ghg[:], dap(g_hgrn, l * 256, [[0, 128], [1, 256]]), w=["ghg"])
        oml = sb(f"oml{l}", [128, 256], stack=ph)
        lbm = sb(f"lbm{l}", [128, 256], stack=ph)
        C_uv = sb("C_uv", [128, 512], stack=ph); C_t = sb("C_t", [128, 512], stack=ph)
        lbt = C_uv[:, 0:256]
        lbT = sb(f"lbT{l}", [64, 4], stack=ph); omlT = sb(f"omlT{l}", [64, 4], stack=ph); nomlT = sb(f"nomlT{l}", [64, 4], stack=ph)
        if l == 0:
            memset("dve", lbt, 0.0, ["C_uv"]); memset("dve", lbT[:], 0.0, ["lbT"])
        else:
            t0 = C_t[:, 0:256]; t1 = C_t[:, 256:512]
            dma(t0, dap(hlb, 0, [[0, 128], [1, 256]]), w=["C_t"])
            dma(t1, dap(hlb, 256, [[0, 128], [1, 256]]), w=["C_t"])
            tt("dve", t1, t1, t0, ALU.subtract, ["C_t"], ["C_t"])
            act(lbt, t1, AF.Sigmoid, ["C_t"], ["C_uv"])
            u0 = sb("lbtmp2", [64, 4], stack=ph); u1 = sb("lbtmp3", [64, 4], stack=ph)
            dma(u0[:], dap(hlb, 0, [[1, 64], [64, 4]]), w=["lbu0"], slow=True)
            dma(u1[:], dap(hlb, 256, [[1, 64], [64, 4]]), w=["lbu1"], slow=True)
            tt("dve", u1[:], u1[:], u0[:], ALU.subtract, ["lbu0", "lbu1"], ["lbu1"])
            act(lbT[:], u1[:], AF.Sigmoid, ["lbu1"], ["lbT"])
        ts("dve", oml[:], lbt, -1.0, ALU.mult, 1.0, ALU.add, r=["C_uv"], w=["oml"])
        ts("dve", lbm[:], lbt, 1e-30, ALU.max, r=["C_uv"], w=["lbm"])
        ts("dve", omlT[:], lbT[:], -1.0, ALU.mult, 1.0, ALU.add, r=["lbT"], w=["omlT"])
        ts("dve", nomlT[:], omlT[:], -1.0, ALU.mult, r=["omlT"], w=["nomlT"])
        WsT = sb(f"WsT{l}", [128, 4, 128], BF, ph); bsT = sb(f"bsT{l}", [128, 4], stack=ph)
        dma(bsT[:], dap(b_s, l * 512, [[1, 128], [128, 4]]), w=["bsT"], slow=True)
        wsm = sb("wsm", [128, 128], BF, ph)
        for h in range(4):
            i = stgpos[0] % 2
            stgpos[0] += 1
            dma(stg[i][:, 0:128], w_s[l, h, :, :], w=[f"stg{i}"])
            tt("dve", wsm[:], stg[i][:, 0:128], ltm[:], ALU.mult, [f"stg{i}", "ltm"], ["wsm"])
            pb, pk = psum("pj")
            mm(pb[:, 0:128], wsm[:, :], I_bf[:, :], r=["wsm", "I_bf"], w=[pk])
            cp("dve", WsT[:, h, :], pb[:, 0:128], [pk], ["WsT"])

        KT = sb("KT", [64, 4, NW, 128], BF, ph)
        VW = sb("VW", [128, NW, 4, 96], BF, ph)
        MP = sb("MP", [128, 4, NW, 128], BF, ph)
        MS = sb("MS", [128, 4, NW, 4], stack=ph)
        for h in range(4):
            for (j0, nj) in ((0, 6), (6, 6), (12, 5)):
                i = stgpos[0] % 2
                stgpos[0] += 1
                mstg = stg[i][:, 0:nj * 128].rearrange("p (j t) -> p j t", j=nj)
                dma(mstg, dap(wtab, h * TABN + 128 * j0, [[1, 128], [128, nj], [1, 128]]), r=["wtab"], w=[f"stg{i}"])
                cp("pool", MP[:, h, j0:j0 + nj, :], mstg, [f"stg{i}"], ["MP"])
            for t in range(4):
                dma(MS[:, h, :, t], dap(wtabr, h * TABN + 127 - t, [[1, 128], [128, NW]]), r=["wtabr"], w=["MS"], slow=True)
        memset("pool", VW[:], 1.0, [f"VW{j}" for j in range(NW)])

        A_qT = sb("A_qT", [64, 4, 128], BF, ph)
        A_kv = stg[0][:, 0:512]
        pexps = [sb("pexp", [128, 4, 128], stack=ph) for _ in range(2)]
        rden = sb("rden", [128, 4], stack=ph)
        mix = sb("mix", [128, DM], BF, ph)
        xnTr = sb("xnTr", [128, 8, 128], BF, ph)
        mixT = xnTr
        Kcb = Pts[0][:, 0:8, :].rearrange("p (a b) c -> p a (b c)", a=4)
        B_qT = sb("B_qT", [64, 4, 128], BF, ph); B_kT = sb("B_kT", [64, 4, 128], BF, ph)
        B_k = C_uv[:, 0:256]; Bv = sb("Bv", [128, 4, 96], BF, ph)
        B_o = C_uv[:, 256:512]; Bif = sb("Bif", [128, 8], stack=ph)
        bv4 = sb("bv4", [128, 64], stack=ph)
        ktil = sb("ktil", [128, 256], BF, ph); St = sb("St", [128, 128], BF, ph)
        Cst = sb("Cst", [64, 4, 65], stack=ph); Cst_bf = sb("Cst_bf", [64, 4, 96], BF, ph)
        edL = sb("edL", [64, 4], stack=ph)
        Bh = sb("Bh", [128, 4, 64], stack=ph); Bsq65 = sb("Bsq", [128, 4, 80], stack=ph); Bsq = Bsq65[:, :, 0:64]
        mrun = sb("mrun", [4, 1], stack=ph); m4 = sb("m4", [4, 32], stack=ph)
        memset("pool", Bv[:], 1.0, ["Bv"])
        vrows = sb("vrows", [128, 256], stack=ph); vr_bf = sb("vr_bf", [128, 256], BF, ph)
        D_qT = sb("D_qT", [64, 4, 128], stack=ph); D_sT = sb("D_sT", [64, 4, 128], stack=ph)
        D_bT = sb("D_bT", [64, 4, 128], stack=ph); D_nbT = sb("D_nbT", [64, 4, 128], stack=ph)
        D_e = sb("D_e", [64, 4, 128], stack=ph)
        D_ek = sb("D_ek", [64, 4, 128], stack=ph)
        D_eq = D_ek
        D_qh = sb("D_qh", [64, 4, 128], BF, ph); D_qt = sb("D_qt", [64, 4, 128], F32, ph)
        D_kI = sb("D_kI", [64, 4, 128], F32, ph)
        D_sg = sb("D_sg", [128, 256], stack=ph); D_lf = sb("D_lf", [128, 256], stack=ph)
        D_kd = sb("D_kd", [128, 256], stack=ph); D_kb = sb("D_kb", [128, 256], BF, ph)
        D_v = sb("D_v", [128, 256], BF, ph); D_AT = sb("D_AT", [128, 128], BF, ph)
        Sst = sb("Sst", [64, 4, 64], stack=ph); S_bf = sb("S_bf", [64, 4, 64], BF, ph)
        D_o = Bh; D_gs = sb("D_gs", [128, 256], stack=ph)
        memset("pool", D_AT[:], 0.0, ["D_AT"])
        outst = Bsq65[0:64, :, 0:65]; Cout = Bh[0:64]
        Cin = Cout; nin = sb("nin", [64, 4], stack=ph); em0 = sb("em0", [64, 4], stack=ph)

        print("phase M sbuf remaining", nc.sbuf_bytes_remaining)

        def v4(i, L):
            c = {0: 0, 2: 16}.get(i, 32 + 4 * i if i < 2 else 28 + 4 * i)
            return bv4[0:L, c:c + 4]

        def proj_fm(c0, srcT, ksrc, L, dst, kdst, scale=None, eng="act"):
            pb, pk = psum("pj")
            pv = pb[0:64, :].rearrange("p (a b) -> p a b", a=4)
            for h in range(4):
                for k in range(8):
                    mm(pv[:, h, 0:L], W[:, k, c0 + 64 * h:c0 + 64 * h + 64], srcT[:, k, 0:L], k == 0, k == 7,
                       r=["W", ksrc], w=[pk])
            if scale is None:
                cp(eng, dst[:, :, 0:L], pv[:, :, 0:L], [pk], [kdst])
            else:
                ts("dve", dst[:, :, 0:L], pv[:, :, 0:L], scale, ALU.mult, r=[pk], w=[kdst])

        def proj_tm(c0, n, srcT, ksrc, L):
            pb, pk = psum("pj")
            for k in range(8):
                mm(pb[0:L, 0:n], srcT[:, k, 0:L], W[:, k, c0:c0 + n], k == 0, k == 7, r=["W", ksrc], w=[pk])
            return pb, pk

        def head_rstd(src3, L, sq3, ksrc, col):
            tt("dve", sq3[0:L], src3, src3, ALU.mult, [ksrc], ["Bsq"])
            P.op("dve", lambda e, o=v4(col, L), i=sq3[0:L]: e.tensor_reduce(out=o, in_=i, axis=AX.X, op=ALU.add),
                 ["Bsq"], ["bv4"])
            act(v4(col, L), v4(col, L), AF.Ln, ["bv4", "epsT"], ["bv4"], bias=epsT[0:L, 0:1], scale=1.0 / 64)
            act(v4(col, L), v4(col, L), AF.Exp, ["bv4"], ["bv4"], scale=-0.5)
            return v4(col, L)

        apos = [0]

        def front(L, xrows, xkey, sl):
            xt_, xn_, xnT_ = xts[sl], xns[sl], xnTs[sl]
            kx, kn, kt = f"xt{sl}", f"xn{sl}", f"xnT{sl}"
            dma(xt_[0:L, :], xrows, r=[xkey], w=[kx])
            rmsnorm_rstd(xt_[0:L, :], L, DM, 1.0 / DM, ssq[0:L, sl:sl + 1], rstd[0:L, sl:sl + 1], xn_[0:L, :], [kx], kn, sfx=f"m{sl}", lcol=sl)
            ts("dve", xn_[0:L, :], xt_[0:L, :], rstd[0:L, sl:sl + 1], ALU.mult, r=[kx, f"rstdm{sl}"], w=[kn])
            transposes(xn_, L, 8, xnT_, I_bf, kn, kt, "I_bf")

        def chunk(L, xrows, xkey, orows, okey, wins, cur, maskf, prompt_ci=None, samp=None, sl=0, do_front=True, mid=None):
            if do_front:
                front(L, xrows, xkey, sl)
            xt, xn, xnT = xts[sl], xns[sl], xnTs[sl]
            KX, KN, KT_ = f"xt{sl}", f"xn{sl}", f"xnT{sl}"
            if samp is None:
                transposes(xn, L, 8, xnTr, J_bf, KN, "xnTr", "J_bf")
                ksrcT, kkey = xnTr, "xnTr"
            else:
                ksrcT, kkey = xnT, KT_
            if KFORK:
                P.fork()
                curset[0] = 0
            proj_fm(0, xnT, KT_, L, A_qT, "A_qT", scale=0.125)
            pb, pk = psum("pj")
            pv = pb[0:64, :].rearrange("p (a b) -> p a b", a=4)
            for h in range(4):
                for k in range(8):
                    mm(pv[:, h, 0:L], W[:, k, 256 + 64 * h:320 + 64 * h], ksrcT[:, k, 0:L], k == 0, k == 7,
                       r=["W", kkey], w=[pk])
            cp("act", KT[:, :, cur, 0:L], pv[:, :, 0:L], [pk], [f"KT{cur}"])
            pb, pk = proj_tm(512, 256, ksrcT, kkey, L)
            cp("dve", VW[0:L, cur, :, 0:64], pb[0:L, 0:256].rearrange("p (h d) -> p h d", h=4), [pk], [f"VW{cur}"])
            need_kv = (samp is not None) or (prompt_ci is not None and prompt_ci >= nch - 16)
            if need_kv:
                pb, pk = proj_tm(256, 512, xnT, KT_, L)
                cp("act", A_kv[0:L, :], pb[0:L, 0:512], [pk], ["stg0"])
                if samp is None:
                    r0 = (prompt_ci - (nch - 16)) * T + (2048 - 16 * T if nch >= 16 else 0)
                    if nch >= 16:
                        dma(kwp[l, r0:r0 + T, :], A_kv[0:L, 0:256], r=["stg0"], w=["o_kwp"])
                        dma(vwp[l, r0:r0 + T, :], A_kv[0:L, 256:512], r=["stg0"], w=["o_vwp"])
                else:
                    dma(kws[l, samp, 2044:2048, :], A_kv[0:L, 0:256], r=["stg0"], w=[f"kws{l}{samp}n"])
                    dma(vws[l, samp, 2044:2048, :], A_kv[0:L, 256:512], r=["stg0"], w=[f"vws{l}{samp}n"])
            if KCUT < 2:
                return
            pvb, pvk = psum("pv")
            pvv = pvb[:, 0:320].rearrange("p (h d) -> p h d", h=4)[:, :, 0:65]
            nw = len(wins)
            groups = []
            g0 = 0
            while g0 < nw:
                g1 = g0
                while g1 < nw and g1 - g0 < 4 and wins[g1][1] == wins[g0][1]:
                    g1 += 1
                groups.append((g0, g1))
                g0 = g1
            for h in range(4):
                Pt = Pts[h % 2]
                kpt = f"Pt{h % 2}"
                for (ga, gb) in groups:
                    Lk = wins[ga][1]
                    n = gb - ga
                    sbk, sk = psum("sc")
                    sv = sbk[:, :].rearrange("p (a b) -> p a b", a=4)
                    for gi in range(n):
                        slot = wins[ga + gi][0]
                        mm(sv[0:Lk, gi, 0:L], KT[:, h, slot, 0:Lk], A_qT[:, h, 0:L], r=[f"KT{slot}", "A_qT"], w=[sk])
                    pi = apos[0] % 2
                    apos[0] += 1
                    pexp = pexps[pi]
                    act(pexp[0:Lk, 0:n, 0:L], sv[0:Lk, 0:n, 0:L], AF.Exp, [sk], [f"pexp{pi}"])
                    tt("dve" if pi == 0 else "pool", Pt[0:Lk, ga:gb, 0:L], pexp[0:Lk, 0:n, 0:L], maskf(h, ga, n, Lk), ALU.mult,
                       [f"pexp{pi}", "MP", "MS"], [kpt])
                for wi, (slot, Lk) in enumerate(wins):
                    mm(pvv[0:L, h, :], Pt[0:Lk, wi, 0:L], VW[0:Lk, slot, h, 0:65], wi == 0, wi == nw - 1,
                       r=[kpt, f"VW{slot}"], w=[pvk])
            P.op("dve", lambda e, o=rden[0:L, :], i=pvv[0:L, :, 64]: e.reciprocal(out=o, in_=i), [pvk], ["rden"])
            tt("dve", mix[0:L, 0:256].rearrange("p (h d) -> p h d", h=4), pvv[0:L, :, 0:64],
               rden[0:L, :].unsqueeze(2).to_broadcast([L, 4, 64]), ALU.mult, [pvk, "rden"], ["mix"])

            if KFORK:
                P.next_thread()
                curset[0] = 1
            proj_fm(768, xnT, KT_, L, B_qT, "B_qT", eng="act")
            if KP < 1:
                return
            proj_fm(1024, xnT, KT_, L, B_kT, "B_kT", scale=0.125)
            if KP < 2:
                return
            pb, pk = proj_tm(1024, 512, xnT, KT_, L)
            cp("dve", B_k[0:L, :], pb[0:L, 0:256], [pk], ["C_uv"])
            cp("dve", Bv[0:L, :, 0:64], pb[0:L, 256:512].rearrange("p (h d) -> p h d", h=4), [pk], ["Bv"])
            if KP < 3:
                return
            pb, pk = proj_tm(1536, 264, xnT, KT_, L)
            cp("act", B_o[0:L, :], pb[0:L, 0:256], [pk], ["C_uv"])
            cp("dve", Bif[0:L, :], pb[0:L, 256:264], [pk], ["Bif"])
            if KB < 1:
                return
            tt("dve", Bif[0:L, :], Bif[0:L, :], bfi[0:L, :], ALU.add, ["Bif", "bfi"], ["Bif"])
            sp_, cs_, a_, wk_, ecs_, den_, rd_ = (v4(i, L) for i in range(7))
            act(sp_, Bif[0:L, 4:8], AF.Exp, ["Bif"], ["bv4"], scale=-1.0)
            act(sp_, sp_, AF.Ln, ["bv4", "oneT"], ["bv4"], bias=oneT[0:L, 0:1])
            if KB < 2:
                return
            mb, mk = psum("mx")
            mm(mb[0:L, 0:4], tri_f[0:L, 0:L], sp_, r=["tri_f", "bv4"], w=[mk])
            mm(mb[0:64, 16:20], ones_f[0:L, 0:64], sp_, r=["ones_f", "bv4"], w=[mk])
            mm(mb[0:4, 32:33], sp_, ones_f[0:L, 0:1], r=["ones_f", "bv4"], w=[mk])
            cp("dve", cs_, mb[0:L, 0:4], [mk], ["bv4"])
            tt("dve", a_, Bif[0:L, 0:4], cs_, ALU.add, ["Bif", "bv4"], ["bv4"])
            act(wk_, a_, AF.Exp, ["bv4"], ["bv4"])
            act(ecs_, cs_, AF.Exp, ["bv4"], ["bv4"])
            act(edL[:, :], mb[0:64, 16:20], AF.Exp, [mk], ["edL"], scale=-1.0)
            cp("dve", m4[:, 1:2], mb[0:4, 32:33], [mk], ["m4"])
            if KB < 3:
                return
            mb2, mk2 = psum("mx")
            mm(mb2[0:4, 0:L], a_, I_f[0:L, 0:L], r=["bv4", "I_f"], w=[mk2])
            P.op("dve", lambda e, o=m4[:, 0:1], i=mb2[0:4, 0:L]: e.tensor_reduce(out=o, in_=i, axis=AX.X, op=ALU.max),
                 [mk2], ["m4"])
            tt("dve", mrun[:, :], mrun[:, :], m4[:, 0:1], ALU.max, ["mrun", "m4"], ["mrun"])
            tt("dve", mrun[:, :], mrun[:, :], m4[:, 1:2], ALU.subtract, ["mrun", "m4"], ["mrun"])
            if KB < 4:
                return
            for h in range(4):
                ts("dve", ktil[0:L, 64 * h:64 * h + 64], B_k[0:L, 64 * h:64 * h + 64], wk_[:, h:h + 1], ALU.mult,
                   0.125, ALU.mult, r=["C_uv", "bv4"], w=["ktil"])
            if KB < 5:
                return
            brb, brk = psum("mx")
            brv = brb[:, 0:320].rearrange("p (h d) -> p h d", h=4)[:, :, 0:65]
            for h in range(4):
                sbk, sk = psum("sc")
                mm(sbk[0:L, 0:L], B_kT[:, h, 0:L], B_qT[:, h, 0:L], r=["B_kT", "B_qT"], w=[sk])
                stt("dve", St[0:L, 0:L], sbk[0:L, 0:L], wk_[:, h:h + 1], tri_f[0:L, 0:L], ALU.mult, ALU.mult,
                    r=[sk, "bv4", "tri_f"], w=["St"])
                mm(brv[0:L, h, :], St[0:L, 0:L], Bv[0:L, h, 0:65], True, False, r=["St", "Bv"], w=[brk])
                mm(brv[0:L, h, :], B_qT[:, h, 0:L], Cst_bf[:, h, 0:65], False, True, r=["B_qT", "Cst_bf"], w=[brk])
            if KB < 6:
                return
            ts("dve", den_, brv[0:L, :, 64], -1.0, ALU.mult, r=[brk], w=["bv4"])
            tt("dve", den_, den_, brv[0:L, :, 64], ALU.max, [brk, "bv4"], ["bv4"])
            tt("dve", den_, den_, ecs_, ALU.max, ["bv4"], ["bv4"])
            P.op("dve", lambda e, o=rd_, i=den_: e.reciprocal(out=o, in_=i), ["bv4"], ["bv4"])
            tt("dve", Bh[0:L], brv[0:L, :, 0:64], rd_.unsqueeze(2).to_broadcast([L, 4, 64]), ALU.mult,
               [brk, "bv4"], ["Bh"])
            if KB < 7:
                return
            cb, ck = psum("mx")
            cv = cb[0:64, 0:320].rearrange("p (h d) -> p h d", h=4)[:, :, 0:65]
            for h in range(4):
                mm(cv[:, h, :], ktil[0:L, 64 * h:64 * h + 64], Bv[0:L, h, 0:65], r=["ktil", "Bv"], w=[ck])
            tt("dve", Cst[:], Cst[:], cv, ALU.add, ["Cst", ck], ["Cst"])
            tt("dve", Cst[:], Cst[:], edL[:, :].unsqueeze(2).to_broadcast([64, 4, 65]), ALU.mult, ["Cst", "edL"], ["Cst"])
            cp("dve", Cst_bf[:, :, 0:65], Cst[:], ["Cst"], ["Cst_bf"])
            if KB < 8:
                return
            rs = head_rstd(Bh[0:L], L, Bsq, "Bh", 7)
            act(B_o[0:L, :], B_o[0:L, :], AF.Sigmoid, ["C_uv"], ["C_uv"])
            tt("dve", Bh[0:L], Bh[0:L], rs.unsqueeze(2).to_broadcast([L, 4, 64]), ALU.mult, ["Bh", "bv4"], ["Bh"])
            tt("dve", Bh[0:L].rearrange("p h d -> p (h d)"), Bh[0:L].rearrange("p h d -> p (h d)"), gml[0:L, :], ALU.mult, ["Bh", "gml"], ["Bh"])
            tt("dve", mix[0:L, 256:512], Bh[0:L].rearrange("p h d -> p (h d)"), B_o[0:L, :], ALU.mult, ["Bh", "C_uv"], ["mix"])

            if KFORK:
                P.join()
                curset[0] = None
            if mid is not None:
                mid()
            if KFORK2:
                P.fork()
                curset[0] = 0
            pb, pk = proj_tm(1856, 512, xnT, KT_, L)
            cp("act", C_uv[0:L, :], pb[0:L, 0:512], [pk], ["C_uv"])
            tt("pool", C_t[0:L, :], C_uv[0:L, :], C_uv[0:L, :], ALU.mult, ["C_uv"], ["C_t"])
            ts("pool", C_t[0:L, :], C_t[0:L, :], 0.044715, ALU.mult, 1.0, ALU.add, r=["C_t"], w=["C_t"])
            tt("pool", C_t[0:L, :], C_t[0:L, :], C_uv[0:L, :], ALU.mult, ["C_t", "C_uv"], ["C_t"])
            act(C_t[0:L, :], C_t[0:L, :], AF.Sigmoid, ["C_t"], ["C_t"], scale=1.5957691216057308)
            tt("pool", C_uv[0:L, :], C_uv[0:L, :], C_t[0:L, :], ALU.mult, ["C_t", "C_uv"], ["C_uv"])
            rmsnorm_rstd(C_uv[0:L, 256:512], L, 256, 1.0 / 256, ssq[0:L, 2:3], rstd[0:L, 2:3], C_t[0:L, 0:256], ["C_uv"], "C_t", sfx="c", lcol=2)
            stt("dve", vrows[0:L, :], C_uv[0:L, 256:512], rstd[0:L, 2:3], gcv[0:L, :], ALU.mult, ALU.mult,
                r=["C_uv", "rstdc", "gcv"], w=["vrows"])
            cp("pool", vr_bf[0:L, :], vrows[0:L, :], ["vrows"], ["vr_bf"])
            if samp is not None:
                dma(ocv[l, samp, :, :], vrows[0:L, :], r=["vrows"], w=[f"ocv{l}{samp}"])
            gb, gk = psum("mx")
            for h in range(4):
                mm(gb[0:L, 64 * h:64 * h + 64], WsT[0:L, h, 0:L], vr_bf[0:L, 64 * h:64 * h + 64], r=["WsT", "vr_bf"], w=[gk])
            for h in range(4):
                stt("dve", mix[0:L, 512 + 64 * h:576 + 64 * h], gb[0:L, 64 * h:64 * h + 64], bsT[0:L, h:h + 1],
                    C_uv[0:L, 64 * h:64 * h + 64], ALU.add, ALU.mult, r=[gk, "bsT", "C_uv"], w=["mix"])

            if KFORK2:
                P.next_thread()
                curset[0] = 1
            proj_fm(2368, xnT, KT_, L, D_qT, "D_qT", eng="act")
            proj_fm(2624, xnT, KT_, L, D_sT, "D_sT", eng="dve")
            pb, pk = proj_tm(2624, 512, xnT, KT_, L)
            act(D_sg[0:L, :], pb[0:L, 0:256], AF.Sigmoid, [pk], ["D_sg"])
            cp("dve", D_v[0:L, :], pb[0:L, 256:512], [pk], ["D_v"])
            pb, pk = proj_tm(3136, 256, xnT, KT_, L)
            act(D_gs[0:L, :], pb[0:L, 0:256], AF.Silu, [pk], ["D_gs"])
            tt("pool", D_kd[0:L, :], D_sg[0:L, :], oml[0:L, :], ALU.mult, ["D_sg", "oml"], ["D_kd"])
            tt("pool", D_lf[0:L, :], D_kd[0:L, :], lbm[0:L, :], ALU.add, ["D_kd", "lbm"], ["D_lf"])
            act(D_lf[0:L, :], D_lf[0:L, :], AF.Ln, ["D_lf"], ["D_lf"])
            tt("pool", D_kd[0:L, :], oml[0:L, :], D_kd[0:L, :], ALU.subtract, ["D_kd", "oml"], ["D_kd"])
            act(D_sT[:, :, 0:L], D_sT[:, :, 0:L], AF.Sigmoid, ["D_sT"], ["D_sT"])
            for h in range(4):
                ts("dve", D_sT[:, h, 0:L], D_sT[:, h, 0:L], nomlT[:, h:h + 1], ALU.mult, omlT[:, h:h + 1], ALU.add,
                   r=["D_sT", "nomlT", "omlT"], w=["D_sT"])
            pb, pk = psum("pj")
            pbv = pb[0:64, :].rearrange("p (a b) -> p a b", a=4)
            for h in range(4):
                mm(pbv[:, h, 0:L], D_lf[0:L, 64 * h:64 * h + 64], tri_f[0:L, 0:L], r=["D_lf", "tri_f"], w=[pk])
            cp("act", D_bT[:, :, 0:L], pbv[:, :, 0:L], [pk], ["D_bT"])
            ts("dve", D_nbT[:, :, 0:L], pbv[:, :, 0:L], -1.0, ALU.mult, r=[pk], w=["D_nbT"])
            act(D_e[:, :, 0:L], D_bT[:, :, 0:L], AF.Exp, ["D_bT"], ["D_e"])
            tt("dve", D_qh[:, :, 0:L], D_qT[:, :, 0:L], D_e[:, :, 0:L], ALU.mult, ["D_qT", "D_e"], ["D_qh"])
            bs_ = min(BS, L)
            nb = (L + BS - 1) // BS
            cp("pool", D_eq[:, :, 0:bs_], D_e[:, :, 0:bs_], ["D_e"], ["D_ek"])
            for h in range(4):
                for I in range(1, nb):
                    act(D_eq[:, h, BS * I:BS * I + BS], D_bT[:, h, BS * I:BS * I + BS], AF.Exp, ["D_bT", "D_nbT"], ["D_ek"],
                        bias=D_nbT[:, h, BS * I - 1:BS * I])
            tt("dve", D_qt[:, :, 0:L], D_qT[:, :, 0:L], D_eq[:, :, 0:L], ALU.mult, ["D_qT", "D_ek"], ["D_qt"])
            ob, ok_ = psum("mx")
            ov = ob[:, 0:256].rearrange("p (h d) -> p h d", h=4)
            for h in range(4):
                for I in range(nb):
                    n = min(L, BS * (I + 1))
                    bias = zeroT[0:64, 0:1] if I == 0 else D_bT[:, h, BS * I - 1:BS * I]
                    act(D_ek[:, I, 0:n], D_bT[:, h, 0:n], AF.Exp, ["D_bT", "zeroT"], ["D_ek"], bias=bias, scale=-1.0)
                for I in range(nb):
                    n = min(L, BS * (I + 1))
                    tt("dve", D_kI[:, I, 0:n], D_sT[:, h, 0:n], D_ek[:, I, 0:n], ALU.mult, ["D_sT", "D_ek"], ["D_kI"])
                sbk, sk = psum("sc")
                for I in range(nb):
                    n = min(L, BS * (I + 1))
                    mm(sbk[0:n, BS * I:BS * I + bs_], D_kI[:, I, 0:n], D_qt[:, h, BS * I:BS * I + bs_], r=["D_kI", "D_qt"], w=[sk])
                for I in range(nb):
                    n = min(L, BS * (I + 1))
                    tt("dve", D_AT[0:n, BS * I:BS * I + bs_], sbk[0:n, BS * I:BS * I + bs_], tri_f[0:n, BS * I:BS * I + bs_],
                       ALU.mult, [sk, "tri_f"], ["D_AT"])
                mm(ov[0:L, h, :], D_AT[0:L, 0:L], D_v[0:L, 64 * h:64 * h + 64], True, False, r=["D_AT", "D_v"], w=[ok_])
                mm(ov[0:L, h, :], D_qh[:, h, 0:L], S_bf[:, h, :], False, True, r=["D_qh", "S_bf"], w=[ok_])
            cp("act", D_o[0:L], ov[0:L], [ok_], ["Bh"])
            db, dk = psum("pj")
            mm(db[0:L, 0:256], ls_f[0:L, 0:L], D_lf[0:L, :], r=["ls_f", "D_lf"], w=[dk])
            act(D_sg[0:L, :], db[0:L, 0:256], AF.Exp, [dk], ["D_sg"])
            tt("pool", D_kb[0:L, :], D_kd[0:L, :], D_sg[0:L, :], ALU.mult, ["D_kd", "D_sg"], ["D_kb"])
            sb2, sk2 = psum("mx")
            sv2 = sb2[0:64, 0:256].rearrange("p (h d) -> p h d", h=4)
            for h in range(4):
                mm(sv2[:, h, :], D_kb[0:L, 64 * h:64 * h + 64], D_v[0:L, 64 * h:64 * h + 64], r=["D_kb", "D_v"], w=[sk2])
            for h in range(4):
                stt("dve", Sst[:, h, :], Sst[:, h, :], D_e[:, h, L - 1:L], sv2[:, h, :], ALU.mult, ALU.add,
                    r=["Sst", "D_e", sk2], w=["Sst"])
            cp("dve", S_bf[:], Sst[:], ["Sst"], ["S_bf"])
            rs = head_rstd(D_o[0:L], L, Bsq, "Bh", 8)
            tt("dve", D_o[0:L], D_o[0:L], rs.unsqueeze(2).to_broadcast([L, 4, 64]), ALU.mult, ["Bh", "bv4"], ["Bh"])
            tt("dve", D_o[0:L].rearrange("p h d -> p (h d)"), D_o[0:L].rearrange("p h d -> p (h d)"), ghg[0:L, :], ALU.mult, ["Bh", "ghg"], ["Bh"])
            tt("dve", mix[0:L, 768:1024], D_o[0:L].rearrange("p h d -> p (h d)"), D_gs[0:L, :], ALU.mult, ["Bh", "D_gs"], ["mix"])

            if KFORK2:
                P.join()
                curset[0] = None
            transposes(mix, L, 8, mixT, I_bf, "mix", "xnTr", "I_bf")
            for hf in range(2):
                pb, pk = psum("pj")
                for k in range(8):
                    mm(pb[0:L, :], mixT[:, k, 0:L], WO[:, k, 512 * hf:512 * hf + 512], k == 0, k == 7, r=["xnTr", "WO"], w=[pk])
                tt("dve", xt[0:L, 512 * hf:512 * hf + 512], xt[0:L, 512 * hf:512 * hf + 512], pb[0:L, :], ALU.add, [KX, pk], [KX])
            dma(orows, xt[0:L, :], r=[KX], w=[okey])

        def state_out(dC, dn, dm, dS, tag):
            act(m4[:, 2:3], mrun[:, :], AF.Exp, ["mrun"], ["m4"], scale=-1.0)
            ts("dve", m4[:, 16:20], I_f[0:4, 0:4], m4[:, 2:3], ALU.mult, r=["I_f", "m4"], w=["m4"])
            mb, mk = psum("mx")
            mm(mb[0:64, 0:4], ones_f[0:4, 0:64], m4[:, 16:20], r=["ones_f", "m4"], w=[mk])
            cp("dve", em0[:, :], mb[0:64, 0:4], [mk], ["em0"])
            tt("dve", outst[:], Cst[:], em0[:, :].unsqueeze(2).to_broadcast([64, 4, 65]), ALU.mult, ["Cst", "em0"], ["Bsq"])
            mb, mk = psum("mx")
            mv = mb[0:64, 0:256].rearrange("p (h d) -> p h d", h=4)
            for h in range(4):
                mm(mv[:, h, :], outst[:, h, 0:64], I_f[0:64, 0:64], r=["Bsq", "I_f"], w=[mk])
            cp("dve", Cout[:], mv, [mk], ["Bh"])
            dma(dC.rearrange("h v k -> v h k"), Cout[:], r=["Bh"], w=["oC" + tag])
            dma(dn.rearrange("h k -> k h"), outst[:, :, 64], r=["Bsq"], w=["on" + tag], slow=True)
            dma(dm.rearrange("(h o) -> h o", o=1), mrun[:, :], r=["mrun"], w=["om" + tag])
            dma(dS.rearrange("h d v -> d h v"), Sst[:], r=["Sst"], w=["oS" + tag])

        if KSTOP < 2:
            P.barrier(); ph.close(); return
        memset("dve", Cst[:], 0.0, ["Cst"]); memset("dve", Cst_bf[:], 0.0, ["Cst_bf"])
        memset("dve", Sst[:], 0.0, ["Sst"]); memset("dve", S_bf[:], 0.0, ["S_bf"])
        memset("dve", mrun[:], 0.0, ["mrun"])
        front(T, xsrc_p[0:T, :], "xsrc0", 0)
        for ci in range(nch):
            wins = [((ci - j) % NW, T) for j in range(0, min(16, ci) + 1)]
            nxt = None
            if ci + 1 < nch:
                nxt = (lambda c=ci + 1: front(T, xsrc_p[c * T:(c + 1) * T, :], f"xsrc{c}", c % 2))
            chunk(T, xsrc_p[ci * T:(ci + 1) * T, :], f"xsrc{ci}", xmid[ci * T:(ci + 1) * T, :], f"xmid{ci}", wins, ci % NW,
                  (lambda h, i0, n, Lk: MP[0:Lk, h, i0:i0 + n, :]), prompt_ci=ci, sl=ci % 2, do_front=False, mid=nxt)
            if KSTOP >= 3:
                flush_pending(1)
        if KSTOP >= 3:
            flush_pending(100)
        if KSTOP >= 4:
            state_out(oCp[l], onp[l], omp[l], oSp[l], "p")
        if KSTOP < 5:
            P.barrier(); ph.close(); return

        for s in range(4):
            for q4 in range(4):
                i = stgpos[0] % 2; stgpos[0] += 1
                dma(stg[i][:, 0:1024].rearrange("p (j c) -> p j c", j=4),
                    kc[l, s, q4 * 512:(q4 + 1) * 512, :].rearrange("(j p) c -> p j c", p=128), w=[f"stg{i}"])
                cp("pool", Kcb[:], stg[i][:, 0:1024].rearrange("p (j c) -> p j c", j=4), [f"stg{i}"], ["Pt0"])
                for h in range(4):
                    pb, pk = psum("pj")
                    pv = pb[0:64, :].rearrange("p (a b) -> p a b", a=4)
                    for jj in range(4):
                        mm(pv[:, jj, :], Kcb[:, jj, 64 * h:64 * h + 64], I_bf[:, :], r=["Pt0", "I_bf"], w=[pk])
                    j0 = q4 * 4
                    cp("act" if h % 2 == 0 else "dve", KT[:, h, j0:j0 + 4, :], pv, [pk], [f"KT{j0 + q}" for q in range(4)])
                i = stgpos[0] % 2; stgpos[0] += 1
                dma(stg[i][:, 0:1024].rearrange("p (j c) -> p j c", j=4),
                    vc[l, s, q4 * 512:(q4 + 1) * 512, :].rearrange("(j p) c -> p j c", p=128), w=[f"stg{i}"])
                for jj in range(4):
                    j = q4 * 4 + jj
                    cp("pool", VW[:, j, :, 0:64], stg[i][:, jj * 256:(jj + 1) * 256].rearrange("p (h d) -> p h d", h=4),
                       [f"stg{i}"], [f"VW{j}"])
            dma(Cin[:], mC[l, s].rearrange("h v k -> v h k"), w=["Bh"])
            dma(nin[:], mn[l, s].rearrange("h k -> k h"), w=["nin"], slow=True)
            dma(em0[:], dap(mm_, (l * 4 + s) * 4, [[0, 64], [1, 4]]), w=["em0"])
            dma(mrun[:], dap(mm_, (l * 4 + s) * 4, [[1, 4], [1, 1]]), w=["mrun"])
            dma(Sst[:], hS[l, s].rearrange("h d v -> d h v"), w=["Sst"])
            cp("dve", S_bf[:], Sst[:], ["Sst"], ["S_bf"])
            act(em0[:], em0[:], AF.Exp, ["em0"], ["em0"])
            mb, mk = psum("mx")
            mv = mb[0:64, 0:256].rearrange("p (h d) -> p h d", h=4)
            for h in range(4):
                mm(mv[:, h, :], Cin[:, h, :], I_f[0:64, 0:64], r=["Bh", "I_f"], w=[mk])
            tt("dve", Cst[:, :, 0:64], mv, em0[:, :].unsqueeze(2).to_broadcast([64, 4, 64]), ALU.mult, [mk, "em0"], ["Cst"])
            tt("dve", Cst[:, :, 64], nin[:], em0[:], ALU.mult, ["nin", "em0"], ["Cst"])
            cp("dve", Cst_bf[:, :, 0:65], Cst[:], ["Cst"], ["Cst_bf"])
            wins = [(j, T) for j in range(16)] + [(16, 4)]
            r0 = NTP + 4 * s
            chunk(4, xsrc_s[4 * s:4 * s + 4, :], f"xsrcs{s}", xmid[r0:r0 + 4, :], f"xmids{s}", wins, 16,
                  (lambda h, i0, n, Lk: MS[0:Lk, h, i0:i0 + n, :]), samp=s)
            state_out(oCs[l, s], ons[l, s], oms[l, s], oSs[l, s], f"s{s}")
        P.barrier()
        ph.close()

    def phase_F(l, last):
        ph = ExitStack()
        WU = sb(f"wup{l}", [128, 8, DFF], BF, ph)
        WD = sb(f"wdn{l}", [128, 32, DM], BF, ph)
        gT = sb(f"gTf{l}", [128, 8], stack=ph)
        dma(gT[:, 0:8], dap(g_mlp, l * DM, [[1, 128], [128, 8]]), w=["gT"], slow=True)
        stg4 = stg + [sb("stgx", [128, 1088], stack=ph) for _ in range(2)]
        load_weights(WU, w_up[l], 8, DFF, gT, lambda k: k, "WU", bufs=stg4)
        load_weights(WD, w_down[l], 32, DM, None, None, "WD", bufs=stg4)
        hTs = [sb("hT", [128, 32, 128], BF, ph) for _ in range(2)]
        rls = [sb("rl", [128, 4, 128], stack=ph) for _ in range(2)]
        gfin = None
        if last:
            gfin = sb("gfin", [128, DM], stack=ph)
            dma(gfin[:], dap(g_final, 0, [[0, 128], [1, DM]]), w=["gfin"])
        rlpos = [0]

        def fchunk(L, rows_in, kin, rows_out, kout, sl):
            xt_, xn_, xnT_, hT = xts[sl], xns[sl], xnTs[sl], hTs[sl]
            kx, kn, kt, kh = f"xt{sl}", f"xn{sl}", f"xnT{sl}", f"hT{sl}"
            sf = str(sl)
            dma(xt_[0:L, :], rows_in, r=[kin], w=[kx])
            rmsnorm_rstd(xt_[0:L, :], L, DM, 1.0 / DM, ssq[0:L, 4 + sl:5 + sl], rstd[0:L, 4 + sl:5 + sl], xn_[0:L, :], [kx], kn, sfx=sf, lcol=4 + sl)
            ts("dve", xn_[0:L, :], xt_[0:L, :], rstd[0:L, 4 + sl:5 + sl], ALU.mult, r=[kx, "rstd" + sf], w=[kn])
            transposes(xn_, L, 8, xnT_, I_bf, kn, kt, "I_bf")
            for g in range(8):
                pb, pk = psum("sc" if g % 2 else "mx")
                pv = pb[:, :].rearrange("p (a b) -> p a b", a=4)
                for q in range(4):
                    f = 4 * g + q
                    for k in range(8):
                        mm(pv[:, q, 0:L], WU[:, k, 128 * f:128 * f + 128], xnT_[:, k, 0:L], k == 0, k == 7, r=["WU", kt], w=[pk])
                ri = rlpos[0] % 2
                rlpos[0] += 1
                rl = rls[ri]
                act(rl[:, :, 0:L], pv[:, :, 0:L], AF.Relu, [pk], [f"rl{ri}"])
                tt("pool" if g % 2 else "dve", hT[:, 4 * g:4 * g + 4, 0:L], rl[:, :, 0:L], rl[:, :, 0:L], ALU.mult, [f"rl{ri}"], [kh])
            for hf in range(2):
                pb, pk = psum("pj")
                for f in range(32):
                    mm(pb[0:L, :], hT[:, f, 0:L], WD[:, f, 512 * hf:512 * hf + 512], f == 0, f == 31, r=[kh, "WD"], w=[pk])
                tt("dve", xt_[0:L, 512 * hf:512 * hf + 512], xt_[0:L, 512 * hf:512 * hf + 512], pb[0:L, :], ALU.add, [kx, pk], [kx])
            if last:
                rmsnorm_rstd(xt_[0:L, :], L, DM, 1.0 / DM, ssq[0:L, 6 + sl:7 + sl], rstd[0:L, 6 + sl:7 + sl], xn_[0:L, :], [kx], kn, sfx="f" + sf, lcol=6 + sl)
                stt("dve", xt_[0:L, :], xt_[0:L, :], rstd[0:L, 6 + sl:7 + sl], gfin[0:L, :], ALU.mult, ALU.mult, r=[kx, "rstdf" + sf, "gfin"], w=[kx])
            dma(rows_out, xt_[0:L, :], r=[kx], w=[kout])

        for ci in range(nch):
            dst = y_p[ci * T:(ci + 1) * T, :] if last else xnext[ci * T:(ci + 1) * T, :]
            fchunk(T, xmid[ci * T:(ci + 1) * T, :], f"xmid{ci}", dst, f"xsrc{ci}", ci % 2)
        dst = y_s[:, :] if last else xnext[NTP:NTP + 16, :]
        fchunk(16, xmid[NTP:NTP + 16, :], "xmids_all", dst, "xsrcs_all", nch % 2)
        P.barrier()
        ph.close()

    ltm = sb("ltm", [128, 128])
    tt("dve", ltm[:], ls_f[:], I_f[:], ALU.add, ["ls_f", "I_f"], ["ltm"])

    if KSTOP >= 1:
        phase_M(0, xp, xs)
    if KSTOP >= 6:
        phase_F(0, False)
    if KSTOP >= 7 and os.environ.get("KSKIPM1", "0") != "1":
        phase_M(1, xnext, xnext[NTP:NTP + 16, :])
    if KSTOP >= 8:
        phase_F(1, True)

    semnames = sorted(P.final.keys())
    sems = {s: es.enter_context(nc.semaphore(s)) for s in semnames}
    blk = es.enter_context(nc.Block())

    def run(engname, handle):
        for waits, fn, inc in P.ops[engname]:
            for s, v in waits:
                handle.wait_ge(sems[s], v)
            if fn is not None:
                fn(handle).then_inc(sems[inc[0]], inc[1])

    @blk.tensor
    def _(e):
        run("pe", e)

    @blk.scalar
    def _(e):
        run("act", e)

    @blk.vector
    def _(e):
        run("dve", e)

    @blk.gpsimd
    def _(e):
        run("pool", e)

    @blk.sync
    def _(e):
        run("sp", e)
        for s, v in P.final.items():
            e.wait_ge(sems[s], v)

    es.close()
    return nc


def _consts():
    i = np.arange(128)
    cI = np.eye(128, dtype=np.float32)
    cJ = cI[::-1].copy()
    cTri = (i[:, None] <= i[None, :]).astype(np.float32)
    cLs = (i[:, None] > i[None, :]).astype(np.float32)
    mult = np.zeros(2049, np.float32)
    for w, d in ((128, 1), (512, 4), (2048, 16)):
        mult[np.arange(w // d + 1) * d] += 1.0
    o = np.arange(2049)
    dd = np.maximum(o, 1).astype(np.float32)
    large = 16 + (np.log(dd / 16) / np.float32(np.log(2048 / 16)) * 16).astype(np.int32)
    large = np.clip(large, 16, 31)
    bucket = np.where(o < 16, o, large)
    MO = np.zeros((32, TABN), np.float32)
    MO[bucket, o + 127] = mult
    MOr = MO[:, ::-1][:, -TABN:].copy()
    MOr = np.zeros((32, TABN), np.float32)
    MOr[:, 0:2303] = MO[:, 0:2303][:, ::-1]
    return cI, cJ, cTri, cLs, MO, MOr


_NC_CACHE = {}


def kernel(x_prompt, x_sample, cache_k_win, cache_v_win, state_mlstm_C, state_mlstm_n, state_mlstm_m,
           state_hgrn_S, rel_bias, w_in, w_out, g_attn, g_mlp, w_up, w_down, b_i, b_f, g_mlstm, g_cv,
           w_s, b_s, hgrn_lb, g_hgrn, g_final):
    f = lambda a: np.ascontiguousarray(np.asarray(a, dtype=np.float32))
    nch = NCH
    if nch not in _NC_CACHE:
        _NC_CACHE[nch] = build(nch)
    nc = _NC_CACHE[nch]
    cI, cJ, cTri, cLs, MO, MOr = _consts()
    shared = dict(relb=f(rel_bias), w_in=f(w_in), w_out=f(w_out), g_attn=f(g_attn), g_mlp=f(g_mlp), w_up=f(w_up),
                  w_down=f(w_down), b_i=f(b_i), b_f=f(b_f), g_mlstm=f(g_mlstm), g_cv=f(g_cv), w_s=f(w_s), b_s=f(b_s),
                  hlb=f(hgrn_lb), g_hgrn=f(g_hgrn), g_final=f(g_final).reshape(1, DM),
                  cI=cI, cJ=cJ, cTri=cTri, cLs=cLs, cMO=MO, cMOr=MOr)
    xp_ = f(x_prompt); xs_ = f(x_sample)
    kc_ = f(cache_k_win); vc_ = f(cache_v_win)
    in_maps = []
    for c in range(8):
        sl = slice(4 * c, 4 * c + 4)
        m = dict(shared)
        m["xp"] = np.ascontiguousarray(xp_[c % 2, :nch * T])
        m["xs"] = np.ascontiguousarray(xs_[sl].reshape(16, DM))
        m["kc"] = np.ascontiguousarray(kc_[:, sl].reshape(2, 4, 2048, 256))
        m["vc"] = np.ascontiguousarray(vc_[:, sl].reshape(2, 4, 2048, 256))
        m["mC"] = f(state_mlstm_C)[:, sl].copy()
        m["mn"] = f(state_mlstm_n)[:, sl].copy()
        m["mm"] = f(state_mlstm_m)[:, sl].copy()
        m["hS"] = f(state_hgrn_S)[:, sl].copy()
        in_maps.append(m)
    res = run_bass_kernel_spmd(nc, in_maps, core_ids=list(range(8))).results
    B = 2
    cat = lambda name, ax: np.concatenate([res[c][name] for c in range(8)], axis=ax)
    y_p = np.zeros((B, 8192, DM), np.float32)
    y_p[:, :nch * T] = np.stack([res[b]["y_p"] for b in range(B)])
    y_s = cat("y_s", 0).reshape(32, 4, DM)
    stack2 = lambda name: np.stack([res[b][name] for b in range(B)], axis=1)
    kwp = stack2("kwp").reshape(2, B, 2048, 4, 64)
    vwp = stack2("vwp").reshape(2, B, 2048, 4, 64)
    kws = cat("kws", 1).reshape(2, 32, 2048, 4, 64)
    vws = cat("vws", 1).reshape(2, 32, 2048, 4, 64)
    Cp = stack2("oCp"); np_ = stack2("onp"); mp = stack2("omp")
    Cs = cat("oCs", 1); ns = cat("ons", 1); ms = cat("oms", 1)
    Sp = stack2("oSp"); Ss = cat("oSs", 1)
    cvs = cat("ocv", 1).reshape(2, 32, 4, 4, 64)
    return (y_p, y_s, kwp, vwp, kws, vws, Cp, np_, mp, Cs, ns, ms, Sp, Ss, cvs)
```

```python
import os
from contextlib import ExitStack
import numpy as np
import concourse.bass as bass
import concourse.mybir as mybir
from concourse.bass_utils import run_bass_kernel_spmd

F32 = mybir.dt.float32
BF = mybir.dt.bfloat16
AF = mybir.ActivationFunctionType
ALU = mybir.AluOpType
AX = mybir.AxisListType

NCH = int(os.environ.get("KNCH", "64"))
SAME_SYNC = os.environ.get("KSAME", "1") == "1"
KSTOP = int(os.environ.get("KSTOP", "99"))
KCUT = int(os.environ.get("KCUT", "99"))
KB = int(os.environ.get("KB", "99"))
KFM = int(os.environ.get("KFORK", "1"))
KFORK = KFM in (1, 2)
KFORK2 = KFM in (1, 3)
KP = int(os.environ.get("KP", "99"))
T = 128
DM = 1024
DIN = 3336
DFF = 4096
BS = 64
NW = 17
TABN = 2304
EPS = 1e-6
COMPUTE = ("pe", "act", "dve", "pool")


class Prog:
    def __init__(self, kdma=8):
        self.ops = {e: [] for e in ("pe", "act", "dve", "pool", "sp")}
        self.cnt = {e: 0 for e in self.ops}
        self.dcnt = {e: 0 for e in self.ops}
        self.lastw = {}
        self.readers = {}
        self.waited = {e: {} for e in self.ops}
        self.K = kdma
        self.final = {}

    def fork(self):
        self.threads = [[]]

    def next_thread(self):
        self.threads.append([])

    def join(self):
        lists = self.threads
        self.threads = None
        tot = [max(1, len(x)) for x in lists]
        pos = [0] * len(lists)
        while True:
            best, bf = -1, 2.0
            for i, x in enumerate(lists):
                if pos[i] < len(x):
                    f = pos[i] / tot[i]
                    if f < bf:
                        best, bf = i, f
            if best < 0:
                break
            a = lists[best][pos[best]]
            pos[best] += 1
            self.op(*a)

    def op(self, eng, fn, r=(), w=(), dma=False):
        if getattr(self, "threads", None) is not None:
            self.threads[-1].append((eng, fn, tuple(r), tuple(w), dma))
            return None
        deps = []
        for x in r:
            t = self.lastw.get(x)
            if t:
                deps.append(t)
            if x.startswith("ps"):
                for s, (v, e) in self.readers.get(x, {}).items():
                    if e != eng:
                        deps.append((s, v, e, s.startswith("d_")))
        for x in w:
            t = self.lastw.get(x)
            if t:
                deps.append(t)
            for s, (v, e) in self.readers.get(x, {}).items():
                deps.append((s, v, e, s.startswith("d_")))
        if dma:
            i = self.dcnt[eng]
            self.dcnt[eng] += 1
            sem = f"d_{eng}{i % self.K}"
            val = 16 * (i // self.K + 1)
            if i >= self.K:
                deps.append((sem, val - 16, eng, True))
            tok = (sem, val, eng, True)
            inc = (sem, 16)
        else:
            self.cnt[eng] += 1
            sem = f"c_{eng}"
            val = self.cnt[eng]
            tok = (sem, val, eng, False)
            inc = (sem, 1)
        self.final[sem] = val
        waits = {}
        for (s, v, e, isd) in deps:
            if e == eng and not isd and not dma:
                if eng == "pe" or not SAME_SYNC:
                    continue
            if self.waited[eng].get(s, 0) >= v:
                continue
            waits[s] = max(waits.get(s, 0), v)
        for s, v in waits.items():
            self.waited[eng][s] = v
        self.ops[eng].append((list(waits.items()), fn, inc))
        for x in r:
            d = self.readers.setdefault(x, {})
            if d.get(sem, (0, None))[0] < val:
                d[sem] = (val, eng)
        for x in w:
            self.lastw[x] = tok
            self.readers[x] = {}
        return tok

    def barrier(self):
        for eng in self.ops:
            waits = []
            for s, v in self.final.items():
                if s == f"c_{eng}" and eng == "pe":
                    continue
                if self.waited[eng].get(s, 0) >= v:
                    continue
                self.waited[eng][s] = v
                waits.append((s, v))
            if waits:
                self.ops[eng].append((waits, None, None))


def build(nch):
    nc = bass.Bass("TRN2", target_bir_lowering=False)
    P = Prog()
    es = ExitStack()

    def din(name, shape):
        return nc.dram_tensor(name, list(shape), F32, kind="ExternalInput").ap()

    def dout(name, shape):
        return nc.dram_tensor(name, list(shape), F32, kind="ExternalOutput").ap()

    def dint(name, shape):
        return nc.dram_tensor(name, list(shape), F32, kind="Internal").ap()

    NTP = nch * T
    xp = din("xp", [NTP, DM])
    xs = din("xs", [16, DM])
    kc = din("kc", [2, 4, 2048, 256])
    vc = din("vc", [2, 4, 2048, 256])
    mC = din("mC", [2, 4, 4, 64, 64])
    mn = din("mn", [2, 4, 4, 64])
    mm_ = din("mm", [2, 4, 4])
    hS = din("hS", [2, 4, 4, 64, 64])
    relb = din("relb", [32, 4])
    w_in = din("w_in", [2, DM, DIN])
    w_out = din("w_out", [2, DM, DM])
    g_attn = din("g_attn", [2, DM])
    g_mlp = din("g_mlp", [2, DM])
    w_up = din("w_up", [2, DM, DFF])
    w_down = din("w_down", [2, DFF, DM])
    b_i = din("b_i", [2, 4])
    b_f = din("b_f", [2, 4])
    g_mlstm = din("g_mlstm", [2, 256])
    g_cv = din("g_cv", [2, 256])
    w_s = din("w_s", [2, 4, 128, 128])
    b_s = din("b_s", [2, 4, 128])
    hlb = din("hlb", [2, 256])
    g_hgrn = din("g_hgrn", [2, 256])
    g_final = din("g_final", [1, DM])
    cI = din("cI", [128, 128])
    cJ = din("cJ", [128, 128])
    cTri = din("cTri", [128, 128])
    cLs = din("cLs", [128, 128])
    cMO = din("cMO", [32, TABN])
    cMOr = din("cMOr", [32, TABN])

    y_p = dout("y_p", [NTP, DM])
    y_s = dout("y_s", [16, DM])
    kwp = dout("kwp", [2, 2048, 256])
    vwp = dout("vwp", [2, 2048, 256])
    kws = dout("kws", [2, 4, 2048, 256])
    vws = dout("vws", [2, 4, 2048, 256])
    oCp = dout("oCp", [2, 4, 64, 64])
    onp = dout("onp", [2, 4, 64])
    omp = dout("omp", [2, 4])
    oCs = dout("oCs", [2, 4, 4, 64, 64])
    ons = dout("ons", [2, 4, 4, 64])
    oms = dout("oms", [2, 4, 4])
    oSp = dout("oSp", [2, 4, 64, 64])
    oSs = dout("oSs", [2, 4, 4, 64, 64])
    ocv = dout("ocv", [2, 4, 4, 256])

    xmid = dint("xmid", [NTP + 16, DM])
    xnext = dint("xnext", [NTP + 16, DM])
    wtab = dint("wtab", [4, TABN])
    wtabr = dint("wtabr", [4, TABN])

    uid = [0]

    def sb(name, shape, dt=F32, stack=None):
        uid[0] += 1
        return (stack or es).enter_context(nc.sbuf_tensor(f"{name}_{uid[0]}", list(shape), dt))

    banks = [es.enter_context(nc.psum_tensor(f"ps{i}", [128, 512], F32)) for i in range(8)]
    rings = {"pj": [0, 1, 2], "sc": [3, 4], "pv": [5], "mx": [6, 7]}
    rpos = {k: 0 for k in rings}

    ringsets = {0: {"pj": [0], "sc": [1, 2], "pv": [3], "mx": [3]},
                1: {"pj": [4, 5], "sc": [6], "mx": [7], "pv": [7]}}
    curset = [None]

    def psum(role):
        rg = rings if curset[0] is None else ringsets[curset[0]]
        i = rg[role][rpos[role] % len(rg[role])]
        rpos[role] += 1
        return banks[i], f"ps{i}"

    def mm(out, lhsT, rhs, start=True, stop=True, r=(), w=()):
        P.op("pe", lambda e, o=out, a=lhsT, b=rhs, s=start, t=stop: e.matmul(o, lhsT=a, rhs=b, start=s, stop=t), r, w)

    def act(out, in_, func, r=(), w=(), bias=None, scale=None, accum=None):
        kw = {}
        if bias is not None:
            kw["bias"] = bias
        if scale is not None:
            kw["scale"] = scale
        if accum is not None:
            kw["accum_out"] = accum
        P.op("act", lambda e, o=out, i=in_, f=func, k=kw: e.activation(out=o, in_=i, func=f, **k), r, w)

    def tt(eng, out, a, b, op, r=(), w=()):
        P.op(eng, lambda e, o=out, x=a, y=b, p=op: e.tensor_tensor(out=o, in0=x, in1=y, op=p), r, w)

    def ts(eng, out, a, s1, op0, s2=None, op1=None, r=(), w=()):
        if op1 is None:
            P.op(eng, lambda e, o=out, x=a, q=s1, p=op0: e.tensor_scalar(out=o, in0=x, scalar1=q, scalar2=None, op0=p), r, w)
        else:
            P.op(eng, lambda e, o=out, x=a, q=s1, p=op0, q2=s2, p2=op1: e.tensor_scalar(out=o, in0=x, scalar1=q, scalar2=q2, op0=p, op1=p2), r, w)

    def stt(eng, out, a, s, b, op0, op1, r=(), w=()):
        P.op(eng, lambda e, o=out, x=a, q=s, y=b, p=op0, p2=op1: e.scalar_tensor_tensor(out=o, in0=x, scalar=q, in1=y, op0=p, op1=p2), r, w)

    def cp(eng, out, in_, r=(), w=()):
        if eng == "act":
            P.op("act", lambda e, o=out, i=in_: e.copy(out=o, in_=i), r, w)
        else:
            P.op(eng, lambda e, o=out, i=in_: e.tensor_copy(out=o, in_=i), r, w)

    def memset(eng, ap, val, w=()):
        P.op(eng, lambda e, a=ap, v=val: e.memset(a, v), (), w)

    def dma(out, in_, r=(), w=(), eng="sp", slow=False):
        if slow:
            P.op(eng, lambda e, o=out, i=in_: e.dma_start(out=o, in_=i, allow_slow_non_contiguous=True), r, w, dma=True)
        else:
            P.op(eng, lambda e, o=out, i=in_: e.dma_start(out=o, in_=i), r, w, dma=True)

    def dap(base, off, pat):
        return bass.AP(base.tensor, off, [list(p) for p in pat])

    I_f = sb("I_f", [128, 128]); J_f = sb("J_f", [128, 128]); tri_f = sb("tri_f", [128, 128])
    ls_f = sb("ls_f", [128, 128]); ones_f = sb("ones_f", [128, 128])
    I_bf = sb("I_bf", [128, 128], BF); J_bf = sb("J_bf", [128, 128], BF)
    epsT = sb("epsT", [128, 1]); oneT = sb("oneT", [128, 1]); zeroT = sb("zeroT", [128, 1])
    dma(I_f[:], cI[:, :], w=["I_f"]); dma(J_f[:], cJ[:, :], w=["J_f"])
    dma(tri_f[:], cTri[:, :], w=["tri_f"]); dma(ls_f[:], cLs[:, :], w=["ls_f"])
    memset("pool", ones_f[:], 1.0, ["ones_f"]); memset("pool", epsT[:], EPS, ["epsT"])
    memset("pool", oneT[:], 1.0, ["oneT"]); memset("pool", zeroT[:], 0.0, ["zeroT"])
    cp("pool", I_bf[:], I_f[:], ["I_f"], ["I_bf"]); cp("pool", J_bf[:], J_f[:], ["J_f"], ["J_bf"])

    with ExitStack() as st0:
        rb = sb("rb", [32, 4], stack=st0); erb = sb("erb", [32, 4], stack=st0)
        mo = sb("mo", [32, TABN], stack=st0); wt = sb("wt", [4, TABN], stack=st0)
        dma(rb[:], relb[:, :], w=["rb"])
        act(erb[:], rb[:], AF.Exp, ["rb"], ["erb"])
        for src, dst in ((cMO, wtab), (cMOr, wtabr)):
            dma(mo[:], src[:, :], w=["mo"])
            for c0 in range(0, TABN, 512):
                n = min(512, TABN - c0)
                pb, pk = psum("mx")
                mm(pb[0:4, 0:n], erb[:, :], mo[:, c0:c0 + n], r=["erb", "mo"], w=[pk])
                cp("dve", wt[:, c0:c0 + n], pb[0:4, 0:n], [pk], ["wt"])
            dma(dst[:, :], wt[:], r=["wt"], w=["wtab" if dst is wtab else "wtabr"])
        P.barrier()

    pending = []
    for l in range(2):
        for s in range(4):
            pending.append((kws[l, s, 0:2044, :], kc[l, s, 4:2048, :], f"kws{l}{s}"))
            pending.append((vws[l, s, 0:2044, :], vc[l, s, 4:2048, :], f"vws{l}{s}"))

    def flush_pending(n):
        for _ in range(n):
            if pending:
                o, i, k = pending.pop(0)
                dma(o, i, w=[k], eng="act")

    xt = sb("xt", [128, DM]); xn = sb("xn", [128, DM], BF)
    xnT = sb("xnT", [128, 8, 128], BF)
    xts = [xt, sb("xt2", [128, DM])]
    xns = [xn, sb("xn2", [128, DM], BF)]
    xnTs = [xnT, sb("xnT2", [128, 8, 128], BF)]
    ssq = sb("ssq", [128, 8]); rstd = sb("rstd", [128, 8]); lnv = sb("lnv", [128, 8])
    stg = [sb(f"stg{i}", [128, 1088]) for i in range(2)]
    stgpos = [0]

    def rmsnorm_rstd(src, L, ncol, scale, ssq_ap, rstd_ap, junk, keys_r, key_junk, sfx="", lcol=0):
        memset("dve", ssq_ap, 0.0, ["ssq" + sfx])
        act(junk, src, AF.Square, keys_r + ["ssq" + sfx], [key_junk, "ssq" + sfx], accum=ssq_ap)
        act(lnv[0:L, lcol:lcol + 1], ssq_ap, AF.Ln, ["ssq" + sfx, "epsT"], ["lnv" + sfx], bias=epsT[0:L, 0:1], scale=scale)
        act(rstd_ap, lnv[0:L, lcol:lcol + 1], AF.Exp, ["lnv" + sfx], ["rstd" + sfx], scale=-0.5)

    def transposes(src, L, nk, dst, mat, kr, kw, kmat):
        for g in range(0, nk, 4):
            pb, pk = psum("pj")
            pv = pb[:, :].rearrange("p (a b) -> p a b", a=4)
            for k in range(g, min(g + 4, nk)):
                mm(pv[:, k - g, 0:L], src[0:L, k * 128:(k + 1) * 128], mat[0:L, 0:L], r=[kr, kmat], w=[pk])
            n = min(4, nk - g)
            cp("act" if (g // 4) % 2 == 0 else "dve", dst[:, g:g + n, 0:L], pv[:, 0:n, 0:L], [pk], [kw])

    def load_weights(dst3, src2, nk, ncols, gT, gsel, keyw, segs=None, bufs=None, bkeys=None):
        if segs is None:
            segs = [(0, ncols, 0)]
        segs = [(c0 + o, min(1024, n - o), d0 + o) for (c0, n, d0) in segs for o in range(0, n, 1024)]
        for k in range(nk):
            for (c0, n, d0) in segs:
                sg = bufs if bufs is not None else stg
                i = stgpos[0] % len(sg)
                stgpos[0] += 1
                ce = "pool" if (bufs is None or i % 2 == 0) else "dve"
                sk_ = bkeys[i] if bkeys is not None else f"stg{i}"
                dma(sg[i][:, 0:n], src2[k * 128:(k + 1) * 128, c0:c0 + n], w=[sk_], eng=("sp" if i % 2 == 0 else "act"))
                gi = gsel(k) if gT is not None else None
                if gi is None:
                    cp(ce, dst3[:, k, d0:d0 + n], sg[i][:, 0:n], [sk_], [keyw])
                else:
                    ts(ce, dst3[:, k, d0:d0 + n], sg[i][:, 0:n], gT[:, gi:gi + 1], ALU.mult,
                       r=[sk_, "gT"], w=[keyw])

    def phase_M(l, xsrc_p, xsrc_s):
        ph = ExitStack()
        W = sb(f"win{l}", [128, 8, 3392], BF, ph)
        WO = sb(f"wout{l}", [128, 8, DM], BF, ph)
        gT = sb(f"gT{l}", [128, 12], stack=ph)
        dma(gT[:, 0:8], dap(g_attn, l * DM, [[1, 128], [128, 8]]), w=["gT"], slow=True)
        dma(gT[:, 8:10], dap(g_mlstm, l * 256, [[1, 128], [128, 2]]), w=["gT"], slow=True)
        dma(gT[:, 10:12], dap(g_hgrn, l * 256, [[1, 128], [128, 2]]), w=["gT"], slow=True)
        Pts = [sb("Pt", [128, NW, 128], BF, ph) for _ in range(2)]
        stgm = stg + [p_[:, :, :].rearrange("p a b -> p (a b)").bitcast(F32) for p_ in Pts]
        stgk = ["stg0", "stg1", "Pt0", "Pt1"]
        load_weights(W, w_in[l], 8, DIN, gT, lambda k: k, "W", segs=[(0, 1800, 0), (1800, 1536, 1856)], bufs=stgm, bkeys=stgk)
        load_weights(WO, w_out[l], 8, DM, None, None, "WO", bufs=stgm, bkeys=stgk)

        bfi = sb(f"bfi{l}", [128, 8], stack=ph)
        dma(bfi[:, 0:4], dap(b_i, l * 4, [[0, 128], [1, 4]]), w=["bfi"])
        dma(bfi[:, 4:8], dap(b_f, l * 4, [[0, 128], [1, 4]]), w=["bfi"])
        gcv = sb(f"gcv{l}", [128, 256], stack=ph)
        dma(gcv[:], dap(g_cv, l * 256, [[0, 128], [1, 256]]), w=["gcv"])
        gml = sb(f"gml{l}", [128, 256], stack=ph); ghg = sb(f"ghg{l}", [128, 256], stack=ph)
        dma(gml[:], dap(g_mlstm, l * 256, [[0, 128], [1, 256]]), w=["gml"])
        dma(ghg[:], dap(g_hgrn, l * 256, [[0, 128], [1, 256]]), w=["ghg"])
        oml = sb(f"oml{l}", [128, 256], stack=ph)
        lbm = sb(f"lbm{l}", [128, 256], stack=ph)
        C_uv = sb("C_uv", [128, 512], stack=ph); C_t = sb("C_t", [128, 512], stack=ph)
        lbt = C_uv[:, 0:256]
        lbT = sb(f"lbT{l}", [64, 4], stack=ph); omlT = sb(f"omlT{l}", [64, 4], stack=ph); nomlT = sb(f"nomlT{l}", [64, 4], stack=ph)
        if l == 0:
            memset("dve", lbt, 0.0, ["C_uv"]); memset("dve", lbT[:], 0.0, ["lbT"])
        else:
            t0 = C_t[:, 0:256]; t1 = C_t[:, 256:512]
            dma(t0, dap(hlb, 0, [[0, 128], [1, 256]]), w=["C_t"])
            dma(t1, dap(hlb, 256, [[0, 128], [1, 256]]), w=["C_t"])
            tt("dve", t1, t1, t0, ALU.subtract, ["C_t"], ["C_t"])
            act(lbt, t1, AF.Sigmoid, ["C_t"], ["C_uv"])
            u0 = sb("lbtmp2", [64, 4], stack=ph); u1 = sb("lbtmp3", [64, 4], stack=ph)
            dma(u0[:], dap(hlb, 0, [[1, 64], [64, 4]]), w=["lbu0"], slow=True)
            dma(u1[:], dap(hlb, 256, [[1, 64], [64, 4]]), w=["lbu1"], slow=True)
            tt("dve", u1[:], u1[:], u0[:], ALU.subtract, ["lbu0", "lbu1"], ["lbu1"])
            act(lbT[:], u1[:], AF.Sigmoid, ["lbu1"], ["lbT"])
        ts("dve", oml[:], lbt, -1.0, ALU.mult, 1.0, ALU.add, r=["C_uv"], w=["oml"])
        ts("dve", lbm[:], lbt, 1e-30, ALU.max, r=["C_uv"], w=["lbm"])
        ts("dve", omlT[:], lbT[:], -1.0, ALU.mult, 1.0, ALU.add, r=["lbT"], w=["omlT"])
        ts("dve", nomlT[:], omlT[:], -1.0, ALU.mult, r=["omlT"], w=["nomlT"])
        WsT = sb(f"WsT{l}", [128, 4, 128], BF, ph); bsT = sb(f"bsT{l}", [128, 4], stack=ph)
        dma(bsT[:], dap(b_s, l * 512, [[1, 128], [128, 4]]), w=["bsT"], slow=True)
        wsm = sb("wsm", [128, 128], BF, ph)
        for h in range(4):
            i = stgpos[0] % 2
            stgpos[0] += 1
            dma(stg[i][:, 0:128], w_s[l, h, :, :], w=[f"stg{i}"])
            tt("dve", wsm[:], stg[i][:, 0:128], ltm[:], ALU.mult, [f"stg{i}", "ltm"], ["wsm"])
            pb, pk = psum("pj")
            mm(pb[:, 0:128], wsm[:, :], I_bf[:, :], r=["wsm", "I_bf"], w=[pk])
            cp("dve", WsT[:, h, :], pb[:, 0:128], [pk], ["WsT"])

        KT = sb("KT", [64, 4, NW, 128], BF, ph)
        VW = sb("VW", [128, NW, 4, 96], BF, ph)
        MP = sb("MP", [128, 4, NW, 128], BF, ph)
        MS = sb("MS", [128, 4, NW, 4], stack=ph)
        for h in range(4):
            for (j0, nj) in ((0, 6), (6, 6), (12, 5)):
                i = stgpos[0] % 2
                stgpos[0] += 1
                mstg = stg[i][:, 0:nj * 128].rearrange("p (j t) -> p j t", j=nj)
                dma(mstg, dap(wtab, h * TABN + 128 * j0, [[1, 128], [128, nj], [1, 128]]), r=["wtab"], w=[f"stg{i}"])
                cp("pool", MP[:, h, j0:j0 + nj, :], mstg, [f"stg{i}"], ["MP"])
            for t in range(4):
                dma(MS[:, h, :, t], dap(wtabr, h * TABN + 127 - t, [[1, 128], [128, NW]]), r=["wtabr"], w=["MS"], slow=True)
        memset("pool", VW[:], 1.0, [f"VW{j}" for j in range(NW)])

        A_qT = sb("A_qT", [64, 4, 128], BF, ph)
        A_kv = stg[0][:, 0:512]
        pexps = [sb("pexp", [128, 4, 128], stack=ph) for _ in range(2)]
        rden = sb("rden", [128, 4], stack=ph)
        mix = sb("mix", [128, DM], BF, ph)
        xnTr = sb("xnTr", [128, 8, 128], BF, ph)
        mixT = xnTr
        Kcb = Pts[0][:, 0:8, :].rearrange("p (a b) c -> p a (b c)", a=4)
        B_qT = sb("B_qT", [64, 4, 128], BF, ph); B_kT = sb("B_kT", [64, 4, 128], BF, ph)
        B_k = C_uv[:, 0:256]; Bv = sb("Bv", [128, 4, 96], BF, ph)
        B_o = C_uv[:, 256:512]; Bif = sb("Bif", [128, 8], stack=ph)
        bv4 = sb("bv4", [128, 64], stack=ph)
        ktil = sb("ktil", [128, 256], BF, ph); St = sb("St", [128, 128], BF, ph)
        Cst = sb("Cst", [64, 4, 65], stack=ph); Cst_bf = sb("Cst_bf", [64, 4, 96], BF, ph)
        edL = sb("edL", [64, 4], stack=ph)
        Bh = sb("Bh", [128, 4, 64], stack=ph); Bsq65 = sb("Bsq", [128, 4, 80], stack=ph); Bsq = Bsq65[:, :, 0:64]
        mrun = sb("mrun", [4, 1], stack=ph); m4 = sb("m4", [4, 32], stack=ph)
        memset("pool", Bv[:], 1.0, ["Bv"])
        vrows = sb("vrows", [128, 256], stack=ph); vr_bf = sb("vr_bf", [128, 256], BF, ph)
        D_qT = sb("D_qT", [64, 4, 128], stack=ph); D_sT = sb("D_sT", [64, 4, 128], stack=ph)
        D_bT = sb("D_bT", [64, 4, 128], stack=ph); D_nbT = sb("D_nbT", [64, 4, 128], stack=ph)
        D_e = sb("D_e", [64, 4, 128], stack=ph)
        D_ek = sb("D_ek", [64, 4, 128], stack=ph)
        D_eq = D_ek
        D_qh = sb("D_qh", [64, 4, 128], BF, ph); D_qt = sb("D_qt", [64, 4, 128], F32, ph)
        D_kI = sb("D_kI", [64, 4, 128], F32, ph)
        D_sg = sb("D_sg", [128, 256], stack=ph); D_lf = sb("D_lf", [128, 256], stack=ph)
        D_kd = sb("D_kd", [128, 256], stack=ph); D_kb = sb("D_kb", [128, 256], BF, ph)
        D_v = sb("D_v", [128, 256], BF, ph); D_AT = sb("D_AT", [128, 128], BF, ph)
        Sst = sb("Sst", [64, 4, 64], stack=ph); S_bf = sb("S_bf", [64, 4, 64], BF, ph)
        D_o = Bh; D_gs = sb("D_gs", [128, 256], stack=ph)
        memset("pool", D_AT[:], 0.0, ["D_AT"])
        outst = Bsq65[0:64, :, 0:65]; Cout = Bh[0:64]
        Cin = Cout; nin = sb("nin", [64, 4], stack=ph); em0 = sb("em0", [64, 4], stack=ph)

        print("phase M sbuf remaining", nc.sbuf_bytes_remaining)

        def v4(i, L):
            c = {0: 0, 2: 16}.get(i, 32 + 4 * i if i < 2 else 28 + 4 * i)
            return bv4[0:L, c:c + 4]

        def proj_fm(c0, srcT, ksrc, L, dst, kdst, scale=None, eng="act"):
            pb, pk = psum("pj")
            pv = pb[0:64, :].rearrange("p (a b) -> p a b", a=4)
            for h in range(4):
                for k in range(8):
                    mm(pv[:, h, 0:L], W[:, k, c0 + 64 * h:c0 + 64 * h + 64], srcT[:, k, 0:L], k == 0, k == 7,
                       r=["W", ksrc], w=[pk])
            if scale is None:
                cp(eng, dst[:, :, 0:L], pv[:, :, 0:L], [pk], [kdst])
            else:
                ts("dve", dst[:, :, 0:L], pv[:, :, 0:L], scale, ALU.mult, r=[pk], w=[kdst])

        def proj_tm(c0, n, srcT, ksrc, L):
            pb, pk = psum("pj")
            for k in range(8):
                mm(pb[0:L, 0:n], srcT[:, k, 0:L], W[:, k, c0:c0 + n], k == 0, k == 7, r=["W", ksrc], w=[pk])
            return pb, pk

        def head_rstd(src3, L, sq3, ksrc, col):
            tt("dve", sq3[0:L], src3, src3, ALU.mult, [ksrc], ["Bsq"])
            P.op("dve", lambda e, o=v4(col, L), i=sq3[0:L]: e.tensor_reduce(out=o, in_=i, axis=AX.X, op=ALU.add),
                 ["Bsq"], ["bv4"])
            act(v4(col, L), v4(col, L), AF.Ln, ["bv4", "epsT"], ["bv4"], bias=epsT[0:L, 0:1], scale=1.0 / 64)
            act(v4(col, L), v4(col, L), AF.Exp, ["bv4"], ["bv4"], scale=-0.5)
            return v4(col, L)

        apos = [0]

        def front(L, xrows, xkey, sl):
            xt_, xn_, xnT_ = xts[sl], xns[sl], xnTs[sl]
            kx, kn, kt = f"xt{sl}", f"xn{sl}", f"xnT{sl}"
            dma(xt_[0:L, :], xrows, r=[xkey], w=[kx])
            rmsnorm_rstd(xt_[0:L, :], L, DM, 1.0 / DM, ssq[0:L, sl:sl + 1], rstd[0:L, sl:sl + 1], xn_[0:L, :], [kx], kn, sfx=f"m{sl}", lcol=sl)
            ts("dve", xn_[0:L, :], xt_[0:L, :], rstd[0:L, sl:sl + 1], ALU.mult, r=[kx, f"rstdm{sl}"], w=[kn])
            transposes(xn_, L, 8, xnT_, I_bf, kn, kt, "I_bf")

        def chunk(L, xrows, xkey, orows, okey, wins, cur, maskf, prompt_ci=None, samp=None, sl=0, do_front=True, mid=None):
            if do_front:
                front(L, xrows, xkey, sl)
            xt, xn, xnT = xts[sl], xns[sl], xnTs[sl]
            KX, KN, KT_ = f"xt{sl}", f"xn{sl}", f"xnT{sl}"
            if samp is None:
                transposes(xn, L, 8, xnTr, J_bf, KN, "xnTr", "J_bf")
                ksrcT, kkey = xnTr, "xnTr"
            else:
                ksrcT, kkey = xnT, KT_
            if KFORK:
                P.fork()
                curset[0] = 0
            proj_fm(0, xnT, KT_, L, A_qT, "A_qT", scale=0.125)
            pb, pk = psum("pj")
            pv = pb[0:64, :].rearrange("p (a b) -> p a b", a=4)
            for h in range(4):
                for k in range(8):
                    mm(pv[:, h, 0:L], W[:, k, 256 + 64 * h:320 + 64 * h], ksrcT[:, k, 0:L], k == 0, k == 7,
                       r=["W", kkey], w=[pk])
            cp("act", KT[:, :, cur, 0:L], pv[:, :, 0:L], [pk], [f"KT{cur}"])
            pb, pk = proj_tm(512, 256, ksrcT, kkey, L)
            cp("dve", VW[0:L, cur, :, 0:64], pb[0:L, 0:256].rearrange("p (h d) -> p h d", h=4), [pk], [f"VW{cur}"])
            need_kv = (samp is not None) or (prompt_ci is not None and prompt_ci >= nch - 16)
            if need_kv:
                pb, pk = proj_tm(256, 512, xnT, KT_, L)
                cp("act", A_kv[0:L, :], pb[0:L, 0:512], [pk], ["stg0"])
                if samp is None:
                    r0 = (prompt_ci - (nch - 16)) * T + (2048 - 16 * T if nch >= 16 else 0)
                    if nch >= 16:
                        dma(kwp[l, r0:r0 + T, :], A_kv[0:L, 0:256], r=["stg0"], w=["o_kwp"])
                        dma(vwp[l, r0:r0 + T, :], A_kv[0:L, 256:512], r=["stg0"], w=["o_vwp"])
                else:
                    dma(kws[l, samp, 2044:2048, :], A_kv[0:L, 0:256], r=["stg0"], w=[f"kws{l}{samp}n"])
                    dma(vws[l, samp, 2044:2048, :], A_kv[0:L, 256:512], r=["stg0"], w=[f"vws{l}{samp}n"])
            if KCUT < 2:
                return
            pvb, pvk = psum("pv")
            pvv = pvb[:, 0:320].rearrange("p (h d) -> p h d", h=4)[:, :, 0:65]
            nw = len(wins)
            groups = []
            g0 = 0
            while g0 < nw:
                g1 = g0
                while g1 < nw and g1 - g0 < 4 and wins[g1][1] == wins[g0][1]:
                    g1 += 1
                groups.append((g0, g1))
                g0 = g1
            for h in range(4):
                Pt = Pts[h % 2]
                kpt = f"Pt{h % 2}"
                for (ga, gb) in groups:
                    Lk = wins[ga][1]
                    n = gb - ga
                    sbk, sk = psum("sc")
                    sv = sbk[:, :].rearrange("p (a b) -> p a b", a=4)
                    for gi in range(n):
                        slot = wins[ga + gi][0]
                        mm(sv[0:Lk, gi, 0:L], KT[:, h, slot, 0:Lk], A_qT[:, h, 0:L], r=[f"KT{slot}", "A_qT"], w=[sk])
                    pi = apos[0] % 2
                    apos[0] += 1
                    pexp = pexps[pi]
                    act(pexp[0:Lk, 0:n, 0:L], sv[0:Lk, 0:n, 0:L], AF.Exp, [sk], [f"pexp{pi}"])
                    tt("dve" if pi == 0 else "pool", Pt[0:Lk, ga:gb, 0:L], pexp[0:Lk, 0:n, 0:L], maskf(h, ga, n, Lk), ALU.mult,
                       [f"pexp{pi}", "MP", "MS"], [kpt])
                for wi, (slot, Lk) in enumerate(wins):
                    mm(pvv[0:L, h, :], Pt[0:Lk, wi, 0:L], VW[0:Lk, slot, h, 0:65], wi == 0, wi == nw - 1,
                       r=[kpt, f"VW{slot}"], w=[pvk])
            P.op("dve", lambda e, o=rden[0:L, :], i=pvv[0:L, :, 64]: e.reciprocal(out=o, in_=i), [pvk], ["rden"])
            tt("dve", mix[0:L, 0:256].rearrange("p (h d) -> p h d", h=4), pvv[0:L, :, 0:64],
               rden[0:L, :].unsqueeze(2).to_broadcast([L, 4, 64]), ALU.mult, [pvk, "rden"], ["mix"])

            if KFORK:
                P.next_thread()
                curset[0] = 1
            proj_fm(768, xnT, KT_, L, B_qT, "B_qT", eng="act")
            if KP < 1:
                return
            proj_fm(1024, xnT, KT_, L, B_kT, "B_kT", scale=0.125)
            if KP < 2:
                return
            pb, pk = proj_tm(1024, 512, xnT, KT_, L)
            cp("dve", B_k[0:L, :], pb[0:L, 0:256], [pk], ["C_uv"])
            cp("dve", Bv[0:L, :, 0:64], pb[0:L, 256:512].rearrange("p (h d) -> p h d", h=4), [pk], ["Bv"])
            if KP < 3:
                return
            pb, pk = proj_tm(1536, 264, xnT, KT_, L)
            cp("act", B_o[0:L, :], pb[0:L, 0:256], [pk], ["C_uv"])
            cp("dve", Bif[0:L, :], pb[0:L, 256:264], [pk], ["Bif"])
            if KB < 1:
                return
            tt("dve", Bif[0:L, :], Bif[0:L, :], bfi[0:L, :], ALU.add, ["Bif", "bfi"], ["Bif"])
            sp_, cs_, a_, wk_, ecs_, den_, rd_ = (v4(i, L) for i in range(7))
            act(sp_, Bif[0:L, 4:8], AF.Exp, ["Bif"], ["bv4"], scale=-1.0)
            act(sp_, sp_, AF.Ln, ["bv4", "oneT"], ["bv4"], bias=oneT[0:L, 0:1])
            if KB < 2:
                return
            mb, mk = psum("mx")
            mm(mb[0:L, 0:4], tri_f[0:L, 0:L], sp_, r=["tri_f", "bv4"], w=[mk])
            mm(mb[0:64, 16:20], ones_f[0:L, 0:64], sp_, r=["ones_f", "bv4"], w=[mk])
            mm(mb[0:4, 32:33], sp_, ones_f[0:L, 0:1], r=["ones_f", "bv4"], w=[mk])
            cp("dve", cs_, mb[0:L, 0:4], [mk], ["bv4"])
            tt("dve", a_, Bif[0:L, 0:4], cs_, ALU.add, ["Bif", "bv4"], ["bv4"])
            act(wk_, a_, AF.Exp, ["bv4"], ["bv4"])
            act(ecs_, cs_, AF.Exp, ["bv4"], ["bv4"])
            act(edL[:, :], mb[0:64, 16:20], AF.Exp, [mk], ["edL"], scale=-1.0)
            cp("dve", m4[:, 1:2], mb[0:4, 32:33], [mk], ["m4"])
            if KB < 3:
                return
            mb2, mk2 = psum("mx")
            mm(mb2[0:4, 0:L], a_, I_f[0:L, 0:L], r=["bv4", "I_f"], w=[mk2])
            P.op("dve", lambda e, o=m4[:, 0:1], i=mb2[0:4, 0:L]: e.tensor_reduce(out=o, in_=i, axis=AX.X, op=ALU.max),
                 [mk2], ["m4"])
            tt("dve", mrun[:, :], mrun[:, :], m4[:, 0:1], ALU.max, ["mrun", "m4"], ["mrun"])
            tt("dve", mrun[:, :], mrun[:, :], m4[:, 1:2], ALU.subtract, ["mrun", "m4"], ["mrun"])
            if KB < 4:
                return
            for h in range(4):
                ts("dve", ktil[0:L, 64 * h:64 * h + 64], B_k[0:L, 64 * h:64 * h + 64], wk_[:, h:h + 1], ALU.mult,
                   0.125, ALU.mult, r=["C_uv", "bv4"], w=["ktil"])
            if KB < 5:
                return
            brb, brk = psum("mx")
            brv = brb[:, 0:320].rearrange("p (h d) -> p h d", h=4)[:, :, 0:65]
            for h in range(4):
                sbk, sk = psum("sc")
                mm(sbk[0:L, 0:L], B_kT[:, h, 0:L], B_qT[:, h, 0:L], r=["B_kT", "B_qT"], w=[sk])
                stt("dve", St[0:L, 0:L], sbk[0:L, 0:L], wk_[:, h:h + 1], tri_f[0:L, 0:L], ALU.mult, ALU.mult,
                    r=[sk, "bv4", "tri_f"], w=["St"])
                mm(brv[0:L, h, :], St[0:L, 0:L], Bv[0:L, h, 0:65], True, False, r=["St", "Bv"], w=[brk])
                mm(brv[0:L, h, :], B_qT[:, h, 0:L], Cst_bf[:, h, 0:65], False, True, r=["B_qT", "Cst_bf"], w=[brk])
            if KB < 6:
                return
            ts("dve", den_, brv[0:L, :, 64], -1.0, ALU.mult, r=[brk], w=["bv4"])
            tt("dve", den_, den_, brv[0:L, :, 64], ALU.max, [brk, "bv4"], ["bv4"])
            tt("dve", den_, den_, ecs_, ALU.max, ["bv4"], ["bv4"])
            P.op("dve", lambda e, o=rd_, i=den_: e.reciprocal(out=o, in_=i), ["bv4"], ["bv4"])
            tt("dve", Bh[0:L], brv[0:L, :, 0:64], rd_.unsqueeze(2).to_broadcast([L, 4, 64]), ALU.mult,
               [brk, "bv4"], ["Bh"])
            if KB < 7:
                return
            cb, ck = psum("mx")
            cv = cb[0:64, 0:320].rearrange("p (h d) -> p h d", h=4)[:, :, 0:65]
            for h in range(4):
                mm(cv[:, h, :], ktil[0:L, 64 * h:64 * h + 64], Bv[0:L, h, 0:65], r=["ktil", "Bv"], w=[ck])
            tt("dve", Cst[:], Cst[:], cv, ALU.add, ["Cst", ck], ["Cst"])
            tt("dve", Cst[:], Cst[:], edL[:, :].unsqueeze(2).to_broadcast([64, 4, 65]), ALU.mult, ["Cst", "edL"], ["Cst"])
            cp("dve", Cst_bf[:, :, 0:65], Cst[:], ["Cst"], ["Cst_bf"])
            if KB < 8:
                return
            rs = head_rstd(Bh[0:L], L, Bsq, "Bh", 7)
            act(B_o[0:L, :], B_o[0:L, :], AF.Sigmoid, ["C_uv"], ["C_uv"])
            tt("dve", Bh[0:L], Bh[0:L], rs.unsqueeze(2).to_broadcast([L, 4, 64]), ALU.mult, ["Bh", "bv4"], ["Bh"])
            tt("dve", Bh[0:L].rearrange("p h d -> p (h d)"), Bh[0:L].rearrange("p h d -> p (h d)"), gml[0:L, :], ALU.mult, ["Bh", "gml"], ["Bh"])
            tt("dve", mix[0:L, 256:512], Bh[0:L].rearrange("p h d -> p (h d)"), B_o[0:L, :], ALU.mult, ["Bh", "C_uv"], ["mix"])

            if KFORK:
                P.join()
                curset[0] = None
            if mid is not None:
                mid()
            if KFORK2:
                P.fork()
                curset[0] = 0
            pb, pk = proj_tm(1856, 512, xnT, KT_, L)
            cp("act", C_uv[0:L, :], pb[0:L, 0:512], [pk], ["C_uv"])
            tt("pool", C_t[0:L, :], C_uv[0:L, :], C_uv[0:L, :], ALU.mult, ["C_uv"], ["C_t"])
            ts("pool", C_t[0:L, :], C_t[0:L, :], 0.044715, ALU.mult, 1.0, ALU.add, r=["C_t"], w=["C_t"])
            tt("pool", C_t[0:L, :], C_t[0:L, :], C_uv[0:L, :], ALU.mult, ["C_t", "C_uv"], ["C_t"])
            act(C_t[0:L, :], C_t[0:L, :], AF.Sigmoid, ["C_t"], ["C_t"], scale=1.5957691216057308)
            tt("pool", C_uv[0:L, :], C_uv[0:L, :], C_t[0:L, :], ALU.mult, ["C_t", "C_uv"], ["C_uv"])
            rmsnorm_rstd(C_uv[0:L, 256:512], L, 256, 1.0 / 256, ssq[0:L, 2:3], rstd[0:L, 2:3], C_t[0:L, 0:256], ["C_uv"], "C_t", sfx="c", lcol=2)
            stt("dve", vrows[0:L, :], C_uv[0:L, 256:512], rstd[0:L, 2:3], gcv[0:L, :], ALU.mult, ALU.mult,
                r=["C_uv", "rstdc", "gcv"], w=["vrows"])
            cp("pool", vr_bf[0:L, :], vrows[0:L, :], ["vrows"], ["vr_bf"])
            if samp is not None:
                dma(ocv[l, samp, :, :], vrows[0:L, :], r=["vrows"], w=[f"ocv{l}{samp}"])
            gb, gk = psum("mx")
            for h in range(4):
                mm(gb[0:L, 64 * h:64 * h + 64], WsT[0:L, h, 0:L], vr_bf[0:L, 64 * h:64 * h + 64], r=["WsT", "vr_bf"], w=[gk])
            for h in range(4):
                stt("dve", mix[0:L, 512 + 64 * h:576 + 64 * h], gb[0:L, 64 * h:64 * h + 64], bsT[0:L, h:h + 1],
                    C_uv[0:L, 64 * h:64 * h + 64], ALU.add, ALU.mult, r=[gk, "bsT", "C_uv"], w=["mix"])

            if KFORK2:
                P.next_thread()
                curset[0] = 1
            proj_fm(2368, xnT, KT_, L, D_qT, "D_qT", eng="act")
            proj_fm(2624, xnT, KT_, L, D_sT, "D_sT", eng="dve")
            pb, pk = proj_tm(2624, 512, xnT, KT_, L)
            act(D_sg[0:L, :], pb[0:L, 0:256], AF.Sigmoid, [pk], ["D_sg"])
            cp("dve", D_v[0:L, :], pb[0:L, 256:512], [pk], ["D_v"])
            pb, pk = proj_tm(3136, 256, xnT, KT_, L)
            act(D_gs[0:L, :], pb[0:L, 0:256], AF.Silu, [pk], ["D_gs"])
            tt("pool", D_kd[0:L, :], D_sg[0:L, :], oml[0:L, :], ALU.mult, ["D_sg", "oml"], ["D_kd"])
            tt("pool", D_lf[0:L, :], D_kd[0:L, :], lbm[0:L, :], ALU.add, ["D_kd", "lbm"], ["D_lf"])
            act(D_lf[0:L, :], D_lf[0:L, :], AF.Ln, ["D_lf"], ["D_lf"])
            tt("pool", D_kd[0:L, :], oml[0:L, :], D_kd[0:L, :], ALU.subtract, ["D_kd", "oml"], ["D_kd"])
            act(D_sT[:, :, 0:L], D_sT[:, :, 0:L], AF.Sigmoid, ["D_sT"], ["D_sT"])
            for h in range(4):
                ts("dve", D_sT[:, h, 0:L], D_sT[:, h, 0:L], nomlT[:, h:h + 1], ALU.mult, omlT[:, h:h + 1], ALU.add,
                   r=["D_sT", "nomlT", "omlT"], w=["D_sT"])
            pb, pk = psum("pj")
            pbv = pb[0:64, :].rearrange("p (a b) -> p a b", a=4)
            for h in range(4):
                mm(pbv[:, h, 0:L], D_lf[0:L, 64 * h:64 * h + 64], tri_f[0:L, 0:L], r=["D_lf", "tri_f"], w=[pk])
            cp("act", D_bT[:, :, 0:L], pbv[:, :, 0:L], [pk], ["D_bT"])
            ts("dve", D_nbT[:, :, 0:L], pbv[:, :, 0:L], -1.0, ALU.mult, r=[pk], w=["D_nbT"])
            act(D_e[:, :, 0:L], D_bT[:, :, 0:L], AF.Exp, ["D_bT"], ["D_e"])
            tt("dve", D_qh[:, :, 0:L], D_qT[:, :, 0:L], D_e[:, :, 0:L], ALU.mult, ["D_qT", "D_e"], ["D_qh"])
            bs_ = min(BS, L)
            nb = (L + BS - 1) // BS
            cp("pool", D_eq[:, :, 0:bs_], D_e[:, :, 0:bs_], ["D_e"], ["D_ek"])
            for h in range(4):
                for I in range(1, nb):
                    act(D_eq[:, h, BS * I:BS * I + BS], D_bT[:, h, BS * I:BS * I + BS], AF.Exp, ["D_bT", "D_nbT"], ["D_ek"],
                        bias=D_nbT[:, h, BS * I - 1:BS * I])
            tt("dve", D_qt[:, :, 0:L], D_qT[:, :, 0:L], D_eq[:, :, 0:L], ALU.mult, ["D_qT", "D_ek"], ["D_qt"])
            ob, ok_ = psum("mx")
            ov = ob[:, 0:256].rearrange("p (h d) -> p h d", h=4)
            for h in range(4):
                for I in range(nb):
                    n = min(L, BS * (I + 1))
                    bias = zeroT[0:64, 0:1] if I == 0 else D_bT[:, h, BS * I - 1:BS * I]
                    act(D_ek[:, I, 0:n], D_bT[:, h, 0:n], AF.Exp, ["D_bT", "zeroT"], ["D_ek"], bias=bias, scale=-1.0)
                for I in range(nb):
                    n = min(L, BS * (I + 1))
                    tt("dve", D_kI[:, I, 0:n], D_sT[:, h, 0:n], D_ek[:, I, 0:n], ALU.mult, ["D_sT", "D_ek"], ["D_kI"])
                sbk, sk = psum("sc")
                for I in range(nb):
                    n = min(L, BS * (I + 1))
                    mm(sbk[0:n, BS * I:BS * I + bs_], D_kI[:, I, 0:n], D_qt[:, h, BS * I:BS * I + bs_], r=["D_kI", "D_qt"], w=[sk])
                for I in range(nb):
                    n = min(L, BS * (I + 1))
                    tt("dve", D_AT[0:n, BS * I:BS * I + bs_], sbk[0:n, BS * I:BS * I + bs_], tri_f[0:n, BS * I:BS * I + bs_],
                       ALU.mult, [sk, "tri_f"], ["D_AT"])
                mm(ov[0:L, h, :], D_AT[0:L, 0:L], D_v[0:L, 64 * h:64 * h + 64], True, False, r=["D_AT", "D_v"], w=[ok_])
                mm(ov[0:L, h, :], D_qh[:, h, 0:L], S_bf[:, h, :], False, True, r=["D_qh", "S_bf"], w=[ok_])
            cp("act", D_o[0:L], ov[0:L], [ok_], ["Bh"])
            db, dk = psum("pj")
            mm(db[0:L, 0:256], ls_f[0:L, 0:L], D_lf[0:L, :], r=["ls_f", "D_lf"], w=[dk])
            act(D_sg[0:L, :], db[0:L, 0:256], AF.Exp, [dk], ["D_sg"])
            tt("pool", D_kb[0:L, :], D_kd[0:L, :], D_sg[0:L, :], ALU.mult, ["D_kd", "D_sg"], ["D_kb"])
            sb2, sk2 = psum("mx")
            sv2 = sb2[0:64, 0:256].rearrange("p (h d) -> p h d", h=4)
            for h in range(4):
                mm(sv2[:, h, :], D_kb[0:L, 64 * h:64 * h + 64], D_v[0:L, 64 * h:64 * h + 64], r=["D_kb", "D_v"], w=[sk2])
            for h in range(4):
                stt("dve", Sst[:, h, :], Sst[:, h, :], D_e[:, h, L - 1:L], sv2[:, h, :], ALU.mult, ALU.add,
                    r=["Sst", "D_e", sk2], w=["Sst"])
            cp("dve", S_bf[:], Sst[:], ["Sst"], ["S_bf"])
            rs = head_rstd(D_o[0:L], L, Bsq, "Bh", 8)
            tt("dve", D_o[0:L], D_o[0:L], rs.unsqueeze(2).to_broadcast([L, 4, 64]), ALU.mult, ["Bh", "bv4"], ["Bh"])
            tt("dve", D_o[0:L].rearrange("p h d -> p (h d)"), D_o[0:L].rearrange("p h d -> p (h d)"), ghg[0:L, :], ALU.mult, ["Bh", "ghg"], ["Bh"])
            tt("dve", mix[0:L, 768:1024], D_o[0:L].rearrange("p h d -> p (h d)"), D_gs[0:L, :], ALU.mult, ["Bh", "D_gs"], ["mix"])

            if KFORK2:
                P.join()
                curset[0] = None
            transposes(mix, L, 8, mixT, I_bf, "mix", "xnTr", "I_bf")
            for hf in range(2):
                pb, pk = psum("pj")
                for k in range(8):
                    mm(pb[0:L, :], mixT[:, k, 0:L], WO[:, k, 512 * hf:512 * hf + 512], k == 0, k == 7, r=["xnTr", "WO"], w=[pk])
                tt("dve", xt[0:L, 512 * hf:512 * hf + 512], xt[0:L, 512 * hf:512 * hf + 512], pb[0:L, :], ALU.add, [KX, pk], [KX])
            dma(orows, xt[0:L, :], r=[KX], w=[okey])

        def state_out(dC, dn, dm, dS, tag):
            act(m4[:, 2:3], mrun[:, :], AF.Exp, ["mrun"], ["m4"], scale=-1.0)
            ts("dve", m4[:, 16:20], I_f[0:4, 0:4], m4[:, 2:3], ALU.mult, r=["I_f", "m4"], w=["m4"])
            mb, mk = psum("mx")
            mm(mb[0:64, 0:4], ones_f[0:4, 0:64], m4[:, 16:20], r=["ones_f", "m4"], w=[mk])
            cp("dve", em0[:, :], mb[0:64, 0:4], [mk], ["em0"])
            tt("dve", outst[:], Cst[:], em0[:, :].unsqueeze(2).to_broadcast([64, 4, 65]), ALU.mult, ["Cst", "em0"], ["Bsq"])
            mb, mk = psum("mx")
            mv = mb[0:64, 0:256].rearrange("p (h d) -> p h d", h=4)
            for h in range(4):
                mm(mv[:, h, :], outst[:, h, 0:64], I_f[0:64, 0:64], r=["Bsq", "I_f"], w=[mk])
            cp("dve", Cout[:], mv, [mk], ["Bh"])
            dma(dC.rearrange("h v k -> v h k"), Cout[:], r=["Bh"], w=["oC" + tag])
            dma(dn.rearrange("h k -> k h"), outst[:, :, 64], r=["Bsq"], w=["on" + tag], slow=True)
            dma(dm.rearrange("(h o) -> h o", o=1), mrun[:, :], r=["mrun"], w=["om" + tag])
            dma(dS.rearrange("h d v -> d h v"), Sst[:], r=["Sst"], w=["oS" + tag])

        if KSTOP < 2:
            P.barrier(); ph.close(); return
        memset("dve", Cst[:], 0.0, ["Cst"]); memset("dve", Cst_bf[:], 0.0, ["Cst_bf"])
        memset("dve", Sst[:], 0.0, ["Sst"]); memset("dve", S_bf[:], 0.0, ["S_bf"])
        memset("dve", mrun[:], 0.0, ["mrun"])
        front(T, xsrc_p[0:T, :], "xsrc0", 0)
        for ci in range(nch):
            wins = [((ci - j) % NW, T) for j in range(0, min(16, ci) + 1)]
            nxt = None
            if ci + 1 < nch:
                nxt = (lambda c=ci + 1: front(T, xsrc_p[c * T:(c + 1) * T, :], f"xsrc{c}", c % 2))
            chunk(T, xsrc_p[ci * T:(ci + 1) * T, :], f"xsrc{ci}", xmid[ci * T:(ci + 1) * T, :], f"xmid{ci}", wins, ci % NW,
                  (lambda h, i0, n, Lk: MP[0:Lk, h, i0:i0 + n, :]), prompt_ci=ci, sl=ci % 2, do_front=False, mid=nxt)
            if KSTOP >= 3:
                flush_pending(1)
        if KSTOP >= 3:
            flush_pending(100)
        if KSTOP >= 4:
            state_out(oCp[l], onp[l], omp[l], oSp[l], "p")
        if KSTOP < 5:
            P.barrier(); ph.close(); return

        for s in range(4):
            for q4 in range(4):
                i = stgpos[0] % 2; stgpos[0] += 1
                dma(stg[i][:, 0:1024].rearrange("p (j c) -> p j c", j=4),
                    kc[l, s, q4 * 512:(q4 + 1) * 512, :].rearrange("(j p) c -> p j c", p=128), w=[f"stg{i}"])
                cp("pool", Kcb[:], stg[i][:, 0:1024].rearrange("p (j c) -> p j c", j=4), [f"stg{i}"], ["Pt0"])
                for h in range(4):
                    pb, pk = psum("pj")
                    pv = pb[0:64, :].rearrange("p (a b) -> p a b", a=4)
                    for jj in range(4):
                        mm(pv[:, jj, :], Kcb[:, jj, 64 * h:64 * h + 64], I_bf[:, :], r=["Pt0", "I_bf"], w=[pk])
                    j0 = q4 * 4
                    cp("act" if h % 2 == 0 else "dve", KT[:, h, j0:j0 + 4, :], pv, [pk], [f"KT{j0 + q}" for q in range(4)])
                i = stgpos[0] % 2; stgpos[0] += 1
                dma(stg[i][:, 0:1024].rearrange("p (j c) -> p j c", j=4),
                    vc[l, s, q4 * 512:(q4 + 1) * 512, :].rearrange("(j p) c -> p j c", p=128), w=[f"stg{i}"])
                for jj in range(4):
                    j = q4 * 4 + jj
                    cp("pool", VW[:, j, :, 0:64], stg[i][:, jj * 256:(jj + 1) * 256].rearrange("p (h d) -> p h d", h=4),
                       [f"stg{i}"], [f"VW{j}"])
            dma(Cin[:], mC[l, s].rearrange("h v k -> v h k"), w=["Bh"])
            dma(nin[:], mn[l, s].rearrange("h k -> k h"), w=["nin"], slow=True)
            dma(em0[:], dap(mm_, (l * 4 + s) * 4, [[0, 64], [1, 4]]), w=["em0"])
            dma(mrun[:], dap(mm_, (l * 4 + s) * 4, [[1, 4], [1, 1]]), w=["mrun"])
            dma(Sst[:], hS[l, s].rearrange("h d v -> d h v"), w=["Sst"])
            cp("dve", S_bf[:], Sst[:], ["Sst"], ["S_bf"])
            act(em0[:], em0[:], AF.Exp, ["em0"], ["em0"])
            mb, mk = psum("mx")
            mv = mb[0:64, 0:256].rearrange("p (h d) -> p h d", h=4)
            for h in range(4):
                mm(mv[:, h, :], Cin[:, h, :], I_f[0:64, 0:64], r=["Bh", "I_f"], w=[mk])
            tt("dve", Cst[:, :, 0:64], mv, em0[:, :].unsqueeze(2).to_broadcast([64, 4, 64]), ALU.mult, [mk, "em0"], ["Cst"])
            tt("dve", Cst[:, :, 64], nin[:], em0[:], ALU.mult, ["nin", "em0"], ["Cst"])
            cp("dve", Cst_bf[:, :, 0:65], Cst[:], ["Cst"], ["Cst_bf"])
            wins = [(j, T) for j in range(16)] + [(16, 4)]
            r0 = NTP + 4 * s
            chunk(4, xsrc_s[4 * s:4 * s + 4, :], f"xsrcs{s}", xmid[r0:r0 + 4, :], f"xmids{s}", wins, 16,
                  (lambda h, i0, n, Lk: MS[0:Lk, h, i0:i0 + n, :]), samp=s)
            state_out(oCs[l, s], ons[l, s], oms[l, s], oSs[l, s], f"s{s}")
        P.barrier()
        ph.close()

    def phase_F(l, last):
        ph = ExitStack()
        WU = sb(f"wup{l}", [128, 8, DFF], BF, ph)
        WD = sb(f"wdn{l}", [128, 32, DM], BF, ph)
        gT = sb(f"gTf{l}", [128, 8], stack=ph)
        dma(gT[:, 0:8], dap(g_mlp, l * DM, [[1, 128], [128, 8]]), w=["gT"], slow=True)
        stg4 = stg + [sb("stgx", [128, 1088], stack=ph) for _ in range(2)]
        load_weights(WU, w_up[l], 8, DFF, gT, lambda k: k, "WU", bufs=stg4)
        load_weights(WD, w_down[l], 32, DM, None, None, "WD", bufs=stg4)
        hTs = [sb("hT", [128, 32, 128], BF, ph) for _ in range(2)]
        rls = [sb("rl", [128, 4, 128], stack=ph) for _ in range(2)]
        gfin = None
        if last:
            gfin = sb("gfin", [128, DM], stack=ph)
            dma(gfin[:], dap(g_final, 0, [[0, 128], [1, DM]]), w=["gfin"])
        rlpos = [0]

        def fchunk(L, rows_in, kin, rows_out, kout, sl):
            xt_, xn_, xnT_, hT = xts[sl], xns[sl], xnTs[sl], hTs[sl]
            kx, kn, kt, kh = f"xt{sl}", f"xn{sl}", f"xnT{sl}", f"hT{sl}"
            sf = str(sl)
            dma(xt_[0:L, :], rows_in, r=[kin], w=[kx])
            rmsnorm_rstd(xt_[0:L, :], L, DM, 1.0 / DM, ssq[0:L, 4 + sl:5 + sl], rstd[0:L, 4 + sl:5 + sl], xn_[0:L, :], [kx], kn, sfx=sf, lcol=4 + sl)
            ts("dve", xn_[0:L, :], xt_[0:L, :], rstd[0:L, 4 + sl:5 + sl], ALU.mult, r=[kx, "rstd" + sf], w=[kn])
            transposes(xn_, L, 8, xnT_, I_bf, kn, kt, "I_bf")
            for g in range(8):
                pb, pk = psum("sc" if g % 2 else "mx")
                pv = pb[:, :].rearrange("p (a b) -> p a b", a=4)
                for q in range(4):
                    f = 4 * g + q
                    for k in range(8):
                        mm(pv[:, q, 0:L], WU[:, k, 128 * f:128 * f + 128], xnT_[:, k, 0:L], k == 0, k == 7, r=["WU", kt], w=[pk])
                ri = rlpos[0] % 2
                rlpos[0] += 1
                rl = rls[ri]
                act(rl[:, :, 0:L], pv[:, :, 0:L], AF.Relu, [pk], [f"rl{ri}"])
                tt("pool" if g % 2 else "dve", hT[:, 4 * g:4 * g + 4, 0:L], rl[:, :, 0:L], rl[:, :, 0:L], ALU.mult, [f"rl{ri}"], [kh])
            for hf in range(2):
                pb, pk = psum("pj")
                for f in range(32):
                    mm(pb[0:L, :], hT[:, f, 0:L], WD[:, f, 512 * hf:512 * hf + 512], f == 0, f == 31, r=[kh, "WD"], w=[pk])
                tt("dve", xt_[0:L, 512 * hf:512 * hf + 512], xt_[0:L, 512 * hf:512 * hf + 512], pb[0:L, :], ALU.add, [kx, pk], [kx])
            if last:
                rmsnorm_rstd(xt_[0:L, :], L, DM, 1.0 / DM, ssq[0:L, 6 + sl:7 + sl], rstd[0:L, 6 + sl:7 + sl], xn_[0:L, :], [kx], kn, sfx="f" + sf, lcol=6 + sl)
                stt("dve", xt_[0:L, :], xt_[0:L, :], rstd[0:L, 6 + sl:7 + sl], gfin[0:L, :], ALU.mult, ALU.mult, r=[kx, "rstdf" + sf, "gfin"], w=[kx])
            dma(rows_out, xt_[0:L, :], r=[kx], w=[kout])

        for ci in range(nch):
            dst = y_p[ci * T:(ci + 1) * T, :] if last else xnext[ci * T:(ci + 1) * T, :]
            fchunk(T, xmid[ci * T:(ci + 1) * T, :], f"xmid{ci}", dst, f"xsrc{ci}", ci % 2)
        dst = y_s[:, :] if last else xnext[NTP:NTP + 16, :]
        fchunk(16, xmid[NTP:NTP + 16, :], "xmids_all", dst, "xsrcs_all", nch % 2)
        P.barrier()
        ph.close()

    ltm = sb("ltm", [128, 128])
    tt("dve", ltm[:], ls_f[:], I_f[:], ALU.add, ["ls_f", "I_f"], ["ltm"])

    if KSTOP >= 1:
        phase_M(0, xp, xs)
    if KSTOP >= 6:
        phase_F(0, False)
    if KSTOP >= 7 and os.environ.get("KSKIPM1", "0") != "1":
        phase_M(1, xnext, xnext[NTP:NTP + 16, :])
    if KSTOP >= 8:
        phase_F(1, True)

    semnames = sorted(P.final.keys())
    sems = {s: es.enter_context(nc.semaphore(s)) for s in semnames}
    blk = es.enter_context(nc.Block())

    def run(engname, handle):
        for waits, fn, inc in P.ops[engname]:
            for s, v in waits:
                handle.wait_ge(sems[s], v)
            if fn is not None:
                fn(handle).then_inc(sems[inc[0]], inc[1])

    @blk.tensor
    def _(e):
        run("pe", e)

    @blk.scalar
    def _(e):
        run("act", e)

    @blk.vector
    def _(e):
        run("dve", e)

    @blk.gpsimd
    def _(e):
        run("pool", e)

    @blk.sync
    def _(e):
        run("sp", e)
        for s, v in P.final.items():
            e.wait_ge(sems[s], v)

    es.close()
    return nc


def _consts():
    i = np.arange(128)
    cI = np.eye(128, dtype=np.float32)
    cJ = cI[::-1].copy()
    cTri = (i[:, None] <= i[None, :]).astype(np.float32)
    cLs = (i[:, None] > i[None, :]).astype(np.float32)
    mult = np.zeros(2049, np.float32)
    for w, d in ((128, 1), (512, 4), (2048, 16)):
        mult[np.arange(w // d + 1) * d] += 1.0
    o = np.arange(2049)
    dd = np.maximum(o, 1).astype(np.float32)
    large = 16 + (np.log(dd / 16) / np.float32(np.log(2048 / 16)) * 16).astype(np.int32)
    large = np.clip(large, 16, 31)
    bucket = np.where(o < 16, o, large)
    MO = np.zeros((32, TABN), np.float32)
    MO[bucket, o + 127] = mult
    MOr = MO[:, ::-1][:, -TABN:].copy()
    MOr = np.zeros((32, TABN), np.float32)
    MOr[:, 0:2303] = MO[:, 0:2303][:, ::-1]
    return cI, cJ, cTri, cLs, MO, MOr


_NC_CACHE = {}


def kernel(x_prompt, x_sample, cache_k_win, cache_v_win, state_mlstm_C, state_mlstm_n, state_mlstm_m,
           state_hgrn_S, rel_bias, w_in, w_out, g_attn, g_mlp, w_up, w_down, b_i, b_f, g_mlstm, g_cv,
           w_s, b_s, hgrn_lb, g_hgrn, g_final):
    f = lambda a: np.ascontiguousarray(np.asarray(a, dtype=np.float32))
    nch = NCH
    if nch not in _NC_CACHE:
        _NC_CACHE[nch] = build(nch)
    nc = _NC_CACHE[nch]
    cI, cJ, cTri, cLs, MO, MOr = _consts()
    shared = dict(relb=f(rel_bias), w_in=f(w_in), w_out=f(w_out), g_attn=f(g_attn), g_mlp=f(g_mlp), w_up=f(w_up),
                  w_down=f(w_down), b_i=f(b_i), b_f=f(b_f), g_mlstm=f(g_mlstm), g_cv=f(g_cv), w_s=f(w_s), b_s=f(b_s),
                  hlb=f(hgrn_lb), g_hgrn=f(g_hgrn), g_final=f(g_final).reshape(1, DM),
                  cI=cI, cJ=cJ, cTri=cTri, cLs=cLs, cMO=MO, cMOr=MOr)
    xp_ = f(x_prompt); xs_ = f(x_sample)
    kc_ = f(cache_k_win); vc_ = f(cache_v_win)
    in_maps = []
    for c in range(8):
        sl = slice(4 * c, 4 * c + 4)
        m = dict(shared)
        m["xp"] = np.ascontiguousarray(xp_[c % 2, :nch * T])
        m["xs"] = np.ascontiguousarray(xs_[sl].reshape(16, DM))
        m["kc"] = np.ascontiguousarray(kc_[:, sl].reshape(2, 4, 2048, 256))
        m["vc"] = np.ascontiguousarray(vc_[:, sl].reshape(2, 4, 2048, 256))
        m["mC"] = f(state_mlstm_C)[:, sl].copy()
        m["mn"] = f(state_mlstm_n)[:, sl].copy()
        m["mm"] = f(state_mlstm_m)[:, sl].copy()
        m["hS"] = f(state_hgrn_S)[:, sl].copy()
        in_maps.append(m)
    res = run_bass_kernel_spmd(nc, in_maps, core_ids=list(range(8))).results
    B = 2
    cat = lambda name, ax: np.concatenate([res[c][name] for c in range(8)], axis=ax)
    y_p = np.zeros((B, 8192, DM), np.float32)
    y_p[:, :nch * T] = np.stack([res[b]["y_p"] for b in range(B)])
    y_s = cat("y_s", 0).reshape(32, 4, DM)
    stack2 = lambda name: np.stack([res[b][name] for b in range(B)], axis=1)
    kwp = stack2("kwp").reshape(2, B, 2048, 4, 64)
    vwp = stack2("vwp").reshape(2, B, 2048, 4, 64)
    kws = cat("kws", 1).reshape(2, 32, 2048, 4, 64)
    vws = cat("vws", 1).reshape(2, 32, 2048, 4, 64)
    Cp = stack2("oCp"); np_ = stack2("onp"); mp = stack2("omp")
    Cs = cat("oCs", 1); ns = cat("ons", 1); ms = cat("oms", 1)
    Sp = stack2("oSp"); Ss = cat("oSs", 1)
    cvs = cat("ocv", 1).reshape(2, 32, 4, 4, 64)
    return (y_p, y_s, kwp, vwp, kws, vws, Cp, np_, mp, Cs, ns, ms, Sp, Ss, cvs)
```
